# Optimizing a Trainium2 kernel written in Bass

```python
import math
import jax, jax.numpy as jnp
from jax import lax
import numpy as np

D_MODEL = 1024
BATCH = 8
SEQ = 4096
DEPTH = 1
DEC_BATCH = 8
DEC_SEQ = 64
PAST_LEN = 2048

CHUNK = 64
GMLP_CHUNK = 128
GMLP_GROUPS = 4
GMLP_HEAD = 128
GMLP_WIDTH = GMLP_GROUPS * GMLP_HEAD
N_HEADS = 8
QK_NOPE = 64
QK_ROPE = 32
QK_HEAD = QK_NOPE + QK_ROPE
V_HEAD = 64
Q_LORA = 384
KV_LORA = 256
MLA_WIDTH = N_HEADS * V_HEAD
MIX_WIDTH = GMLP_WIDTH + MLA_WIDTH
IN_WIDTH = 2 * GMLP_WIDTH + Q_LORA + KV_LORA + QK_ROPE
IN_SPLITS = (GMLP_WIDTH, 2 * GMLP_WIDTH, 2 * GMLP_WIDTH + Q_LORA, 2 * GMLP_WIDTH + Q_LORA + KV_LORA)
D_FF = -(-8 * D_MODEL // (3 * 256)) * 256
ROPE_THETA = 10000.0
EPS = 1e-6
Q_BLOCK = 128
SCALE = QK_HEAD ** -0.5

kernel_name = 'hybrid_gmlp_mla_streaming_step'


def rmsnorm(x, g):
    xf = x.astype(jnp.float32)
    y = xf * lax.rsqrt(jnp.mean(xf * xf, axis=-1, keepdims=True) + EPS)
    return (y * g.astype(jnp.float32)).astype(x.dtype)


def rope(x, pos):
    half = QK_ROPE // 2
    inv = ROPE_THETA ** (-jnp.arange(half, dtype=jnp.float32) / half)
    ang = pos.astype(jnp.float32)[:, None] * inv[None, :]
    ang = ang.reshape(ang.shape[:1] + (1,) * (x.ndim - 3) + (half,))
    cos, sin = jnp.cos(ang), jnp.sin(ang)
    xf = x.astype(jnp.float32)
    x1, x2 = xf[..., :half], xf[..., half:]
    return jnp.concatenate([x1 * cos - x2 * sin, x2 * cos + x1 * sin], axis=-1).astype(x.dtype)


def ada_modulation(c, w, b):
    m = jax.nn.silu(c) @ w + b
    return jnp.split(m[:, None, :], 6, axis=-1)


def gmlp_mix(u, v, w_s, b_s):
    B, T, _ = v.shape
    L = min(T, GMLP_CHUNK)
    n = T // L
    p = jnp.arange(L)
    mask = (p[None, :] // CHUNK) <= (p[:, None] // CHUNK)
    ws = jnp.where(mask[None], w_s[:, :L, :L], 0.0)
    vc = v.reshape(B, n, L, GMLP_GROUPS, GMLP_HEAD)
    mixed = jnp.einsum('gij,bnjgc->bnigc', ws, vc) + b_s[:, :L].T[None, None, :, :, None]
    return u * mixed.reshape(B, T, GMLP_WIDTH)


def mla_queries(c_q, pos, q_norm_g, w_uq, qn_g, qr_g):
    B, T, _ = c_q.shape
    q = (rmsnorm(c_q, q_norm_g) @ w_uq).reshape(B, T, N_HEADS, QK_HEAD)
    q_nope = rmsnorm(q[..., :QK_NOPE], qn_g)
    q_rope = rope(rmsnorm(q[..., QK_NOPE:], qr_g), pos)
    return jnp.concatenate([q_nope, q_rope], axis=-1)


def mla_keys_values(ckv, krope, w_ukv, kn_g):
    B, S, _ = ckv.shape
    kv = (ckv @ w_ukv).reshape(B, S, N_HEADS, QK_NOPE + V_HEAD)
    k_nope = rmsnorm(kv[..., :QK_NOPE], kn_g)
    k = jnp.concatenate([k_nope, jnp.broadcast_to(krope[:, :, None, :], (B, S, N_HEADS, QK_ROPE))], axis=-1)
    return k, kv[..., QK_NOPE:]


def attend(q, k, v, q_pos, k_pos):
    s = jnp.einsum('bqhd,bkhd->bhqk', q, k).astype(jnp.float32) * SCALE
    mask = (k_pos[None, :] // CHUNK) <= (q_pos[:, None] // CHUNK)
    s = jnp.where(mask[None, None], s, jnp.finfo(jnp.float32).min)
    p = jax.nn.softmax(s, axis=-1).astype(v.dtype)
    return jnp.einsum('bhqk,bkhd->bqhd', p, v)


def prompt_attention(q, k, v):
    B, T, H, Dk = q.shape
    nb = T // Q_BLOCK
    qb = q.reshape(B, nb, Q_BLOCK, H, Dk).transpose(1, 0, 2, 3, 4)
    k_pos = jnp.arange(T)

    def block(args):
        qi, i = args
        q_pos = i * Q_BLOCK + jnp.arange(Q_BLOCK)
        return attend(qi, k, v, q_pos, k_pos)

    out = lax.map(block, (qb, jnp.arange(nb)))
    return out.transpose(1, 0, 2, 3, 4).reshape(B, T, MLA_WIDTH)


def trunk_layer(x, c, pos, past, w):
    B, T, _ = x.shape
    sh1, sc1, g1, sh2, sc2, g2 = ada_modulation(c, w['w_ada'], w['b_ada'])
    h = rmsnorm(x, w['norm1_g']) * (1.0 + sc1) + sh1
    z = h @ w['w_in']
    u, v, c_q, c_kv, k_r = jnp.split(z, IN_SPLITS, axis=-1)
    u = jax.nn.gelu(u)
    v = jax.nn.gelu(v)
    y_a = gmlp_mix(u, v, w['w_s'], w['b_s'])
    q = mla_queries(c_q, pos, w['q_norm_g'], w['w_uq'], w['qn_g'], w['qr_g'])
    ckv = rmsnorm(c_kv, w['kv_norm_g'])
    krope = rope(rmsnorm(k_r, w['kr_g']), pos)
    if past is None:
        k, vv = mla_keys_values(ckv, krope, w['w_ukv'], w['kn_g'])
        y_b = prompt_attention(q, k, vv)
    else:
        ckv_all = jnp.concatenate([past[0], ckv], axis=1)
        krope_all = jnp.concatenate([past[1], krope], axis=1)
        k, vv = mla_keys_values(ckv_all, krope_all, w['w_ukv'], w['kn_g'])
        k_pos = jnp.arange(ckv_all.shape[1])
        y_b = attend(q, k, vv, pos, k_pos).reshape(B, T, MLA_WIDTH)
    x = x + g1 * (jnp.concatenate([y_a, y_b], axis=-1) @ w['w_out'])
    h2 = rmsnorm(x, w['norm2_g']) * (1.0 + sc2) + sh2
    gate, up = jnp.split(h2 @ w['w_ffn_in'], 2, axis=-1)
    x = x + g2 * ((jax.nn.silu(gate) * up) @ w['w_ffn_out'])
    return x, ckv, krope, v


def setup_inputs(seed: int = 0) -> dict:
    key = jax.random.key(seed)
    ks = jax.random.split(key, 24)

    def nrm(k, shape, scale):
        return jax.random.normal(k, shape, jnp.float32) * scale

    def gain(k, n):
        return 1.0 + 0.1 * jax.random.normal(k, (DEPTH, n), jnp.float32)

    L = DEPTH
    return {
        'x_prompt': nrm(ks[0], (BATCH, SEQ, D_MODEL), 1.0),
        'x_sample': nrm(ks[1], (DEC_BATCH, DEC_SEQ, D_MODEL), 1.0),
        'cache_ckv': nrm(ks[2], (L, DEC_BATCH, PAST_LEN, KV_LORA), 1.0),
        'cache_krope': nrm(ks[3], (L, DEC_BATCH, PAST_LEN, QK_ROPE), 1.0),
        'c_prompt': nrm(ks[4], (BATCH, D_MODEL), 1.0),
        'c_sample': nrm(ks[5], (DEC_BATCH, D_MODEL), 1.0),
        'w_ada': nrm(ks[6], (L, D_MODEL, 6 * D_MODEL), 0.5 * D_MODEL ** -0.5),
        'b_ada': nrm(ks[7], (L, 6 * D_MODEL), 0.02),
        'norm1_g': gain(ks[8], D_MODEL),
        'w_in': nrm(ks[9], (L, D_MODEL, IN_WIDTH), D_MODEL ** -0.5),
        'w_s': nrm(ks[10], (L, GMLP_GROUPS, GMLP_CHUNK, GMLP_CHUNK), GMLP_CHUNK ** -0.5),
        'b_s': 1.0 + nrm(ks[11], (L, GMLP_GROUPS, GMLP_CHUNK), 0.1),
        'q_norm_g': gain(ks[12], Q_LORA),
        'w_uq': nrm(ks[13], (L, Q_LORA, N_HEADS * QK_HEAD), Q_LORA ** -0.5),
        'kv_norm_g': gain(ks[14], KV_LORA),
        'w_ukv': nrm(ks[15], (L, KV_LORA, N_HEADS * (QK_NOPE + V_HEAD)), KV_LORA ** -0.5),
        'qn_g': gain(ks[16], QK_NOPE),
        'qr_g': gain(ks[17], QK_ROPE),
        'kn_g': gain(ks[18], QK_NOPE),
        'kr_g': gain(ks[19], QK_ROPE),
        'w_out': nrm(ks[20], (L, MIX_WIDTH, D_MODEL), MIX_WIDTH ** -0.5),
        'norm2_g': gain(ks[21], D_MODEL),
        'w_ffn_in': nrm(ks[22], (L, D_MODEL, 2 * D_FF), D_MODEL ** -0.5),
        'w_ffn_out': nrm(ks[23], (L, D_FF, D_MODEL), D_FF ** -0.5),
    }


def reference(x_prompt, x_sample, cache_ckv, cache_krope, c_prompt, c_sample, w_ada, b_ada, norm1_g, w_in, w_s, b_s, q_norm_g, w_uq, kv_norm_g, w_ukv, qn_g, qr_g, kn_g, kr_g, w_out, norm2_g, w_ffn_in, w_ffn_out):
    past_len = cache_ckv.shape[2]
    pos_p = jnp.arange(x_prompt.shape[1])
    pos_s = past_len + jnp.arange(x_sample.shape[1])
    yp, ys = x_prompt, x_sample
    ckv_p, kr_p, ckv_s, kr_s, v_s = [], [], [], [], []
    for l in range(DEPTH):
        w = dict(w_ada=w_ada[l], b_ada=b_ada[l], norm1_g=norm1_g[l], w_in=w_in[l], w_s=w_s[l], b_s=b_s[l],
                 q_norm_g=q_norm_g[l], w_uq=w_uq[l], kv_norm_g=kv_norm_g[l], w_ukv=w_ukv[l],
                 qn_g=qn_g[l], qr_g=qr_g[l], kn_g=kn_g[l], kr_g=kr_g[l], w_out=w_out[l],
                 norm2_g=norm2_g[l], w_ffn_in=w_ffn_in[l], w_ffn_out=w_ffn_out[l])
        yp, a_ckv, a_kr, _ = trunk_layer(yp, c_prompt, pos_p, None, w)
        ys, b_ckv, b_kr, b_v = trunk_layer(ys, c_sample, pos_s, (cache_ckv[l], cache_krope[l]), w)
        ckv_p.append(a_ckv)
        kr_p.append(a_kr)
        ckv_s.append(b_ckv)
        kr_s.append(b_kr)
        v_s.append(b_v)
    return (yp, ys, jnp.stack(ckv_p), jnp.stack(kr_p), jnp.stack(ckv_s), jnp.stack(kr_s), jnp.stack(v_s))
```

```python
import numpy as np
import concourse.bass as bass
import concourse.mybir as mybir
from concourse.bass_utils import run_bass_kernel_spmd

F32 = mybir.dt.float32
BF16 = mybir.dt.bfloat16
AF = mybir.ActivationFunctionType
ALU = mybir.AluOpType
AX = mybir.AxisListType

D = 1024
SEQ = 4096
DEC = 64
PAST = 2048
TB = 512
DFF = 2816
NJ = 22
EPS = 1e-6
SCALE = 96 ** -0.5
SLOT = 4352
NRING = 4
NDSEM = 8


def _dsize(dt):
    return 4 if dt == F32 else 2


class R:
    __slots__ = ("ap", "root", "lo", "hi")

    def __init__(self, ap, root, lo, hi):
        self.ap, self.root, self.lo, self.hi = ap, root, lo, hi


class T:
    def __init__(self, h, root, base_bytes, dtype, F, dram=False, whole=False):
        self.h, self.root, self.base, self.dt, self.F, self.dram, self.whole = h, root, base_bytes, dtype, F, dram, whole

    def __call__(self, p0, n_p, off, *dims):
        if not dims:
            raise ValueError("need dims")
        ap = bass.AP(self.h, p0 * self.F + off, [[self.F, n_p]] + [[s, c] for (s, c) in dims])
        ext = sum(s * (c - 1) for (s, c) in dims) + 1
        sz = _dsize(self.dt)
        if self.whole:
            return R(ap, self.root, 0, 1 << 30)
        return R(ap, self.root, self.base + off * sz, self.base + (off + ext) * sz)

    def d(self, off, *dims):
        ap = bass.AP(self.h, off, [[s, c] for (s, c) in dims])
        ext = sum(s * (c - 1) for (s, c) in dims) + 1
        sz = _dsize(self.dt)
        return R(ap, self.root, off * sz, (off + ext) * sz)


class Op:
    __slots__ = ("eng", "fn", "dma", "deps", "id", "seq", "dsem", "dval", "dprev")


class Sched:
    ENG = ("pe", "act", "dve", "pool", "sp")

    def __init__(self):
        self.ops = []
        self.eng_ops = {e: [] for e in self.ENG}
        self.segs = {}
        self.pending = {e: {} for e in self.ENG}
        self.dma_all = []
        self.ndma = 0

    def _touch(self, op, acc, is_write):
        segs = self.segs.setdefault(acc.root, [])
        lo, hi = acc.lo, acc.hi
        out = []
        covered = []
        for s in segs:
            if s[1] <= lo or s[0] >= hi:
                out.append(s)
                continue
            if s[0] < lo:
                out.append([s[0], lo, s[2], list(s[3])])
            if s[1] > hi:
                out.append([hi, s[1], s[2], list(s[3])])
            mid = [max(s[0], lo), min(s[1], hi), s[2], list(s[3])]
            covered.append(mid)
        for m in covered:
            if m[2] is not None and m[2] != op.id:
                if is_write:
                    op.deps.setdefault(m[2], False)
                else:
                    op.deps[m[2]] = True
            if is_write:
                for rd in m[3]:
                    if rd != op.id:
                        op.deps.setdefault(rd, False)
        if is_write:
            out.append([lo, hi, op.id, []])
        else:
            covered.sort()
            cur = lo
            for m in covered:
                if m[0] > cur:
                    out.append([cur, m[0], None, [op.id]])
                m[3].append(op.id)
                out.append(m)
                cur = m[1]
            if cur < hi:
                out.append([cur, hi, None, [op.id]])
        out.sort(key=lambda s: s[0])
        self.segs[acc.root] = out

    def add(self, eng, fn, r=(), w=(), dma=False):
        op = Op()
        op.eng, op.fn, op.dma, op.deps, op.id = eng, fn, dma, {}, len(self.ops)
        op.seq = op.dsem = op.dval = op.dprev = None
        for k, v in self.pending[eng].items():
            op.deps[k] = v
        self.pending[eng] = {}
        for a in r:
            self._touch(op, a, False)
        for a in w:
            self._touch(op, a, True)
        self.ops.append(op)
        self.eng_ops[eng].append(op)
        if dma:
            op.dsem = self.ndma % NDSEM
            op.dval = 16 * (self.ndma // NDSEM + 1)
            self.ndma += 1
            self.dma_all.append(op.id)
        else:
            op.seq = len([1 for o in self.eng_ops[eng]])
        return op

    def barrier(self):
        deps = {}
        for e in self.ENG:
            if e != "sp" and self.eng_ops[e]:
                deps[self.eng_ops[e][-1].id] = True
        for i in self.dma_all:
            deps[i] = True
        self.dma_all = []
        for e in self.ENG:
            self.pending[e].update(deps)

    def emit(self, nc, block, esem, dsems):
        ops = self.ops

        def run(name, e):
            known = {}
            for op in self.eng_ops[name]:
                for did in sorted(op.deps):
                    dop = ops[did]
                    raw = op.deps[did]
                    if dop.dma:
                        sem, val = dsems[dop.dsem], dop.dval
                    else:
                        if dop.eng == name and not op.dma:
                            if name == "pe":
                                continue
                        sem, val = esem[dop.eng], dop.seq
                    if known.get(sem.num, 0) >= val:
                        continue
                    e.wait_ge(sem, val)
                    known[sem.num] = val
                if op.dma:
                    sem = dsems[op.dsem]
                    if op.dval > 16 and known.get(sem.num, 0) < op.dval - 16:
                        e.wait_ge(sem, op.dval - 16)
                        known[sem.num] = op.dval - 16
                    ins = op.fn(e)
                    ins.then_inc(sem, 16)
                else:
                    ins = op.fn(e)
                    ins.then_inc(esem[name], 1)

        @block.tensor
        def _(e):
            run("pe", e)

        @block.scalar
        def _(e):
            run("act", e)

        @block.vector
        def _(e):
            run("dve", e)

        @block.gpsimd
        def _(e):
            run("pool", e)

        @block.sync
        def _(e):
            run("sp", e)
            for k in range(min(NDSEM, self.ndma)):
                n_uses = (self.ndma - 1 - k) // NDSEM + 1
                e.wait_ge(dsems[k], 16 * n_uses)


def build_program():
    nc = bass.Bass("TRN2", target_bir_lowering=False)
    S = Sched()

    def dram(name, shape, kind, dt=F32):
        h = nc.dram_tensor(name, list(shape), dt, kind=kind)
        return T(h, name, 0, dt, shape[-1], dram=True)

    x_p = dram("x_p", [SEQ, D], "ExternalInput")
    x_s = dram("x_s", [DEC, D], "ExternalInput")
    cckv = dram("cckv", [PAST, 256], "ExternalInput")
    ckr = dram("ckr", [PAST, 32], "ExternalInput")
    c2 = dram("c2", [2, D], "ExternalInput")
    w_ada = dram("w_ada", [D, 6 * D], "ExternalInput")
    b_ada = dram("b_ada", [1, 6 * D], "ExternalInput")
    n1g = dram("norm1_g", [1, D], "ExternalInput")
    n2g = dram("norm2_g", [1, D], "ExternalInput")
    w_in = dram("w_in", [D, 1696], "ExternalInput")
    w_s = dram("w_s", [512, 128], "ExternalInput")
    b_s = dram("b_s", [1, 512], "ExternalInput")
    qng = dram("q_norm_g", [1, 384], "ExternalInput")
    w_uq = dram("w_uq", [384, 768], "ExternalInput")
    kvg = dram("kv_norm_g", [1, 256], "ExternalInput")
    w_ukv = dram("w_ukv", [256, 1024], "ExternalInput")
    qn_g = dram("qn_g", [1, 64], "ExternalInput")
    qr_g = dram("qr_g", [1, 32], "ExternalInput")
    kn_g = dram("kn_g", [1, 64], "ExternalInput")
    kr_g = dram("kr_g", [1, 32], "ExternalInput")
    w_out = dram("w_out", [D, D], "ExternalInput")
    w_fi = dram("w_ffn_in", [D, 2 * DFF], "ExternalInput")
    w_fo = dram("w_ffn_out", [DFF, D], "ExternalInput")
    rope_p = dram("rope_p", [SEQ, 64], "ExternalInput")
    rope_s = dram("rope_s", [DEC, 64], "ExternalInput")

    y_p = dram("y_p", [SEQ, D], "ExternalOutput")
    y_s = dram("y_s", [DEC, D], "ExternalOutput")
    o_ckv_p = dram("o_ckv_p", [SEQ, 256], "ExternalOutput")
    o_kr_p = dram("o_kr_p", [SEQ, 32], "ExternalOutput")
    o_ckv_s = dram("o_ckv_s", [DEC, 256], "ExternalOutput")
    o_kr_s = dram("o_kr_s", [DEC, 32], "ExternalOutput")
    o_v_s = dram("o_v_s", [DEC, 512], "ExternalOutput")

    s_win_u = dram("s_win_u", [128, 4096], "Internal", BF16)
    s_win_v = dram("s_win_v", [128, 4096], "Internal", BF16)
    s_win_q = dram("s_win_q", [128, 3072], "Internal", BF16)
    s_win_kv = dram("s_win_kv", [128, 2304], "Internal", BF16)
    s_wuq = dram("s_wuq", [128, 2304], "Internal", BF16)
    s_wukv = dram("s_wukv", [128, 2048], "Internal", BF16)
    s_wout_a = [dram(f"s_wout_a{v}", [128, 4096], "Internal", BF16) for v in range(2)]
    s_wout_b = [dram(f"s_wout_b{v}", [64, 8192], "Internal", BF16) for v in range(2)]
    s_ffn_in = dram("s_ffn_in", [NJ * 128, 2048], "Internal", BF16)
    s_ffn_out = [dram(f"s_ffn_out{v}", [NJ * 128, 1024], "Internal", BF16) for v in range(2)]

    def sb(name, F, dt):
        h = nc.alloc_sbuf_tensor(name, [128, F], dt)
        return T(h, name, 0, dt, F)

    def view(t, dt):
        h = t.h.bitcast(dt)
        F = t.F * _dsize(t.dt) // _dsize(dt)
        return T(h, t.root, t.base, dt, F)

    KT = sb("KT", 8 * SEQ, BF16)
    VS = sb("VS", 32 * 8 * 65, BF16)
    XB = sb("XB", 4 * D, F32)
    HT = sb("HT", 8 * TB, BF16)
    RK = sb("RK", NJ * TB, BF16)
    RI = sb("RI", 4352, F32)
    RIb = view(RI, BF16)
    RING = sb("RING", NRING * SLOT, BF16)
    CF = sb("CF", 1152, F32)
    CB = sb("CB", 1536, BF16)
    IDF = sb("IDF", 128, F32)
    KTf = view(KT, F32)
    RINGf = view(RING, F32)
    VSf = view(VS, F32)

    PS = []
    PSb = []
    for k in range(8):
        h = nc.alloc_psum_tensor(f"ps{k}", [128, 512], F32)
        PS.append(T(h, f"ps{k}", 0, F32, 512, whole=True))
        PSb.append(T(h.bitcast(BF16), f"ps{k}", 0, BF16, 1024, whole=True))

    rot_state = [0]
    rot_n = [4]

    def rot():
        k = rot_state[0] % rot_n[0]
        rot_state[0] = (k + 1) % rot_n[0]
        return k

    C_A = {(0, 0): 0, (0, 1): 8, (1, 0): 16, (1, 1): 24}
    C_MOD, C_N1G, C_N2G, C_QNG = 32, 96, 104, 112
    C_GQ, C_QRG, C_KRG, C_KVG, C_KNG, C_SC, C_SEL, C_ONE = 128, 192, 224, 256, 512, 576, 640, 896
    C_QRS, C_KRS, C_GQ96 = 1024, 1056, 1088
    B_ID, B_WS, B_BS, B_ONE = 0, 128, 640, 1152

    add = S.add

    def dma(out, in_, slow=False):
        if slow:
            add("sp", lambda e: e.dma_start(out=out.ap, in_=in_.ap, allow_slow_non_contiguous=True), r=[in_], w=[out], dma=True)
        else:
            add("sp", lambda e: e.dma_start(out=out.ap, in_=in_.ap), r=[in_], w=[out], dma=True)

    add("pool", lambda e: e.memset(IDF(0, 128, 0, (1, 128)).ap, 0.0), w=[IDF(0, 128, 0, (1, 128))])
    add("pool", lambda e: e.affine_select(out=IDF(0, 128, 0, (1, 128)).ap, in_=IDF(0, 128, 0, (1, 128)).ap,
                                          pattern=[[-1, 128]], compare_op=ALU.not_equal, fill=1.0, base=0,
                                          channel_multiplier=1),
        r=[IDF(0, 128, 0, (1, 128))], w=[IDF(0, 128, 0, (1, 128))])
    add("pool", lambda e: e.tensor_copy(out=CB(0, 128, B_ID, (1, 128)).ap, in_=IDF(0, 128, 0, (1, 128)).ap),
        r=[IDF(0, 128, 0, (1, 128))], w=[CB(0, 128, B_ID, (1, 128))])
    add("pool", lambda e: e.memset(CB(0, 1, B_ONE, (1, 128)).ap, 1.0), w=[CB(0, 1, B_ONE, (1, 128))])
    add("pool", lambda e: e.memset(CF(0, 128, C_ONE, (1, 64)).ap, 1.0), w=[CF(0, 128, C_ONE, (1, 64))])
    add("pool", lambda e: e.memset(CF(0, 2, C_SEL, (1, 256)).ap, 0.0), w=[CF(0, 2, C_SEL, (1, 256))])
    add("pool", lambda e: e.affine_select(out=CF(0, 2, C_SEL, (1, 128)).ap, in_=CF(0, 2, C_SEL, (1, 128)).ap,
                                          pattern=[[0, 128]], compare_op=ALU.not_equal, fill=1.0, base=0,
                                          channel_multiplier=1),
        r=[CF(0, 2, C_SEL, (1, 128))], w=[CF(0, 2, C_SEL, (1, 128))])
    add("pool", lambda e: e.affine_select(out=CF(0, 2, C_SEL + 128, (1, 128)).ap, in_=CF(0, 2, C_SEL + 128, (1, 128)).ap,
                                          pattern=[[0, 128]], compare_op=ALU.not_equal, fill=1.0, base=-1,
                                          channel_multiplier=1),
        r=[CF(0, 2, C_SEL + 128, (1, 128))], w=[CF(0, 2, C_SEL + 128, (1, 128))])

    ident = lambda n: CB(0, n, B_ID, (1, n))

    dma(CF(0, 128, C_N1G, (1, 8)), n1g.d(0, (1, 128), (128, 8)), slow=True)
    dma(CF(0, 128, C_N2G, (1, 8)), n2g.d(0, (1, 128), (128, 8)), slow=True)
    dma(CF(0, 128, C_QNG, (1, 3)), qng.d(0, (1, 128), (128, 3)), slow=True)
    dma(CF(0, 128, C_GQ, (1, 64)), qn_g.d(0, (0, 128), (1, 64)))
    dma(CF(0, 128, C_KNG, (1, 64)), kn_g.d(0, (0, 128), (1, 64)))
    dma(CF(0, 128, C_QRG, (1, 32)), qr_g.d(0, (0, 128), (1, 32)))
    dma(CF(0, 128, C_KRG, (1, 32)), kr_g.d(0, (0, 128), (1, 32)))
    dma(CF(0, 128, C_KVG, (1, 256)), kvg.d(0, (0, 128), (1, 256)))
    for hf_ in range(2):
        dma(CF(0, 128, C_QRS + hf_ * 16, (1, 16)), qr_g.d((1 - hf_) * 16, (0, 128), (1, 16)))
        dma(CF(0, 128, C_KRS + hf_ * 16, (1, 16)), kr_g.d((1 - hf_) * 16, (0, 128), (1, 16)))
    add("dve", lambda e: e.tensor_tensor(out=CF(0, 128, C_GQ, (1, 64)).ap, in0=CF(0, 128, C_GQ, (1, 64)).ap,
                                         in1=CF(0, 128, C_KNG, (1, 64)).ap, op=ALU.mult),
        r=[CF(0, 128, C_GQ, (1, 64)), CF(0, 128, C_KNG, (1, 64))], w=[CF(0, 128, C_GQ, (1, 64))])
    add("pool", lambda e: e.memset(CF(0, 128, C_GQ96, (1, 1)).ap, 1.0), w=[CF(0, 128, C_GQ96, (1, 1))])
    add("pool", lambda e: e.memset(CF(0, 128, C_GQ96 + 1, (1, 1)).ap, 0.0), w=[CF(0, 128, C_GQ96 + 1, (1, 1))])
    add("dve", lambda e: e.tensor_tensor(out=CF(0, 64, C_KNG, (1, 64)).ap, in0=CF(0, 64, C_GQ, (1, 64)).ap,
                                         in1=IDF(0, 64, 0, (1, 64)).ap, op=ALU.mult),
        r=[CF(0, 64, C_GQ, (1, 64)), IDF(0, 64, 0, (1, 64))], w=[CF(0, 64, C_KNG, (1, 64))])
    add("dve", lambda e: e.tensor_reduce(out=CF(0, 64, C_GQ96, (1, 1)).ap, in_=CF(0, 64, C_KNG, (1, 64)).ap, axis=AX.X, op=ALU.add),
        r=[CF(0, 64, C_KNG, (1, 64))], w=[CF(0, 64, C_GQ96, (1, 1))])

    dma(RI(0, 128, 0, (128, 4), (1, 128)), w_s.d(0, (128, 128), (128 * 128, 4), (1, 128)))
    add("dve", lambda e: e.tensor_copy(out=RIb(0, 128, 1024, (1, 512)).ap, in_=RI(0, 128, 0, (1, 512)).ap),
        r=[RI(0, 128, 0, (1, 512))], w=[RIb(0, 128, 1024, (1, 512))])
    for g in range(4):
        add("pe", lambda e, g=g: e.transpose(out=PSb[0](0, 128, g * 128, (1, 128)).ap,
                                             in_=RIb(0, 128, 1024 + g * 128, (1, 128)).ap, identity=ident(128).ap),
            r=[RIb(0, 128, 1024 + g * 128, (1, 128)), ident(128)], w=[PSb[0](0, 128, g * 128, (1, 128))])
    add("dve", lambda e: e.tensor_copy(out=CB(0, 128, B_WS, (1, 512)).ap, in_=PSb[0](0, 128, 0, (1, 512)).ap),
        r=[PSb[0](0, 128, 0, (1, 512))], w=[CB(0, 128, B_WS, (1, 512))])
    add("dve", lambda e: e.memset(CB(64, 64, B_WS, (128, 4), (1, 64)).ap, 0.0), w=[CB(64, 64, B_WS, (128, 4), (1, 64))])
    dma(RI(0, 1, 1024, (1, 512)), b_s.d(0, (512, 1), (1, 512)))
    add("dve", lambda e: e.tensor_copy(out=CB(0, 1, B_BS, (1, 512)).ap, in_=RI(0, 1, 1024, (1, 512)).ap),
        r=[RI(0, 1, 1024, (1, 512))], w=[CB(0, 1, B_BS, (1, 512))])

    prep_i = [0]

    def prep_bufs():
        i = prep_i[0]
        prep_i[0] += 1
        return (i % 2) * 5632, (i % 2) * 5632

    def act_copy(o, i):
        add("act", lambda e: e.activation(out=o.ap, in_=i.ap, func=AF.Copy), r=[i], w=[o])

    pieces = []

    for kc in range(8):

        def ld(fo, kc=kc):
            dma(KTf(0, 128, fo, (1, 1696)), w_in.d(kc * 128 * 1696, (1696, 128), (1, 1696)))

        def rest(fo, bo, kc=kc):
            act_copy(RK(0, 128, bo, (1, 1696)), KTf(0, 128, fo, (1, 1696)))
            dma(s_win_u.d(kc * 128, (4096, 128), (1024, 4), (1, 128)), RK(0, 128, bo, (128, 4), (1, 128)))
            dma(s_win_v.d(kc * 512, (4096, 128), (1, 512)), RK(0, 128, bo + 512, (1, 512)))
            dma(s_win_q.d(kc * 384, (3072, 128), (1, 384)), RK(0, 128, bo + 1024, (1, 384)))
            dma(s_win_kv.d(kc * 288, (2304, 128), (1, 288)), RK(0, 128, bo + 1408, (1, 288)))
        pieces.append((ld, rest, False))
    for kc in range(3):

        def ld(fo, kc=kc):
            dma(KTf(0, 128, fo, (1, 768)), w_uq.d(kc * 128 * 768, (768, 128), (1, 768)))

        def rest(fo, bo, kc=kc):
            sc = CF(0, 128, C_QNG + kc, (1, 1))
            for (io, n, oo) in ((0, 64, 0), (64, 32, 512)):
                i_ = KTf(0, 128, fo + io, (96, 8), (1, n))
                o_ = RK(0, 128, bo + oo, (n, 8), (1, n))
                add("dve", lambda e, i_=i_, o_=o_, sc=sc: e.tensor_scalar(out=o_.ap, in0=i_.ap, scalar1=sc.ap, scalar2=None, op0=ALU.mult),
                    r=[i_, sc], w=[o_])
            dma(s_wuq.d(kc * 768, (2304, 128), (1, 768)), RK(0, 128, bo, (1, 768)))
        pieces.append((ld, rest, False))
    for kc in range(2):

        def ld(fo, kc=kc):
            dma(KTf(0, 128, fo, (1, 1024)), w_ukv.d(kc * 128 * 1024, (1024, 128), (1, 1024)))

        def rest(fo, bo, kc=kc):
            for part in range(2):
                act_copy(RK(0, 128, bo + part * 512, (64, 8), (1, 64)), KTf(0, 128, fo + part * 64, (128, 8), (1, 64)))
            dma(s_wukv.d(kc * 1024, (2048, 128), (1, 1024)), RK(0, 128, bo, (1, 1024)))
        pieces.append((ld, rest, False))
    for kc in range(8):

        def ld(fo, kc=kc):
            dma(KTf(0, 128, fo, (1, 1024)), w_out.d(kc * 128 * 1024, (1024, 128), (1, 1024)))

        def rest(fo, bo, kc=kc):
            stg = KTf(0, 128, fo, (1, 1024))
            for v in range(2):
                ob = RK(0, 128, bo + v * 1024, (1, 1024))
                gb = GBC(v, 0, 0, 1024)
                add("dve", lambda e, stg=stg, ob=ob, gb=gb: e.tensor_tensor(out=ob.ap, in0=stg.ap, in1=gb.ap, op=ALU.mult),
                    r=[stg, gb], w=[ob])
                if kc < 4:
                    dma(s_wout_a[v].d(kc * 1024, (4096, 128), (1, 1024)), ob)
                else:
                    m = kc - 4
                    dma(s_wout_b[v].d((2 * m) * 1024, (8192, 64), (1, 1024)), RK(0, 64, bo + v * 1024, (1, 1024)))
                    dma(s_wout_b[v].d((2 * m + 1) * 1024, (8192, 64), (1, 1024)), RK(64, 64, bo + v * 1024, (1, 1024)))
        pieces.append((ld, rest, True))
    for j2 in range(NJ // 2):

        def ld(fo, j2=j2):
            dma(KTf(0, 128, fo, (1024, 2), (1, 1024)), w_fo.d(j2 * 2 * 128 * 1024, (1024, 128), (128 * 1024, 2), (1, 1024)))

        def rest(fo, bo, j2=j2):
            stg = KTf(0, 128, fo, (1024, 2), (1, 1024))
            for v in range(2):
                ob = RK(0, 128, bo + v * 2048, (1024, 2), (1, 1024))
                gb = R(bass.AP(XB.h, (v * 2 + 1) * 1024, [[XB.F, 128], [0, 2], [1, 1024]]), XB.root, (v * 2 + 1) * 4096, (v * 2 + 2) * 4096)
                add("dve", lambda e, stg=stg, ob=ob, gb=gb: e.tensor_tensor(out=ob.ap, in0=stg.ap, in1=gb.ap, op=ALU.mult),
                    r=[stg, gb], w=[ob])
                dma(s_ffn_out[v].d(j2 * 2 * 128 * 1024, (1024, 128), (128 * 1024, 2), (1, 1024)), ob)
        pieces.append((ld, rest, True))
    for kc in range(8):

        def ld(fo, kc=kc):
            dma(KTf(0, 128, fo, (1, 5632)), w_fi.d(kc * 128 * 5632, (5632, 128), (1, 5632)))

        def rest(fo, bo, kc=kc):
            for gu in range(2):
                act_copy(RK(0, 128, bo + gu * 128, (256, NJ), (1, 128)), KTf(0, 128, fo + gu * DFF, (128, NJ), (1, 128)))
            for jh in range(2):
                dma(s_ffn_in.d(jh * 11 * 128 * 2048 + kc * 256, (2048, 128), (128 * 2048, 11), (1, 256)),
                    RK(0, 128, bo + jh * 11 * 256, (256, 11), (1, 256)))
        pieces.append((ld, rest, False))


    order = [i for i, p_ in enumerate(pieces) if not p_[2]] + [i for i, p_ in enumerate(pieces) if p_[2]]
    N_NOGATE = len([1 for p_ in pieces if not p_[2]])
    prep_pos = [0]

    def pump(n, limit=None):
        limit = len(order) if limit is None else limit
        while n > 0 and prep_pos[0] < limit:
            k = prep_pos[0]
            if k == 0:
                pieces[order[0]][0]((0) * 5632)
            if k + 1 < len(order) and k + 1 < limit + 1:
                if k + 1 < len(order):
                    pieces[order[k + 1]][0](((k + 1) % 2) * 5632)
            pieces[order[k]][1]((k % 2) * 5632, (k % 2) * 5632)
            prep_pos[0] += 1
            n -= 1

    for b_ in range(2):
        dma(CF(0, 128, C_SC + b_, (2, 8)), c2.d(b_ * D, (1, 128), (128, 8)), slow=True)
    add("act", lambda e: e.activation(out=CF(0, 128, C_SC, (1, 16)).ap, in_=CF(0, 128, C_SC, (1, 16)).ap, func=AF.Silu),
        r=[CF(0, 128, C_SC, (1, 16))], w=[CF(0, 128, C_SC, (1, 16))])
    MSB = lambda off, n: VSf(0, 2, off, (1, n))
    for half in range(2):
        dma(RI(0, 2, 0, (1, 3072)), b_ada.d(half * 3072, (0, 2), (1, 3072)))
        for kc in range(8):
            st = (half * 8 + kc) % 2
            stg = RINGf(0, 128, st * 3072, (1, 3072))
            dma(stg, w_ada.d(kc * 128 * 6144 + half * 3072, (6144, 128), (1, 3072)))

            def mm(e, kc=kc, st=st):
                ins = None
                for cb in range(6):
                    ins = e.matmul(PS[cb](0, 2, 0, (1, 512)).ap, lhsT=CF(0, 128, C_SC + kc * 2, (1, 2)).ap,
                                   rhs=RINGf(0, 128, st * 3072 + cb * 512, (1, 512)).ap, start=(kc == 0), stop=(kc == 7))
                return ins
            add("pe", mm, r=[stg, CF(0, 128, C_SC, (1, 16))], w=[PS[cb](0, 2, 0, (1, 512)) for cb in range(6)])
            if kc % 2 == 1:
                pump(1, N_NOGATE)
        for cb in range(6):
            col = half * 3072 + cb * 512
            add("dve", lambda e, cb=cb, col=col: e.tensor_tensor(out=MSB(col, 512).ap, in0=PS[cb](0, 2, 0, (1, 512)).ap,
                                                                 in1=RI(0, 2, cb * 512, (1, 512)).ap, op=ALU.add),
                r=[PS[cb](0, 2, 0, (1, 512)), RI(0, 2, cb * 512, (1, 512))], w=[MSB(col, 512)])
    WH = [0, 1, 3, 4]

    def tr_mod(e):
        ins = None
        for wi, wh in enumerate(WH):
            for kc in range(8):
                ins = e.transpose(out=PS[6](0, 128, (wi * 8 + kc) * 2, (1, 2)).ap,
                                  in_=MSB(wh * 1024 + kc * 128, 128).ap, identity=IDF(0, 2, 0, (1, 2)).ap)
        return ins
    add("pe", tr_mod, r=[MSB(0, 6144), IDF(0, 2, 0, (1, 2))], w=[PS[6](0, 128, 0, (1, 64))])
    add("dve", lambda e: e.tensor_copy(out=CF(0, 128, C_MOD, (1, 64)).ap, in_=PS[6](0, 128, 0, (1, 64)).ap),
        r=[PS[6](0, 128, 0, (1, 64))], w=[CF(0, 128, C_MOD, (1, 64))])
    for stage, (wi, goff) in enumerate([(1, C_N1G), (3, C_N2G)]):
        for b in range(2):
            src = CF(0, 128, C_MOD + wi * 16 + b, (2, 8))
            dst = CF(0, 128, C_A[(stage, b)], (1, 8))
            add("dve", lambda e, src=src, dst=dst, goff=goff: e.scalar_tensor_tensor(
                out=dst.ap, in0=src.ap, scalar=1.0, in1=CF(0, 128, goff, (1, 8)).ap, op0=ALU.add, op1=ALU.mult),
                r=[src, CF(0, 128, goff, (1, 8))], w=[dst])

    def a_col(stage, b, fc):
        return CF(0, 128, C_A[(stage, b)] + fc, (1, 1))

    def b_col(stage, b, fc):
        wi = 0 if stage == 0 else 2
        return CF(0, 128, C_MOD + wi * 16 + fc * 2 + b, (1, 1))

    def GBC(b, which, off, n):
        return XB(0, 128, (b * 2 + which) * 1024 + off, (1, n))
    for b in range(2):
        for which, wh in enumerate([2, 5]):
            for half in range(2):
                k = 4 + ((b * 4 + which * 2 + half) % 2)
                add("pe", lambda e, b=b, wh=wh, half=half, k=k: e.matmul(
                    PS[k](0, 128, 0, (1, 512)).ap, lhsT=CF(0, 2, C_SEL + b * 128, (1, 128)).ap,
                    rhs=MSB(wh * 1024 + half * 512, 512).ap, start=True, stop=True),
                    r=[CF(0, 2, C_SEL, (1, 256)), MSB(wh * 1024 + half * 512, 512)], w=[PS[k](0, 128, 0, (1, 512))])
                add("act", lambda e, b=b, which=which, half=half, k=k: e.activation(
                    out=GBC(b, which, half * 512, 512).ap, in_=PS[k](0, 128, 0, (1, 512)).ap, func=AF.Copy),
                    r=[PS[k](0, 128, 0, (1, 512))], w=[GBC(b, which, half * 512, 512)])

    pump(1000)
    S.barrier()
    add("pool", lambda e: e.memset(VS(0, 128, 64, (65, 256), (1, 1)).ap, 1.0), w=[VS(0, 128, 64, (65, 256), (1, 1))])

    ring_n = [0]

    def ring_load(pieces):
        k = ring_n[0] % NRING
        ring_n[0] += 1
        base = k * SLOT
        for fn in pieces:
            dst, src = fn(base)
            dma(dst, src)
        return base

    def ld_full(src_t, n, np_=128, src_off=0, row=None):
        row = row if row is not None else src_t.F
        return lambda base: (RING(0, np_, base, (1, n)), src_t.d(src_off, (row, np_), (1, n)))

    def norm_stats(tt, rows):
        if True:
            xin = XB(0, rows, tt * D, (1, D))
            junk = RK(0, rows, tt * D, (1, D))
            ss = RI(0, rows, 4300 + tt, (1, 1))
            rs = RI(0, rows, 4310 + tt, (1, 1))
            add("act", lambda e, xin=xin, junk=junk, ss=ss: e.activation(out=junk.ap, in_=xin.ap, func=AF.Square, accum_out=ss.ap),
                r=[xin], w=[junk, ss])
            add("act", lambda e, ss=ss, rs=rs: e.activation(out=rs.ap, in_=ss.ap, func=AF.Sqrt, scale=1.0 / D, bias=EPS_AP(rows).ap),
                r=[ss, EPS_AP(rows)], w=[rs])
            add("dve", lambda e, rs=rs: e.reciprocal(out=rs.ap, in_=rs.ap), r=[rs], w=[rs])
            add("dve", lambda e, xin=xin, junk=junk, rs=rs: e.tensor_scalar(out=junk.ap, in0=xin.ap, scalar1=rs.ap, scalar2=None, op0=ALU.mult),
                r=[xin, rs], w=[junk])

    def norm_T(stage, b, rows, NT, NQ, skip_stats=False, early=False):
        if not skip_stats:
            for tt in range(NT):
                norm_stats(tt, rows)
        XHT, xh0 = (RIb, 4096) if early else (RK, 0)
        for fcp in range(4):
            k = rot()

            def tr(e, fcp=fcp, k=k):
                ins = None
                for fcl in range(2):
                    fc = fcp * 2 + fcl
                    for tt in range(NT):
                        ins = e.transpose(out=PSb[k](0, 128, fcl * 512 + tt * rows, (1, rows)).ap,
                                          in_=XHT(0, rows, xh0 + tt * D + fc * 128, (1, 128)).ap, identity=ident(rows).ap)
                return ins
            add("pe", tr, r=[XHT(0, rows, xh0, (1, NT * D)), ident(rows)], w=[PSb[k](0, 128, 0, (1, 1024))])
            for fcl in range(2):
                fc = fcp * 2 + fcl
                o = HT(0, 128, fc * TB, (1, NQ))
                i = PSb[k](0, 128, fcl * 512, (1, NQ))
                add("act", lambda e, o=o, i=i, fc=fc: e.activation(out=o.ap, in_=i.ap, func=AF.Identity,
                                                                   scale=a_col(stage, b, fc).ap, bias=b_col(stage, b, fc).ap),
                    r=[i, a_col(stage, b, fc), b_col(stage, b, fc)], w=[o])

    XS = lambda: RI(0, 128, 1024, (1, D))

    def early_load(tok0):
        dma(XS(), x_p.d(tok0 * D, (D, 128), (1, D)))

    def early_stats(tt):
        xin = XS()
        junk = RIb(0, 128, 4096 + tt * D, (1, D))
        ss = RI(0, 128, 4320 + tt, (1, 1))
        rs = RI(0, 128, 4330 + tt, (1, 1))
        add("act", lambda e: e.activation(out=junk.ap, in_=xin.ap, func=AF.Square, accum_out=ss.ap), r=[xin], w=[junk, ss])
        add("act", lambda e: e.activation(out=rs.ap, in_=ss.ap, func=AF.Sqrt, scale=1.0 / D, bias=EPS_AP(128).ap),
            r=[ss, EPS_AP(128)], w=[rs])
        add("dve", lambda e: e.reciprocal(out=rs.ap, in_=rs.ap), r=[rs], w=[rs])
        add("dve", lambda e: e.tensor_scalar(out=junk.ap, in0=xin.ap, scalar1=rs.ap, scalar2=None, op0=ALU.mult),
            r=[xin, rs], w=[junk])

    def EPS_AP(rows):
        return CF(0, rows, 960, (1, 1))
    add("pool", lambda e: e.memset(CF(0, 128, 960, (1, 1)).ap, EPS), w=[CF(0, 128, 960, (1, 1))])

    O_UT, O_V, O_YA, O_YB = 0, 2048, 4096, 6144
    T_SQ, T_SQR, T_QN, T_QR, T_T1, T_T2, T_CKV, T_KR, T_KRO, T_ST = 0, 512, 768, 1280, 1536, 1792, 2048, 2560, 2624, 2688
    TB_QF, TB_KF, TB_CQN, TB_CQT, TB_CKVB, TB_CKVT = 5504, 6272, 7040, 7424, 7808, 8064
    T_ROPE = 4160
    A_PT = [0, 512, 1024, 1536]
    A_OSB, A_RC = 1024, 1536

    def rmsnorm_stats(src, rows, n_in, ss, rs, junk):
        add("act", lambda e: e.activation(out=junk.ap, in_=src.ap, func=AF.Square, accum_out=ss.ap), r=[src], w=[junk, ss])
        add("act", lambda e: e.activation(out=rs.ap, in_=ss.ap, func=AF.Sqrt, scale=1.0 / n_in, bias=EPS_AP(rows).ap),
            r=[ss, EPS_AP(rows)], w=[rs])
        add("dve", lambda e: e.reciprocal(out=rs.ap, in_=rs.ap), r=[rs], w=[rs])

    def rope_ops(x, out, rows, nh, tab):
        xo, oo = x, out
        t1 = RI(0, rows, T_T1, (32, nh), (1, 32))
        add("pool", lambda e: e.tensor_tensor(out=t1.ap, in0=RI(0, rows, xo, (32, nh), (1, 32)).ap,
                                              in1=RI(0, rows, tab, (0, nh), (1, 32)).ap, op=ALU.mult),
            r=[RI(0, rows, xo, (32, nh), (1, 32)), RI(0, rows, tab, (1, 32))], w=[t1])
        for hf in range(2):
            t2 = RI(0, rows, T_T2 + hf * 16, (32, nh), (1, 16))
            xi = RI(0, rows, xo + (1 - hf) * 16, (32, nh), (1, 16))
            sn = RI(0, rows, tab + 32 + hf * 16, (0, nh), (1, 16))
            add("pool", lambda e, t2=t2, xi=xi, sn=sn: e.tensor_tensor(out=t2.ap, in0=xi.ap, in1=sn.ap, op=ALU.mult),
                r=[xi, RI(0, rows, tab + 32, (1, 32))], w=[t2])
        t2a = RI(0, rows, T_T2, (32, nh), (1, 32))
        add("pool", lambda e: e.tensor_tensor(out=oo.ap, in0=t1.ap, in1=t2a.ap, op=ALU.add), r=[t1, t2a], w=[oo])

    RKf = view(RK, F32)

    def ts_default():
        return dict(f=RI, b=RIb, sq=T_SQ, st=T_ST + 16, ckvb=TB_CKVB, ckvT=TB_CKVT, kfull=TB_KF)

    def ts_rk(i):
        B = i * 3072
        return dict(f=RKf, b=RK, sq=B // 2, cf=B // 2 + 512, kf=B // 2 + 768, st=B // 2 + 800,
                    ckvb=B + 1664, ckvT=B + 1920, kfull=B + 2176)

    def kv_steps(ckvf, kropef, kt, rows, w_kv_base, ts, pre=None, alloc=None):
        alloc = alloc or rot
        TF, TBb = ts["f"], ts["b"]
        o_sq, o_st, o_cb, o_ct, o_kf = ts["sq"], ts["st"], ts["ckvb"], ts["ckvT"], ts["kfull"]
        st_ = {}
        steps = []
        if pre is not None:
            steps.append(pre)

        def s1():
            ckvb = TBb(0, rows, o_cb, (1, 256))
            add("pool", lambda e: e.tensor_copy(out=ckvb.ap, in_=ckvf.ap), r=[ckvf], w=[ckvb])
        steps.append(s1)

        def s2():
            k = alloc()
            st_["k"] = k

            def tr(e):
                ins = None
                for kc in range(2):
                    ins = e.transpose(out=PSb[k](0, 128, kc * 128, (1, rows)).ap,
                                      in_=TBb(0, rows, o_cb + kc * 128, (1, 128)).ap, identity=ident(rows).ap)
                return ins
            add("pe", tr, r=[TBb(0, rows, o_cb, (1, 256)), ident(rows)], w=[PSb[k](0, 128, 0, (1, 256))])
        steps.append(s2)

        def s3():
            k = st_["k"]
            ckvT = TBb(0, 128, o_ct, (128, 2), (1, rows))
            add("dve", lambda e: e.tensor_copy(out=ckvT.ap, in_=PSb[k](0, 128, 0, (128, 2), (1, rows)).ap),
                r=[PSb[k](0, 128, 0, (1, 256))], w=[ckvT])
        steps.append(s3)

        def s4():
            ka, kb = alloc(), alloc()
            st_["ka"], st_["kb"] = ka, kb

            def mm(e):
                ins = None
                for kc in range(2):
                    for part, kk in ((0, ka), (1, kb)):
                        ins = e.matmul(PS[kk](0, rows, 0, (1, 512)).ap, lhsT=TBb(0, 128, o_ct + kc * 128, (1, rows)).ap,
                                       rhs=RING(0, 128, w_kv_base + kc * 1024 + part * 512, (1, 512)).ap,
                                       start=(kc == 0), stop=(kc == 1))
                return ins
            add("pe", mm, r=[TBb(0, 128, o_ct, (1, 256)), RING(0, 128, w_kv_base, (1, 2048))],
                w=[PS[ka](0, rows, 0, (1, 512)), PS[kb](0, rows, 0, (1, 512))])
        steps.append(s4)

        def s5():
            ka, kb = st_["ka"], st_["kb"]
            vdst = VS(0, rows, kt * 520, (65, 8), (1, 64))
            sq = TF(0, rows, o_sq, (1, 512))
            add("act", lambda e: e.activation(out=sq.ap, in_=PS[ka](0, rows, 0, (1, 512)).ap, func=AF.Square),
                r=[PS[ka](0, rows, 0, (1, 512))], w=[sq])
            add("act", lambda e: e.activation(out=vdst.ap, in_=PS[kb](0, rows, 0, (64, 8), (1, 64)).ap, func=AF.Copy),
                r=[PS[kb](0, rows, 0, (1, 512))], w=[vdst])
        steps.append(s5)

        def s6():
            st = TF(0, rows, o_st, (1, 8))
            add("dve", lambda e: e.tensor_reduce(out=st.ap, in_=TF(0, rows, o_sq, (64, 8), (1, 64)).ap, axis=AX.X, op=ALU.add),
                r=[TF(0, rows, o_sq, (1, 512))], w=[st])
        steps.append(s6)

        def s7():
            st = TF(0, rows, o_st, (1, 8))
            add("act", lambda e: e.activation(out=st.ap, in_=st.ap, func=AF.Sqrt, scale=1.0 / 64, bias=EPS_AP(rows).ap),
                r=[st, EPS_AP(rows)], w=[st])
        steps.append(s7)

        def s8():
            st = TF(0, rows, o_st, (1, 8))
            add("dve", lambda e: e.reciprocal(out=st.ap, in_=st.ap), r=[st], w=[st])
        steps.append(s8)

        def s9():
            ka = st_["ka"]
            st = TF(0, rows, o_st, (1, 8))
            kf_n = TBb(0, rows, o_kf, (96, 8), (1, 64))
            add("dve", lambda e: e.tensor_tensor(out=kf_n.ap, in0=PS[ka](0, rows, 0, (64, 8), (1, 64)).ap,
                                                 in1=TF(0, rows, o_st, (1, 8), (0, 64)).ap, op=ALU.mult),
                r=[PS[ka](0, rows, 0, (1, 512)), st], w=[kf_n])
            kf_r = TBb(0, rows, o_kf + 64, (96, 8), (1, 32))
            krb = R(bass.AP(kropef.ap.tensor, kropef.ap.offset, [list(kropef.ap.ap[0]), [0, 8], [1, 32]]), kropef.root, kropef.lo, kropef.hi)
            add("pool", lambda e: e.tensor_copy(out=kf_r.ap, in_=krb.ap), r=[kropef], w=[kf_r])
        steps.append(s9)

        def s10():
            kfull = TBb(0, rows, o_kf, (1, 768))
            for hh in range(2):
                k2 = alloc()
                st_["k2", hh] = k2

                def trk(e, hh=hh, k2=k2):
                    ins = None
                    for hl in range(4):
                        h = hh * 4 + hl
                        ins = e.transpose(out=PSb[k2](0, 96, hl * 128, (1, rows)).ap,
                                          in_=TBb(0, rows, o_kf + h * 96, (1, 96)).ap, identity=ident(rows).ap)
                    return ins
                add("pe", trk, r=[kfull, ident(rows)], w=[PSb[k2](0, 96, 0, (1, 512))])
        steps.append(s10)

        def s11():
            for hh in range(2):
                k2 = st_["k2", hh]
                kdst = KT(0, 96, (hh * 4) * SEQ + kt * 128, (SEQ, 4), (1, rows))
                add("act" if hh == 0 else "dve",
                    (lambda e, kdst=kdst, k2=k2: e.activation(out=kdst.ap, in_=PSb[k2](0, 96, 0, (128, 4), (1, rows)).ap, func=AF.Copy)) if hh == 0 else
                    (lambda e, kdst=kdst, k2=k2: e.tensor_copy(out=kdst.ap, in_=PSb[k2](0, 96, 0, (128, 4), (1, rows)).ap)),
                    r=[PSb[k2](0, 96, 0, (1, 512))], w=[kdst])
        steps.append(s11)
        return steps

    def mk_alloc(banks):
        st = [0]

        def alloc():
            k = banks[st[0] % len(banks)]
            st[0] += 1
            return k
        return alloc

    def run_chains(chains, lo=0, hi=None):
        n = max(len(c) for c in chains)
        hi = n if hi is None else min(hi, n)
        for si in range(lo, hi):
            for c in chains:
                if si < len(c):
                    c[si]()

    def kv_common(ckvf, kropef, kt, rows, w_kv_base):
        run_chains([kv_steps(ckvf, kropef, kt, rows, w_kv_base, ts_default())])

    early_state = {"done": False}
    PRE = {}

    def process_block(is_s, blk):
        b = 1 if is_s else 0
        rows = 64 if is_s else 128
        NT = 1 if is_s else 4
        NQ = rows * NT
        xsrc = x_s if is_s else x_p
        ydst = y_s if is_s else y_p
        rsrc = rope_s if is_s else rope_p
        ockv = o_ckv_s if is_s else o_ckv_p
        okr = o_kr_s if is_s else o_kr_p
        t0 = 0 if is_s else blk * TB
        kt0 = 16 if is_s else blk * 4

        dma(XB(0, rows, 0, (D, NT), (1, D)), xsrc.d(t0 * D, (D, rows), (rows * D, NT), (1, D)))
        if is_s:
            wkv0 = ring_load([ld_full(s_win_kv, 2304), lambda base: (RING(0, 128, base + 2304, (1, 2048)), s_wukv.d(0, (2048, 128), (1, 2048)))])
            rot_n[0] = 4
            for g0 in range(0, 16, 4):
                chains = []
                for gi_ in range(4):
                    kt = g0 + gi_
                    if gi_ == 0:
                        ts = ts_default()
                        cf = RI(0, 128, T_CKV, (1, 256))
                        kf = RI(0, 128, T_KRO, (1, 32))
                    else:
                        ts = ts_rk(gi_ - 1)
                        cf = RKf(0, 128, ts["cf"], (1, 256))
                        kf = RKf(0, 128, ts["kf"], (1, 32))

                    def pre(cf=cf, kf=kf, kt=kt):
                        dma(cf, cckv.d(kt * 128 * 256, (256, 128), (1, 256)))
                        dma(kf, ckr.d(kt * 128 * 32, (32, 128), (1, 32)))
                    chains.append(kv_steps(cf, kf, kt, 128, wkv0 + 2304, ts, pre=pre, alloc=mk_alloc([2 * gi_, 2 * gi_ + 1])))
                run_chains(chains)
            rot_n[0] = 4
        if early_state["done"]:
            early_state["done"] = False
        else:
            norm_T(0, b, rows, NT, NQ)
        def pre_or_load(name, pieces):
            v_ = PRE.pop(name, None)
            return v_ if v_ is not None else ring_load(pieces)
        wu = pre_or_load("wu", [ld_full(s_win_u, 4096)])
        wv = pre_or_load("wv", [ld_full(s_win_v, 4096)])
        wq = pre_or_load("wq", [ld_full(s_win_q, 3072)])
        wkv = pre_or_load("wkv", [ld_full(s_win_kv, 2304), lambda base: (RING(0, 128, base + 2304, (1, 2048)), s_wukv.d(0, (2048, 128), (1, 2048)))])

        def emit_IN_u(falloc):
            for c in range(4):
                k = falloc()

                def mm(e, c=c, k=k):
                    ins = None
                    for kc in range(8):
                        ins = e.matmul(PS[k](0, 128, 0, (1, NQ)).ap, lhsT=RING(0, 128, wu + c * 1024 + kc * 128, (1, 128)).ap,
                                       rhs=HT(0, 128, kc * TB, (1, NQ)).ap, start=(kc == 0), stop=(kc == 7))
                    return ins
                add("pe", mm, r=[RING(0, 128, wu, (1, 4096)), HT(0, 128, 0, (1, 8 * TB))], w=[PS[k](0, 128, 0, (1, NQ))])
                o = RK(0, 128, O_UT + c * TB, (1, NQ))
                add("act", lambda e, o=o, k=k: e.activation(out=o.ap, in_=PS[k](0, 128, 0, (1, NQ)).ap, func=AF.Gelu_apprx_tanh),
                    r=[PS[k](0, 128, 0, (1, NQ))], w=[o])

        def emit_IN_v(falloc):
            for tt in range(NT):
                k = falloc()

                def mm(e, tt=tt, k=k):
                    ins = None
                    for kc in range(8):
                        ins = e.matmul(PS[k](0, rows, 0, (1, 512)).ap, lhsT=HT(0, 128, kc * TB + tt * rows, (1, rows)).ap,
                                       rhs=RING(0, 128, wv + kc * 512, (1, 512)).ap, start=(kc == 0), stop=(kc == 7))
                    return ins
                add("pe", mm, r=[RING(0, 128, wv, (1, 4096)), HT(0, 128, 0, (1, 8 * TB))], w=[PS[k](0, rows, 0, (1, 512))])
                o = RK(0, rows, O_V + tt * 512, (1, 512))
                if is_s:
                    vf = RI(0, rows, T_SQ, (1, 512))
                    add("act", lambda e, vf=vf, k=k: e.activation(out=vf.ap, in_=PS[k](0, rows, 0, (1, 512)).ap, func=AF.Gelu_apprx_tanh),
                        r=[PS[k](0, rows, 0, (1, 512))], w=[vf])
                    dma(o_v_s.d(0, (512, rows), (1, 512)), vf)
                    add("pool", lambda e, o=o, vf=vf: e.tensor_copy(out=o.ap, in_=vf.ap), r=[vf], w=[o])
                else:
                    add("act", lambda e, o=o, k=k: e.activation(out=o.ap, in_=PS[k](0, rows, 0, (1, 512)).ap, func=AF.Gelu_apprx_tanh),
                        r=[PS[k](0, rows, 0, (1, 512))], w=[o])

        def emit_G(falloc):
            for g in range(4):
                k = falloc()

                def mm(e, g=g, k=k):
                    ins = None
                    for tt in range(NT):
                        e.matmul(PS[k](0, 128, tt * rows, (1, rows)).ap, lhsT=RK(0, rows, O_V + tt * 512 + g * 128, (1, 128)).ap,
                                 rhs=CB(0, rows, B_WS + g * 128, (1, rows)).ap, start=True, stop=False)
                        ins = e.matmul(PS[k](0, 128, tt * rows, (1, rows)).ap, lhsT=CB(0, 1, B_ONE, (1, 128)).ap,
                                       rhs=CB(0, 1, B_BS + g * 128, (1, rows)).ap, start=False, stop=True)
                    return ins
                add("pe", mm, r=[RK(0, rows, O_V, (1, 2048)), CB(0, 128, B_WS, (1, 1152))], w=[PS[k](0, 128, 0, (1, NQ))])
                o = RK(0, 128, O_YA + g * TB, (1, NQ))
                u_ = RK(0, 128, O_UT + g * TB, (1, NQ))
                add("dve", lambda e, o=o, u_=u_, k=k: e.tensor_tensor(out=o.ap, in0=PS[k](0, 128, 0, (1, NQ)).ap, in1=u_.ap, op=ALU.mult),
                    r=[PS[k](0, 128, 0, (1, NQ)), u_], w=[o])
        rot_n[0] = 4
        XBb = view(XB, BF16)
        SETS = [(RI, RIb), (XB, XBb)]
        Q_SQ, Q_SQR, Q_QR, Q_T1, Q_T2, K_SQ, K_CKVF, K_KRF, K_KROF, K_T1, K_T2, X_ST, X_TAB, X_TABQ, X_TABK = \
            0, 512, 768, 1024, 1280, 1536, 2048, 2304, 2336, 2368, 2400, 2432, 2496, 2560, 2624
        B_CQN, B_CQT, B_QF, B_CKVB, B_CKVT, B_KF = 5376, 5760, 6144, 6912, 7168, 7424

        def tile_chains(tt, F_, Bv, qb, kb_):
            qalloc, kalloc = mk_alloc(qb), mk_alloc(kb_)
            tok0 = t0 + tt * rows
            pre_steps = []
            qs = []
            ks = []
            stq = {}

            def p0():
                dma(F_(0, rows, X_TAB, (1, 64)), rsrc.d(tok0 * 64, (64, rows), (1, 64)))
                for (dst, gC, gS) in ((X_TABQ, C_QRG, C_QRS), (X_TABK, C_KRG, C_KRS)):
                    for part, g_ in ((0, gC), (1, gS)):
                        o = F_(0, rows, dst + part * 32, (1, 32))
                        i = F_(0, rows, X_TAB + part * 32, (1, 32))
                        gg = CF(0, rows, g_, (1, 32))
                        add("pool", lambda e, o=o, i=i, gg=gg: e.tensor_tensor(out=o.ap, in0=i.ap, in1=gg.ap, op=ALU.mult),
                            r=[i, gg], w=[o])
            pre_steps.append(p0)

            def rope_chain(steps, x_off, nh, tab, t1o, t2o, out_fn):
                def r0():
                    t1 = F_(0, rows, t1o, (32, nh), (1, 32))
                    add("pool", lambda e: e.tensor_tensor(out=t1.ap, in0=F_(0, rows, x_off, (32, nh), (1, 32)).ap,
                                                          in1=F_(0, rows, tab, (0, nh), (1, 32)).ap, op=ALU.mult),
                        r=[F_(0, rows, x_off, (1, 32 * nh)), F_(0, rows, tab, (1, 32))], w=[t1])
                    for hf in range(2):
                        t2 = F_(0, rows, t2o + hf * 16, (32, nh), (1, 16))
                        xi = F_(0, rows, x_off + (1 - hf) * 16, (32, nh), (1, 16))
                        sn = F_(0, rows, tab + 32 + hf * 16, (0, nh), (1, 16))
                        add("pool", lambda e, t2=t2, xi=xi, sn=sn: e.tensor_tensor(out=t2.ap, in0=xi.ap, in1=sn.ap, op=ALU.mult),
                            r=[xi, F_(0, rows, tab + 32, (1, 32))], w=[t2])
                steps.append(r0)

                def r1():
                    oo = out_fn()
                    t1 = F_(0, rows, t1o, (32, nh), (1, 32))
                    t2a = F_(0, rows, t2o, (32, nh), (1, 32))
                    add("pool", lambda e: e.tensor_tensor(out=oo.ap, in0=t1.ap, in1=t2a.ap, op=ALU.add), r=[t1, t2a], w=[oo])
                steps.append(r1)

            def q0():
                k = qalloc()
                stq["cq"] = k

                def mmq(e):
                    ins = None
                    for kc in range(8):
                        ins = e.matmul(PS[k](0, rows, 0, (1, 384)).ap, lhsT=HT(0, 128, kc * TB + tt * rows, (1, rows)).ap,
                                       rhs=RING(0, 128, wq + kc * 384, (1, 384)).ap, start=(kc == 0), stop=(kc == 7))
                    return ins
                add("pe", mmq, r=[RING(0, 128, wq, (1, 3072)), HT(0, 128, 0, (1, 8 * TB))], w=[PS[k](0, rows, 0, (1, 384))])
            qs.append(q0)

            def q1():
                k = stq["cq"]
                cq = PS[k](0, rows, 0, (1, 384))
                junk = F_(0, rows, Q_SQ, (1, 384))
                ssq = F_(0, rows, X_ST, (1, 1))
                add("act", lambda e: e.activation(out=junk.ap, in_=cq.ap, func=AF.Square, scale=384 ** -0.5, accum_out=ssq.ap),
                    r=[cq], w=[junk, ssq])
            qs.append(q1)

            def q2():
                ssq = F_(0, rows, X_ST, (1, 1))
                add("act", lambda e: e.activation(out=ssq.ap, in_=ssq.ap, func=AF.Sqrt, bias=EPS_AP(rows).ap),
                    r=[ssq, EPS_AP(rows)], w=[ssq])
            qs.append(q2)

            def q3():
                ssq = F_(0, rows, X_ST, (1, 1))
                add("dve", lambda e: e.reciprocal(out=ssq.ap, in_=ssq.ap), r=[ssq], w=[ssq])
            qs.append(q3)

            def q4():
                k = stq["cq"]
                cq = PS[k](0, rows, 0, (1, 384))
                rq = F_(0, rows, X_ST, (1, 1))
                cqn = Bv(0, rows, B_CQN, (1, 384))
                add("dve", lambda e: e.tensor_scalar(out=cqn.ap, in0=cq.ap, scalar1=rq.ap, scalar2=None, op0=ALU.mult),
                    r=[cq, rq], w=[cqn])
            qs.append(q4)

            def q5():
                k2 = qalloc()
                stq["k2"] = k2

                def trq(e):
                    ins = None
                    for kc in range(3):
                        ins = e.transpose(out=PSb[k2](0, 128, kc * 128, (1, rows)).ap,
                                          in_=Bv(0, rows, B_CQN + kc * 128, (1, 128)).ap, identity=ident(rows).ap)
                    return ins
                add("pe", trq, r=[Bv(0, rows, B_CQN, (1, 384)), ident(rows)], w=[PSb[k2](0, 128, 0, (1, 384))])
            qs.append(q5)

            def q6():
                k2 = stq["k2"]
                cqT = Bv(0, 128, B_CQT, (128, 3), (1, rows))
                add("act", lambda e: e.activation(out=cqT.ap, in_=PSb[k2](0, 128, 0, (128, 3), (1, rows)).ap, func=AF.Copy),
                    r=[PSb[k2](0, 128, 0, (1, 384))], w=[cqT])
            qs.append(q6)

            def q7():
                ka, kb = qalloc(), qalloc()
                stq["ka"], stq["kb"] = ka, kb

                def mmuq(e):
                    ins = None
                    for kc in range(3):
                        e.matmul(PS[ka](0, rows, 0, (1, 512)).ap, lhsT=Bv(0, 128, B_CQT + kc * 128, (1, rows)).ap,
                                 rhs=RING(0, 128, wuq + kc * 768, (1, 512)).ap, start=(kc == 0), stop=(kc == 2))
                        ins = e.matmul(PS[kb](0, rows, 0, (1, 256)).ap, lhsT=Bv(0, 128, B_CQT + kc * 128, (1, rows)).ap,
                                       rhs=RING(0, 128, wuq + kc * 768 + 512, (1, 256)).ap, start=(kc == 0), stop=(kc == 2))
                    return ins
                add("pe", mmuq, r=[Bv(0, 128, B_CQT, (1, 384)), RING(0, 128, wuq, (1, 2304))],
                    w=[PS[ka](0, rows, 0, (1, 512)), PS[kb](0, rows, 0, (1, 256))])
            qs.append(q7)

            def q8():
                ka, kb = stq["ka"], stq["kb"]
                sq, sqr = F_(0, rows, Q_SQ, (1, 512)), F_(0, rows, Q_SQR, (1, 256))
                add("act", lambda e: e.activation(out=sq.ap, in_=PS[ka](0, rows, 0, (1, 512)).ap, func=AF.Square),
                    r=[PS[ka](0, rows, 0, (1, 512))], w=[sq])
                add("act", lambda e: e.activation(out=sqr.ap, in_=PS[kb](0, rows, 0, (1, 256)).ap, func=AF.Square),
                    r=[PS[kb](0, rows, 0, (1, 256))], w=[sqr])
            qs.append(q8)

            def q9():
                stn, str_ = F_(0, rows, X_ST + 2, (1, 8)), F_(0, rows, X_ST + 10, (1, 8))
                add("dve", lambda e: e.tensor_reduce(out=stn.ap, in_=F_(0, rows, Q_SQ, (64, 8), (1, 64)).ap, axis=AX.X, op=ALU.add),
                    r=[F_(0, rows, Q_SQ, (1, 512))], w=[stn])
                add("dve", lambda e: e.tensor_reduce(out=str_.ap, in_=F_(0, rows, Q_SQR, (32, 8), (1, 32)).ap, axis=AX.X, op=ALU.add),
                    r=[F_(0, rows, Q_SQR, (1, 256))], w=[str_])
            qs.append(q9)

            def q10():
                stn, str_ = F_(0, rows, X_ST + 2, (1, 8)), F_(0, rows, X_ST + 10, (1, 8))
                add("act", lambda e: e.activation(out=stn.ap, in_=stn.ap, func=AF.Sqrt, scale=1.0 / 64, bias=EPS_AP(rows).ap),
                    r=[stn, EPS_AP(rows)], w=[stn])
                add("act", lambda e: e.activation(out=str_.ap, in_=str_.ap, func=AF.Sqrt, scale=1.0 / 32, bias=EPS_AP(rows).ap),
                    r=[str_, EPS_AP(rows)], w=[str_])
            qs.append(q10)

            def q11():
                st16 = F_(0, rows, X_ST + 2, (1, 16))
                add("dve", lambda e: e.reciprocal(out=st16.ap, in_=st16.ap), r=[st16], w=[st16])
            qs.append(q11)

            def q12():
                ka, kb = stq["ka"], stq["kb"]
                qf_n = Bv(0, rows, B_QF, (96, 8), (1, 64))
                add("dve", lambda e: e.tensor_tensor(out=qf_n.ap, in0=PS[ka](0, rows, 0, (64, 8), (1, 64)).ap,
                                                     in1=F_(0, rows, X_ST + 2, (1, 8), (0, 64)).ap, op=ALU.mult),
                    r=[PS[ka](0, rows, 0, (1, 512)), F_(0, rows, X_ST + 2, (1, 8))], w=[qf_n])
                qr = F_(0, rows, Q_QR, (32, 8), (1, 32))
                add("dve", lambda e: e.tensor_tensor(out=qr.ap, in0=PS[kb](0, rows, 0, (32, 8), (1, 32)).ap,
                                                     in1=F_(0, rows, X_ST + 10, (1, 8), (0, 32)).ap, op=ALU.mult),
                    r=[PS[kb](0, rows, 0, (1, 256)), F_(0, rows, X_ST + 10, (1, 8))], w=[qr])
            qs.append(q12)
            rope_chain(qs, Q_QR, 8, X_TABQ, Q_T1, Q_T2, lambda: Bv(0, rows, B_QF + 64, (96, 8), (1, 32)))

            def q15():
                qfull = Bv(0, rows, B_QF, (1, 768))
                for hh in range(2):
                    k3 = qalloc()
                    stq["k3", hh] = k3

                    def trqf(e, hh=hh, k3=k3):
                        ins = None
                        for hl in range(4):
                            h = hh * 4 + hl
                            ins = e.transpose(out=PSb[k3](0, 96, hl * 128, (1, rows)).ap,
                                              in_=Bv(0, rows, B_QF + h * 96, (1, 96)).ap, identity=ident(rows).ap)
                        return ins
                    add("pe", trqf, r=[qfull, ident(rows)], w=[PSb[k3](0, 96, 0, (1, 512))])
            qs.append(q15)

            def q16():
                gcol = CF(0, 96, C_GQ96, (1, 1))
                for hh in range(2):
                    k3 = stq["k3", hh]
                    qdst = QTv(0, 96, (hh * 4) * TB + tt * rows, (TB, 4), (1, rows))
                    src = PSb[k3](0, 96, 0, (128, 4), (1, rows))
                    if hh == 0:
                        add("act", lambda e, qdst=qdst, src=src: e.activation(out=qdst.ap, in_=src.ap, func=AF.Identity, scale=gcol.ap, bias=CF(0, 96, C_GQ96 + 1, (1, 1)).ap),
                            r=[PSb[k3](0, 96, 0, (1, 512)), gcol], w=[qdst])
                    else:
                        add("dve", lambda e, qdst=qdst, src=src: e.tensor_scalar(out=qdst.ap, in0=src.ap, scalar1=gcol.ap, scalar2=None, op0=ALU.mult),
                            r=[PSb[k3](0, 96, 0, (1, 512)), gcol], w=[qdst])
            qs.append(q16)

            def k0():
                k = kalloc()
                stq["kv"] = k

                def mmkv(e):
                    ins = None
                    for kc in range(8):
                        ins = e.matmul(PS[k](0, rows, 0, (1, 288)).ap, lhsT=HT(0, 128, kc * TB + tt * rows, (1, rows)).ap,
                                       rhs=RING(0, 128, wkv + kc * 288, (1, 288)).ap, start=(kc == 0), stop=(kc == 7))
                    return ins
                add("pe", mmkv, r=[RING(0, 128, wkv, (1, 2304)), HT(0, 128, 0, (1, 8 * TB))], w=[PS[k](0, rows, 0, (1, 288))])
            ks.append(k0)

            def k1():
                k = stq["kv"]
                ckvp, krp = PS[k](0, rows, 0, (1, 256)), PS[k](0, rows, 256, (1, 32))
                j1, j2 = F_(0, rows, K_SQ, (1, 256)), F_(0, rows, K_SQ + 256, (1, 32))
                s1, s2 = F_(0, rows, X_ST + 24, (1, 1)), F_(0, rows, X_ST + 25, (1, 1))
                add("act", lambda e: e.activation(out=j1.ap, in_=ckvp.ap, func=AF.Square, scale=1.0 / 16, accum_out=s1.ap),
                    r=[ckvp], w=[j1, s1])
                add("act", lambda e: e.activation(out=j2.ap, in_=krp.ap, func=AF.Square, scale=32 ** -0.5, accum_out=s2.ap),
                    r=[krp], w=[j2, s2])
            ks.append(k1)

            def k2():
                s12 = F_(0, rows, X_ST + 24, (1, 2))
                add("act", lambda e: e.activation(out=s12.ap, in_=s12.ap, func=AF.Sqrt, bias=EPS_AP(rows).ap),
                    r=[s12, EPS_AP(rows)], w=[s12])
            ks.append(k2)

            def k3():
                s12 = F_(0, rows, X_ST + 24, (1, 2))
                add("dve", lambda e: e.reciprocal(out=s12.ap, in_=s12.ap), r=[s12], w=[s12])
            ks.append(k3)

            def k4():
                k = stq["kv"]
                ckvp, krp = PS[k](0, rows, 0, (1, 256)), PS[k](0, rows, 256, (1, 32))
                r1, r2 = F_(0, rows, X_ST + 24, (1, 1)), F_(0, rows, X_ST + 25, (1, 1))
                ckvf = F_(0, rows, K_CKVF, (1, 256))
                add("dve", lambda e: e.scalar_tensor_tensor(out=ckvf.ap, in0=ckvp.ap, scalar=r1.ap, in1=CF(0, rows, C_KVG, (1, 256)).ap,
                                                            op0=ALU.mult, op1=ALU.mult),
                    r=[ckvp, r1, CF(0, rows, C_KVG, (1, 256))], w=[ckvf])
                dma(ockv.d(tok0 * 256, (256, rows), (1, 256)), ckvf)
                krf = F_(0, rows, K_KRF, (1, 32))
                add("dve", lambda e: e.tensor_scalar(out=krf.ap, in0=krp.ap, scalar1=r2.ap, scalar2=None, op0=ALU.mult),
                    r=[krp, r2], w=[krf])
            ks.append(k4)
            rope_chain(ks, K_KRF, 1, X_TABK, K_T1, K_T2, lambda: F_(0, rows, K_KROF, (32, 1), (1, 32)))

            def k7():
                dma(okr.d(tok0 * 32, (32, rows), (1, 32)), F_(0, rows, K_KROF, (1, 32)))
            ks.append(k7)
            ts = dict(f=F_, b=Bv, sq=K_SQ, st=X_ST + 32, ckvb=B_CKVB, ckvT=B_CKVT, kfull=B_KF)
            kvs = kv_steps(F_(0, rows, K_CKVF, (1, 256)), F_(0, rows, K_KROF, (1, 32)), kt0 + tt, rows, wkv + 2304, ts, alloc=kalloc)
            ks.extend(kvs)
            return pre_steps, qs, ks

        PAIR = 2
        for tp in range(0, NT, PAIR):
            chains = []
            for j_, tt in enumerate(range(tp, min(tp + PAIR, NT))):
                F_, Bv = SETS[j_]
                pre_steps, qs, ks = tile_chains(tt, F_, Bv, [4 * j_, 4 * j_ + 1], [4 * j_ + 2, 4 * j_ + 3])
                for p_ in pre_steps:
                    p_()
                chains.append(qs)
                chains.append(ks)
            if tp == 0:
                cuts = [5 if (ci % 2 == 0) else 9 for ci in range(len(chains))]
                run_chains([c[:cut] for c, cut in zip(chains, cuts)])
                fill_alloc = mk_alloc([1, 3, 5, 7])
                emit_IN_u(fill_alloc)
                wuq = ring_load([ld_full(s_wuq, 2304)])
                emit_IN_v(fill_alloc)
                emit_G(fill_alloc)
                run_chains([c[cut:] for c, cut in zip(chains, cuts)])
            else:
                run_chains(chains)
        rot_n[0] = 4
        if NT > 1 and PAIR > 1:
            dma(XB(0, rows, 0, (D, NT), (1, D)), xsrc.d(t0 * D, (D, rows), (rows * D, NT), (1, D)))

        G = 512 // NQ
        if is_s:
            ktl = [(kt, 128 if kt < 16 else 64, 0) for kt in range(17)]
        else:
            ktl = [(kt, 128, 0 if kt < kt0 else (kt - kt0) * 128) for kt in range(kt0 + 4)]
        groups = [ktl[i:i + G] for i in range(0, len(ktl), G)]
        if is_s:
            groups = [ktl[0:8], ktl[8:16], ktl[16:17]]
        ybz = RK(64, 64, O_YB, (1, 8 * TB))
        add("pool", lambda e: e.memset(ybz.ap, 0.0), w=[ybz])
        units = [(h, gi) for h in range(8) for gi in range(len(groups))]
        NU = len(units)
        LOOK = 2
        ubank = {}
        deferred = []

        def emit_S(u):
            h, gi = units[u]
            grp = groups[gi]
            k = rot()
            ubank[u] = k
            rk = grp[0][1]
            c0 = grp[0][2]

            def mms(e, grp=grp, k=k, h=h):
                ins = None
                for s_, (kt, rk_, c0_) in enumerate(grp):
                    ins = e.matmul(PS[k](0, rk_, s_ * NQ + c0_, (1, NQ - c0_)).ap,
                                   lhsT=KT(0, 96, h * SEQ + kt * 128, (1, rk_)).ap,
                                   rhs=QTv(0, 96, h * TB + c0_, (1, NQ - c0_)).ap, start=True, stop=True)
                return ins
            kt_lo, kt_hi = grp[0][0], grp[-1][0]
            add("pe", mms, r=[KT(0, 96, h * SEQ + kt_lo * 128, (1, (kt_hi - kt_lo + 1) * 128)), QTv(0, 96, h * TB, (1, NQ))],
                w=[PS[k](0, 128, 0, (1, 512))])
            pb = A_PT[u % 4]
            ncol = len(grp) * NQ - c0
            pt = RIb(0, rk, pb + c0, (1, ncol))
            add("act", lambda e, pt=pt, k=k, rk=rk, c0=c0, ncol=ncol: e.activation(
                out=pt.ap, in_=PS[k](0, rk, c0, (1, ncol)).ap, func=AF.Exp, scale=SCALE),
                r=[PS[k](0, 128, 0, (1, 512))], w=[pt])
            if (not is_s) and grp[0][0] >= kt0:
                mz = RIb(64, 64, pb + c0, (1, 64))
                add("pool", lambda e, mz=mz: e.memset(mz.ap, 0.0), w=[mz])

        def emit_PV(u):
            h, gi = units[u]
            grp = groups[gi]
            acc = 4 + (h % 2)
            pb = A_PT[u % 4]
            first = (gi == 0)
            last = (gi == len(groups) - 1)
            kt_lo, kt_hi = grp[0][0], grp[-1][0]

            def mmv(e, grp=grp, h=h, pb=pb, acc=acc, first=first, last=last):
                ins = None
                for s_, (kt, rk_, c0_) in enumerate(grp):
                    ins = e.matmul(PS[acc](0, 65, c0_, (1, NQ - c0_)).ap,
                                   lhsT=VS(0, rk_, kt * 520 + h * 65, (1, 65)).ap,
                                   rhs=RIb(0, rk_, pb + s_ * NQ + c0_, (1, NQ - c0_)).ap,
                                   start=(first and s_ == 0), stop=(last and s_ == len(grp) - 1))
                return ins
            add("pe", mmv, r=[VS(0, 128, kt_lo * 520, (1, (kt_hi - kt_lo + 1) * 520)), RIb(0, 128, pb, (1, 512))],
                w=[PS[acc](0, 65, 0, (1, NQ))])
            if last:
                ob = A_OSB + (h % 2) * 1024
                osb = RI(0, 65, ob, (1, NQ))
                add("act", lambda e, osb=osb, acc=acc: e.activation(out=osb.ap, in_=PS[acc](0, 65, 0, (1, NQ)).ap, func=AF.Copy),
                    r=[PS[acc](0, 65, 0, (1, NQ))], w=[osb])
                rc = RI(64, 1, ob + 512, (1, NQ))
                add("dve", lambda e, rc=rc, ob=ob: e.reciprocal(out=rc.ap, in_=RI(64, 1, ob, (1, NQ)).ap), r=[osb], w=[rc])

                def fin(h=h, ob=ob, osb=osb, rc=rc):
                    add("pe", lambda e: e.matmul(PS[6](0, 64, 0, (1, NQ)).ap, lhsT=CF(64, 1, C_ONE, (1, 64)).ap, rhs=rc.ap, start=True, stop=True),
                        r=[rc, CF(64, 1, C_ONE, (1, 64))], w=[PS[6](0, 64, 0, (1, NQ))])
                    yb = RK(0, 64, O_YB + h * TB, (1, NQ))
                    add("dve", lambda e: e.tensor_tensor(out=yb.ap, in0=RI(0, 64, ob, (1, NQ)).ap, in1=PS[6](0, 64, 0, (1, NQ)).ap, op=ALU.mult),
                        r=[osb, PS[6](0, 64, 0, (1, NQ))], w=[yb])
                deferred.append((u + min(6, 2 * len(groups) - 2), fin))

        for u in range(min(LOOK, NU)):
            emit_S(u)
        for u in range(NU):
            if u + LOOK < NU:
                emit_S(u + LOOK)
            emit_PV(u)
            while deferred and deferred[0][0] <= u:
                deferred.pop(0)[1]()
        while deferred:
            deferred.pop(0)[1]()

        woa = ring_load([ld_full(s_wout_a[b], 4096)])
        wob0 = ring_load([ld_full(s_wout_b[b], 4096, np_=64, src_off=0),
                          lambda base: (RING(64, 64, base, (1, 4096)), s_wout_b[b].d(0, (8192, 64), (1, 4096)))])
        wob1 = ring_load([ld_full(s_wout_b[b], 4096, np_=64, src_off=4096),
                          lambda base: (RING(64, 64, base, (1, 4096)), s_wout_b[b].d(4096, (8192, 64), (1, 4096)))])
        for tt in range(NT):
            for half in range(2):
                k = rot()

                def mmo(e, tt=tt, half=half, k=k):
                    ins = None
                    for g in range(4):
                        ins = e.matmul(PS[k](0, rows, 0, (1, 512)).ap, lhsT=RK(0, 128, O_YA + g * TB + tt * rows, (1, rows)).ap,
                                       rhs=RING(0, 128, woa + g * 1024 + half * 512, (1, 512)).ap, start=(g == 0), stop=False)
                    for h in range(8):
                        wb = wob0 if h < 4 else wob1
                        ins = e.matmul(PS[k](0, rows, 0, (1, 512)).ap, lhsT=RK(0, 128, O_YB + h * TB + tt * rows, (1, rows)).ap,
                                       rhs=RING(0, 128, wb + (h % 4) * 1024 + half * 512, (1, 512)).ap, start=False, stop=(h == 7))
                    return ins
                add("pe", mmo, r=[RK(0, 128, O_YA, (1, 6144)), RING(0, 128, woa, (1, 4096)), RING(0, 128, wob0, (1, 4096)), RING(0, 128, wob1, (1, 4096))],
                    w=[PS[k](0, rows, 0, (1, 512))])
                xr = XB(0, rows, tt * D + half * 512, (1, 512))
                add("dve", lambda e, xr=xr, k=k: e.tensor_tensor(out=xr.ap, in0=PS[k](0, rows, 0, (1, 512)).ap, in1=xr.ap, op=ALU.add),
                    r=[PS[k](0, rows, 0, (1, 512)), xr], w=[xr])
            norm_stats(tt, rows)
        norm_T(1, b, rows, NT, NQ, skip_stats=True)
        passes = [(0, 1)] if NT == 1 else [(0, 2), (2, 4)]
        def do_prefetch():
            PRE["wu"] = ring_load([ld_full(s_win_u, 4096)])
            PRE["wv"] = ring_load([ld_full(s_win_v, 4096)])
            PRE["wq"] = ring_load([ld_full(s_win_q, 3072)])
            PRE["wkv"] = ring_load([ld_full(s_win_kv, 2304), lambda base: (RING(0, 128, base + 2304, (1, 2048)), s_wukv.d(0, (2048, 128), (1, 2048)))])
        pending_out = [None]
        nxt_tok0 = 0 if is_s else (blk + 1) * TB
        do_early = is_s or blk + 1 < SEQ // TB
        for pi, (ta, tb_) in enumerate(passes):
            for j in range(NJ):
                if pi == 0 and do_early:
                    if j % 5 == 0 and j // 5 < 4:
                        early_load(nxt_tok0 + (j // 5) * 128)
                    if j % 5 == 2 and j // 5 < 4:
                        early_stats(j // 5)
                accb = 4 if pi == 0 else 0
                if pi == 0:
                    pieces = [lambda base, j=j: (RING(0, 128, base, (1, 2048)), s_ffn_in.d(j * 128 * 2048, (2048, 128), (1, 2048))),
                              lambda base, j=j: (RING(0, 128, base + 2048, (1, 1024)), s_ffn_out[b].d(j * 128 * 1024, (1024, 128), (1, 1024)))]
                    wf = ring_load(pieces)
                    WT, wo = RING, wf + 2048
                else:
                    if j == 6 and do_early:
                        do_prefetch()
                    wf = None
                    mslot = 2048 + ((j // 2) % 3) * 2048
                    if j % 2 == 0:
                        dma(RIb(0, 128, mslot, (1024, 2), (1, 1024)),
                            s_ffn_out[b].d(j * 128 * 1024, (1024, 128), (128 * 1024, 2), (1, 1024)))
                    WT, wo = RIb, mslot + (j % 2) * 1024
                at = RK(0, 128, j * TB, (1, NQ))
                if pi == 0:
                    kg, ku = rot(), rot()

                    def mmf(e, wf=wf, kg=kg, ku=ku):
                        ins = None
                        for kc in range(8):
                            e.matmul(PS[kg](0, 128, 0, (1, NQ)).ap, lhsT=RING(0, 128, wf + kc * 256, (1, 128)).ap,
                                     rhs=HT(0, 128, kc * TB, (1, NQ)).ap, start=(kc == 0), stop=(kc == 7))
                        for kc in range(8):
                            ins = e.matmul(PS[ku](0, 128, 0, (1, NQ)).ap, lhsT=RING(0, 128, wf + kc * 256 + 128, (1, 128)).ap,
                                           rhs=HT(0, 128, kc * TB, (1, NQ)).ap, start=(kc == 0), stop=(kc == 7))
                        return ins
                    add("pe", mmf, r=[RING(0, 128, wf, (1, 2048)), HT(0, 128, 0, (1, 8 * TB))],
                        w=[PS[kg](0, 128, 0, (1, NQ)), PS[ku](0, 128, 0, (1, NQ))])
                    sg = RI(0, 128, (j % 2) * 512, (1, NQ))
                    add("act", lambda e, sg=sg, kg=kg: e.activation(out=sg.ap, in_=PS[kg](0, 128, 0, (1, NQ)).ap, func=AF.Silu),
                        r=[PS[kg](0, 128, 0, (1, NQ))], w=[sg])
                    add("dve", lambda e, at=at, sg=sg, ku=ku: e.tensor_tensor(out=at.ap, in0=PS[ku](0, 128, 0, (1, NQ)).ap, in1=sg.ap, op=ALU.mult),
                        r=[PS[ku](0, 128, 0, (1, NQ)), sg], w=[at])

                def mmfo(e, wf=wf, j=j, ta=ta, tb_=tb_, accb=accb, WT=WT, wo=wo):
                    ins = None
                    for tt in range(ta, tb_):
                        for half in range(2):
                            kk = accb + (tt - ta) * 2 + half
                            ins = e.matmul(PS[kk](0, rows, 0, (1, 512)).ap, lhsT=RK(0, 128, j * TB + tt * rows, (1, rows)).ap,
                                           rhs=WT(0, 128, wo + half * 512, (1, 512)).ap, start=(j == 0), stop=(j == NJ - 1))
                    return ins

                def emit_out(mmfo=mmfo, at=at, wf=wf, ta=ta, tb_=tb_, accb=accb, WT=WT, wo=wo):
                    add("pe", mmfo, r=[at, WT(0, 128, wo, (1, 1024))],
                        w=[PS[accb + (tt - ta) * 2 + half](0, rows, 0, (1, 512)) for tt in range(ta, tb_) for half in range(2)])
                if pi == 0:
                    if pending_out[0] is not None:
                        pending_out[0]()
                    pending_out[0] = emit_out
                else:
                    emit_out()
            if pending_out[0] is not None:
                pending_out[0]()
                pending_out[0] = None
            if pi == 0 and do_early:
                norm_T(0, 0, 128, 4, TB, skip_stats=True, early=True)
                early_state["done"] = True
            if pi == len(passes) - 1 and do_early and len(passes) == 1:
                do_prefetch()
            for tt in range(ta, tb_):
                for half in range(2):
                    kk = accb + (tt - ta) * 2 + half
                    xr = XB(0, rows, tt * D + half * 512, (1, 512))
                    add("dve", lambda e, xr=xr, kk=kk: e.tensor_tensor(out=xr.ap, in0=PS[kk](0, rows, 0, (1, 512)).ap, in1=xr.ap, op=ALU.add),
                        r=[PS[kk](0, rows, 0, (1, 512)), xr], w=[xr])
                dma(ydst.d((t0 + tt * rows) * D, (D, rows), (1, D)), XB(0, rows, tt * D, (1, D)))

    QTv = RK

    process_block(True, 0)
    for blk in range(SEQ // TB):
        process_block(False, blk)

    print("sbuf bytes remaining", nc.sbuf_bytes_remaining, "ops", len(S.ops), {k: len(v) for k, v in S.eng_ops.items()})
    return nc, S, None


def _finish(nc, S):
    from contextlib import ExitStack
    with ExitStack() as st:
        esem = {n: st.enter_context(nc.semaphore(f"sem_{n}")) for n in ("pe", "act", "dve", "pool")}
        dsems = [st.enter_context(nc.semaphore(f"dsem{k}")) for k in range(NDSEM)]
        block = st.enter_context(nc.Block())
        S.emit(nc, block, esem, dsems)
    return nc


_CACHE = {}


def _rope_table(pos):
    half = 16
    inv = (np.float32(10000.0) ** (-np.arange(half, dtype=np.float32) / np.float32(half))).astype(np.float32)
    ang = pos.astype(np.float32)[:, None] * inv[None, :]
    cos, sin = np.cos(ang).astype(np.float32), np.sin(ang).astype(np.float32)
    return np.ascontiguousarray(np.concatenate([cos, cos, -sin, sin], axis=1).astype(np.float32))


def kernel(x_prompt, x_sample, cache_ckv, cache_krope, c_prompt, c_sample, w_ada, b_ada, norm1_g, w_in, w_s, b_s,
           q_norm_g, w_uq, kv_norm_g, w_ukv, qn_g, qr_g, kn_g, kr_g, w_out, norm2_g, w_ffn_in, w_ffn_out):
    f = lambda a: np.ascontiguousarray(np.asarray(a, dtype=np.float32))
    if "nc" not in _CACHE:
        nc, S, _ = build_program()
        _finish(nc, S)
        _CACHE["nc"] = nc
    nc = _CACHE["nc"]
    shared = {
        "w_ada": f(w_ada)[0], "b_ada": f(b_ada), "norm1_g": f(norm1_g), "norm2_g": f(norm2_g), "w_in": f(w_in)[0],
        "w_s": f(w_s)[0].reshape(512, 128), "b_s": f(b_s).reshape(1, 512), "q_norm_g": f(q_norm_g), "w_uq": f(w_uq)[0],
        "kv_norm_g": f(kv_norm_g), "w_ukv": f(w_ukv)[0], "qn_g": f(qn_g), "qr_g": f(qr_g), "kn_g": f(kn_g), "kr_g": f(kr_g),
        "w_out": f(w_out)[0], "w_ffn_in": f(w_ffn_in)[0], "w_ffn_out": f(w_ffn_out)[0],
        "rope_p": _rope_table(np.arange(SEQ)), "rope_s": _rope_table(PAST + np.arange(DEC)),
    }
    xp, xs, cc, ck = f(x_prompt), f(x_sample), f(cache_ckv), f(cache_krope)
    cp, cs = f(c_prompt), f(c_sample)
    in_maps = []
    for b in range(8):
        m = dict(shared)
        m["x_p"] = xp[b]
        m["x_s"] = xs[b]
        m["cckv"] = cc[0, b]
        m["ckr"] = ck[0, b]
        m["c2"] = np.ascontiguousarray(np.stack([cp[b], cs[b]], axis=0))
        in_maps.append(m)
    res = run_bass_kernel_spmd(nc, in_maps, core_ids=list(range(8)))
    rr = res.results
    yp = np.stack([rr[b]["y_p"] for b in range(8)], axis=0)
    ys = np.stack([rr[b]["y_s"] for b in range(8)], axis=0)
    ckv_p = np.stack([rr[b]["o_ckv_p"] for b in range(8)], axis=0)[None]
    kr_p = np.stack([rr[b]["o_kr_p"] for b in range(8)], axis=0)[None]
    ckv_s = np.stack([rr[b]["o_ckv_s"] for b in range(8)], axis=0)[None]
    kr_s = np.stack([rr[b]["o_kr_s"] for b in range(8)], axis=0)[None]
    v_s = np.stack([rr[b]["o_v_s"] for b in range(8)], axis=0)[None]
    return (yp.astype(np.float32), ys.astype(np.float32), ckv_p.astype(np.float32), kr_p.astype(np.float32),
            ckv_s.astype(np.float32), kr_s.astype(np.float32), v_s.astype(np.float32))
```

```python
import numpy as np
import concourse.bass as bass
import concourse.mybir as mybir
from concourse.bass_utils import run_bass_kernel_spmd

F32 = mybir.dt.float32
BF16 = mybir.dt.bfloat16
AF = mybir.ActivationFunctionType
ALU = mybir.AluOpType
AX = mybir.AxisListType

D = 1024
SEQ = 4096
DEC = 64
PAST = 2048
TB = 512
DFF = 2816
NJ = 22
EPS = 1e-6
SCALE = 96 ** -0.5
SLOT = 4352
NRING = 4
NDSEM = 8


def _dsize(dt):
    return 4 if dt == F32 else 2


class R:
    __slots__ = ("ap", "root", "lo", "hi")

    def __init__(self, ap, root, lo, hi):
        self.ap, self.root, self.lo, self.hi = ap, root, lo, hi


class T:
    def __init__(self, h, root, base_bytes, dtype, F, dram=False, whole=False):
        self.h, self.root, self.base, self.dt, self.F, self.dram, self.whole = h, root, base_bytes, dtype, F, dram, whole

    def __call__(self, p0, n_p, off, *dims):
        if not dims:
            raise ValueError("need dims")
        ap = bass.AP(self.h, p0 * self.F + off, [[self.F, n_p]] + [[s, c] for (s, c) in dims])
        ext = sum(s * (c - 1) for (s, c) in dims) + 1
        sz = _dsize(self.dt)
        if self.whole:
            return R(ap, self.root, 0, 1 << 30)
        return R(ap, self.root, self.base + off * sz, self.base + (off + ext) * sz)

    def d(self, off, *dims):
        ap = bass.AP(self.h, off, [[s, c] for (s, c) in dims])
        ext = sum(s * (c - 1) for (s, c) in dims) + 1
        sz = _dsize(self.dt)
        return R(ap, self.root, off * sz, (off + ext) * sz)


class Op:
    __slots__ = ("eng", "fn", "dma", "deps", "id", "seq", "dsem", "dval", "dprev")


class Sched:
    ENG = ("pe", "act", "dve", "pool", "sp")

    def __init__(self):
        self.ops = []
        self.eng_ops = {e: [] for e in self.ENG}
        self.segs = {}
        self.pending = {e: {} for e in self.ENG}
        self.dma_all = []
        self.ndma = 0

    def _touch(self, op, acc, is_write):
        segs = self.segs.setdefault(acc.root, [])
        lo, hi = acc.lo, acc.hi
        out = []
        covered = []
        for s in segs:
            if s[1] <= lo or s[0] >= hi:
                out.append(s)
                continue
            if s[0] < lo:
                out.append([s[0], lo, s[2], list(s[3])])
            if s[1] > hi:
                out.append([hi, s[1], s[2], list(s[3])])
            mid = [max(s[0], lo), min(s[1], hi), s[2], list(s[3])]
            covered.append(mid)
        for m in covered:
            if m[2] is not None and m[2] != op.id:
                if is_write:
                    op.deps.setdefault(m[2], False)
                else:
                    op.deps[m[2]] = True
            if is_write:
                for rd in m[3]:
                    if rd != op.id:
                        op.deps.setdefault(rd, False)
        if is_write:
            out.append([lo, hi, op.id, []])
        else:
            covered.sort()
            cur = lo
            for m in covered:
                if m[0] > cur:
                    out.append([cur, m[0], None, [op.id]])
                m[3].append(op.id)
                out.append(m)
                cur = m[1]
            if cur < hi:
                out.append([cur, hi, None, [op.id]])
        out.sort(key=lambda s: s[0])
        self.segs[acc.root] = out

    def add(self, eng, fn, r=(), w=(), dma=False):
        op = Op()
        op.eng, op.fn, op.dma, op.deps, op.id = eng, fn, dma, {}, len(self.ops)
        op.seq = op.dsem = op.dval = op.dprev = None
        for k, v in self.pending[eng].items():
            op.deps[k] = v
        self.pending[eng] = {}
        for a in r:
            self._touch(op, a, False)
        for a in w:
            self._touch(op, a, True)
        self.ops.append(op)
        self.eng_ops[eng].append(op)
        if dma:
            op.dsem = self.ndma % NDSEM
            op.dval = 16 * (self.ndma // NDSEM + 1)
            self.ndma += 1
            self.dma_all.append(op.id)
        else:
            op.seq = len([1 for o in self.eng_ops[eng]])
        return op

    def barrier(self):
        deps = {}
        for e in self.ENG:
            if e != "sp" and self.eng_ops[e]:
                deps[self.eng_ops[e][-1].id] = True
        for i in self.dma_all:
            deps[i] = True
        self.dma_all = []
        for e in self.ENG:
            self.pending[e].update(deps)

    def emit(self, nc, block, esem, dsems):
        ops = self.ops

        def run(name, e):
            known = {}
            for op in self.eng_ops[name]:
                for did in sorted(op.deps):
                    dop = ops[did]
                    raw = op.deps[did]
                    if dop.dma:
                        sem, val = dsems[dop.dsem], dop.dval
                    else:
                        if dop.eng == name and not op.dma:
                            if name == "pe":
                                continue
                        sem, val = esem[dop.eng], dop.seq
                    if known.get(sem.num, 0) >= val:
                        continue
                    e.wait_ge(sem, val)
                    known[sem.num] = val
                if op.dma:
                    sem = dsems[op.dsem]
                    if op.dval > 16 and known.get(sem.num, 0) < op.dval - 16:
                        e.wait_ge(sem, op.dval - 16)
                        known[sem.num] = op.dval - 16
                    ins = op.fn(e)
                    ins.then_inc(sem, 16)
                else:
                    ins = op.fn(e)
                    ins.then_inc(esem[name], 1)

        @block.tensor
        def _(e):
            run("pe", e)

        @block.scalar
        def _(e):
            run("act", e)

        @block.vector
        def _(e):
            run("dve", e)

        @block.gpsimd
        def _(e):
            run("pool", e)

        @block.sync
        def _(e):
            run("sp", e)
            for k in range(min(NDSEM, self.ndma)):
                n_uses = (self.ndma - 1 - k) // NDSEM + 1
                e.wait_ge(dsems[k], 16 * n_uses)


def build_program():
    nc = bass.Bass("TRN2", target_bir_lowering=False)
    S = Sched()

    def dram(name, shape, kind, dt=F32):
        h = nc.dram_tensor(name, list(shape), dt, kind=kind)
        return T(h, name, 0, dt, shape[-1], dram=True)

    x_p = dram("x_p", [SEQ, D], "ExternalInput")
    x_s = dram("x_s", [DEC, D], "ExternalInput")
    cckv = dram("cckv", [PAST, 256], "ExternalInput")
    ckr = dram("ckr", [PAST, 32], "ExternalInput")
    c2 = dram("c2", [2, D], "ExternalInput")
    w_ada = dram("w_ada", [D, 6 * D], "ExternalInput")
    b_ada = dram("b_ada", [1, 6 * D], "ExternalInput")
    n1g = dram("norm1_g", [1, D], "ExternalInput")
    n2g = dram("norm2_g", [1, D], "ExternalInput")
    w_in = dram("w_in", [D, 1696], "ExternalInput")
    w_s = dram("w_s", [512, 128], "ExternalInput")
    b_s = dram("b_s", [1, 512], "ExternalInput")
    qng = dram("q_norm_g", [1, 384], "ExternalInput")
    w_uq = dram("w_uq", [384, 768], "ExternalInput")
    kvg = dram("kv_norm_g", [1, 256], "ExternalInput")
    w_ukv = dram("w_ukv", [256, 1024], "ExternalInput")
    qn_g = dram("qn_g", [1, 64], "ExternalInput")
    qr_g = dram("qr_g", [1, 32], "ExternalInput")
    kn_g = dram("kn_g", [1, 64], "ExternalInput")
    kr_g = dram("kr_g", [1, 32], "ExternalInput")
    w_out = dram("w_out", [D, D], "ExternalInput")
    w_fi = dram("w_ffn_in", [D, 2 * DFF], "ExternalInput")
    w_fo = dram("w_ffn_out", [DFF, D], "ExternalInput")
    rope_p = dram("rope_p", [SEQ, 64], "ExternalInput")
    rope_s = dram("rope_s", [DEC, 64], "ExternalInput")

    y_p = dram("y_p", [SEQ, D], "ExternalOutput")
    y_s = dram("y_s", [DEC, D], "ExternalOutput")
    o_ckv_p = dram("o_ckv_p", [SEQ, 256], "ExternalOutput")
    o_kr_p = dram("o_kr_p", [SEQ, 32], "ExternalOutput")
    o_ckv_s = dram("o_ckv_s", [DEC, 256], "ExternalOutput")
    o_kr_s = dram("o_kr_s", [DEC, 32], "ExternalOutput")
    o_v_s = dram("o_v_s", [DEC, 512], "ExternalOutput")

    s_win_u = dram("s_win_u", [128, 4096], "Internal", BF16)
    s_win_v = dram("s_win_v", [128, 4096], "Internal", BF16)
    s_win_q = dram("s_win_q", [128, 3072], "Internal", BF16)
    s_win_kv = dram("s_win_kv", [128, 2304], "Internal", BF16)
    s_wuq = dram("s_wuq", [128, 2304], "Internal", BF16)
    s_wukv = dram("s_wukv", [128, 2048], "Internal", BF16)
    s_wout_a = [dram(f"s_wout_a{v}", [128, 4096], "Internal", BF16) for v in range(2)]
    s_wout_b = [dram(f"s_wout_b{v}", [64, 8192], "Internal", BF16) for v in range(2)]
    s_ffn_in = dram("s_ffn_in", [NJ * 128, 2048], "Internal", BF16)
    s_ffn_out = [dram(f"s_ffn_out{v}", [NJ * 128, 1024], "Internal", BF16) for v in range(2)]

    def sb(name, F, dt):
        h = nc.alloc_sbuf_tensor(name, [128, F], dt)
        return T(h, name, 0, dt, F)

    def view(t, dt):
        h = t.h.bitcast(dt)
        F = t.F * _dsize(t.dt) // _dsize(dt)
        return T(h, t.root, t.base, dt, F)

    KT = sb("KT", 8 * SEQ, BF16)
    VS = sb("VS", 32 * 8 * 65, BF16)
    XB = sb("XB", 4 * D, F32)
    HT = sb("HT", 8 * TB, BF16)
    RK = sb("RK", NJ * TB, BF16)
    RI = sb("RI", 4352, F32)
    RIb = view(RI, BF16)
    RING = sb("RING", NRING * SLOT, BF16)
    CF = sb("CF", 1152, F32)
    CB = sb("CB", 1536, BF16)
    IDF = sb("IDF", 128, F32)
    KTf = view(KT, F32)
    RINGf = view(RING, F32)
    VSf = view(VS, F32)

    PS = []
    PSb = []
    for k in range(8):
        h = nc.alloc_psum_tensor(f"ps{k}", [128, 512], F32)
        PS.append(T(h, f"ps{k}", 0, F32, 512, whole=True))
        PSb.append(T(h.bitcast(BF16), f"ps{k}", 0, BF16, 1024, whole=True))

    rot_state = [0]
    rot_n = [4]

    def rot():
        k = rot_state[0] % rot_n[0]
        rot_state[0] = (k + 1) % rot_n[0]
        return k

    C_A = {(0, 0): 0, (0, 1): 8, (1, 0): 16, (1, 1): 24}
    C_MOD, C_N1G, C_N2G, C_QNG = 32, 96, 104, 112
    C_GQ, C_QRG, C_KRG, C_KVG, C_KNG, C_SC, C_SEL, C_ONE = 128, 192, 224, 256, 512, 576, 640, 896
    C_QRS, C_KRS, C_GQ96 = 1024, 1056, 1088
    B_ID, B_WS, B_BS, B_ONE = 0, 128, 640, 1152

    add = S.add

    def dma(out, in_, slow=False):
        if slow:
            add("sp", lambda e: e.dma_start(out=out.ap, in_=in_.ap, allow_slow_non_contiguous=True), r=[in_], w=[out], dma=True)
        else:
            add("sp", lambda e: e.dma_start(out=out.ap, in_=in_.ap), r=[in_], w=[out], dma=True)

    add("pool", lambda e: e.memset(IDF(0, 128, 0, (1, 128)).ap, 0.0), w=[IDF(0, 128, 0, (1, 128))])
    add("pool", lambda e: e.affine_select(out=IDF(0, 128, 0, (1, 128)).ap, in_=IDF(0, 128, 0, (1, 128)).ap,
                                          pattern=[[-1, 128]], compare_op=ALU.not_equal, fill=1.0, base=0,
                                          channel_multiplier=1),
        r=[IDF(0, 128, 0, (1, 128))], w=[IDF(0, 128, 0, (1, 128))])
    add("pool", lambda e: e.tensor_copy(out=CB(0, 128, B_ID, (1, 128)).ap, in_=IDF(0, 128, 0, (1, 128)).ap),
        r=[IDF(0, 128, 0, (1, 128))], w=[CB(0, 128, B_ID, (1, 128))])
    add("pool", lambda e: e.memset(CB(0, 1, B_ONE, (1, 128)).ap, 1.0), w=[CB(0, 1, B_ONE, (1, 128))])
    add("pool", lambda e: e.memset(CF(0, 128, C_ONE, (1, 64)).ap, 1.0), w=[CF(0, 128, C_ONE, (1, 64))])
    add("pool", lambda e: e.memset(CF(0, 2, C_SEL, (1, 256)).ap, 0.0), w=[CF(0, 2, C_SEL, (1, 256))])
    add("pool", lambda e: e.affine_select(out=CF(0, 2, C_SEL, (1, 128)).ap, in_=CF(0, 2, C_SEL, (1, 128)).ap,
                                          pattern=[[0, 128]], compare_op=ALU.not_equal, fill=1.0, base=0,
                                          channel_multiplier=1),
        r=[CF(0, 2, C_SEL, (1, 128))], w=[CF(0, 2, C_SEL, (1, 128))])
    add("pool", lambda e: e.affine_select(out=CF(0, 2, C_SEL + 128, (1, 128)).ap, in_=CF(0, 2, C_SEL + 128, (1, 128)).ap,
                                          pattern=[[0, 128]], compare_op=ALU.not_equal, fill=1.0, base=-1,
                                          channel_multiplier=1),
        r=[CF(0, 2, C_SEL + 128, (1, 128))], w=[CF(0, 2, C_SEL + 128, (1, 128))])

    ident = lambda n: CB(0, n, B_ID, (1, n))

    dma(CF(0, 128, C_N1G, (1, 8)), n1g.d(0, (1, 128), (128, 8)), slow=True)
    dma(CF(0, 128, C_N2G, (1, 8)), n2g.d(0, (1, 128), (128, 8)), slow=True)
    dma(CF(0, 128, C_QNG, (1, 3)), qng.d(0, (1, 128), (128, 3)), slow=True)
    dma(CF(0, 128, C_GQ, (1, 64)), qn_g.d(0, (0, 128), (1, 64)))
    dma(CF(0, 128, C_KNG, (1, 64)), kn_g.d(0, (0, 128), (1, 64)))
    dma(CF(0, 128, C_QRG, (1, 32)), qr_g.d(0, (0, 128), (1, 32)))
    dma(CF(0, 128, C_KRG, (1, 32)), kr_g.d(0, (0, 128), (1, 32)))
    dma(CF(0, 128, C_KVG, (1, 256)), kvg.d(0, (0, 128), (1, 256)))
    for hf_ in range(2):
        dma(CF(0, 128, C_QRS + hf_ * 16, (1, 16)), qr_g.d((1 - hf_) * 16, (0, 128), (1, 16)))
        dma(CF(0, 128, C_KRS + hf_ * 16, (1, 16)), kr_g.d((1 - hf_) * 16, (0, 128), (1, 16)))
    add("dve", lambda e: e.tensor_tensor(out=CF(0, 128, C_GQ, (1, 64)).ap, in0=CF(0, 128, C_GQ, (1, 64)).ap,
                                         in1=CF(0, 128, C_KNG, (1, 64)).ap, op=ALU.mult),
        r=[CF(0, 128, C_GQ, (1, 64)), CF(0, 128, C_KNG, (1, 64))], w=[CF(0, 128, C_GQ, (1, 64))])
    add("pool", lambda e: e.memset(CF(0, 128, C_GQ96, (1, 1)).ap, 1.0), w=[CF(0, 128, C_GQ96, (1, 1))])
    add("pool", lambda e: e.memset(CF(0, 128, C_GQ96 + 1, (1, 1)).ap, 0.0), w=[CF(0, 128, C_GQ96 + 1, (1, 1))])
    add("dve", lambda e: e.tensor_tensor(out=CF(0, 64, C_KNG, (1, 64)).ap, in0=CF(0, 64, C_GQ, (1, 64)).ap,
                                         in1=IDF(0, 64, 0, (1, 64)).ap, op=ALU.mult),
        r=[CF(0, 64, C_GQ, (1, 64)), IDF(0, 64, 0, (1, 64))], w=[CF(0, 64, C_KNG, (1, 64))])
    add("dve", lambda e: e.tensor_reduce(out=CF(0, 64, C_GQ96, (1, 1)).ap, in_=CF(0, 64, C_KNG, (1, 64)).ap, axis=AX.X, op=ALU.add),
        r=[CF(0, 64, C_KNG, (1, 64))], w=[CF(0, 64, C_GQ96, (1, 1))])

    dma(RI(0, 128, 0, (128, 4), (1, 128)), w_s.d(0, (128, 128), (128 * 128, 4), (1, 128)))
    add("dve", lambda e: e.tensor_copy(out=RIb(0, 128, 1024, (1, 512)).ap, in_=RI(0, 128, 0, (1, 512)).ap),
        r=[RI(0, 128, 0, (1, 512))], w=[RIb(0, 128, 1024, (1, 512))])
    for g in range(4):
        add("pe", lambda e, g=g: e.transpose(out=PSb[0](0, 128, g * 128, (1, 128)).ap,
                                             in_=RIb(0, 128, 1024 + g * 128, (1, 128)).ap, identity=ident(128).ap),
            r=[RIb(0, 128, 1024 + g * 128, (1, 128)), ident(128)], w=[PSb[0](0, 128, g * 128, (1, 128))])
    add("dve", lambda e: e.tensor_copy(out=CB(0, 128, B_WS, (1, 512)).ap, in_=PSb[0](0, 128, 0, (1, 512)).ap),
        r=[PSb[0](0, 128, 0, (1, 512))], w=[CB(0, 128, B_WS, (1, 512))])
    add("dve", lambda e: e.memset(CB(64, 64, B_WS, (128, 4), (1, 64)).ap, 0.0), w=[CB(64, 64, B_WS, (128, 4), (1, 64))])
    dma(RI(0, 1, 1024, (1, 512)), b_s.d(0, (512, 1), (1, 512)))
    add("dve", lambda e: e.tensor_copy(out=CB(0, 1, B_BS, (1, 512)).ap, in_=RI(0, 1, 1024, (1, 512)).ap),
        r=[RI(0, 1, 1024, (1, 512))], w=[CB(0, 1, B_BS, (1, 512))])

    prep_i = [0]

    def prep_bufs():
        i = prep_i[0]
        prep_i[0] += 1
        return (i % 2) * 5632, (i % 2) * 5632

    def act_copy(o, i):
        add("act", lambda e: e.activation(out=o.ap, in_=i.ap, func=AF.Copy), r=[i], w=[o])

    pieces = []

    for kc in range(8):

        def ld(fo, kc=kc):
            dma(KTf(0, 128, fo, (1, 1696)), w_in.d(kc * 128 * 1696, (1696, 128), (1, 1696)))

        def rest(fo, bo, kc=kc):
            act_copy(RK(0, 128, bo, (1, 1696)), KTf(0, 128, fo, (1, 1696)))
            dma(s_win_u.d(kc * 128, (4096, 128), (1024, 4), (1, 128)), RK(0, 128, bo, (128, 4), (1, 128)))
            dma(s_win_v.d(kc * 512, (4096, 128), (1, 512)), RK(0, 128, bo + 512, (1, 512)))
            dma(s_win_q.d(kc * 384, (3072, 128), (1, 384)), RK(0, 128, bo + 1024, (1, 384)))
            dma(s_win_kv.d(kc * 288, (2304, 128), (1, 288)), RK(0, 128, bo + 1408, (1, 288)))
        pieces.append((ld, rest, False))
    for kc in range(3):

        def ld(fo, kc=kc):
            dma(KTf(0, 128, fo, (1, 768)), w_uq.d(kc * 128 * 768, (768, 128), (1, 768)))

        def rest(fo, bo, kc=kc):
            sc = CF(0, 128, C_QNG + kc, (1, 1))
            for (io, n, oo) in ((0, 64, 0), (64, 32, 512)):
                i_ = KTf(0, 128, fo + io, (96, 8), (1, n))
                o_ = RK(0, 128, bo + oo, (n, 8), (1, n))
                add("dve", lambda e, i_=i_, o_=o_, sc=sc: e.tensor_scalar(out=o_.ap, in0=i_.ap, scalar1=sc.ap, scalar2=None, op0=ALU.mult),
                    r=[i_, sc], w=[o_])
            dma(s_wuq.d(kc * 768, (2304, 128), (1, 768)), RK(0, 128, bo, (1, 768)))
        pieces.append((ld, rest, False))
    for kc in range(2):

        def ld(fo, kc=kc):
            dma(KTf(0, 128, fo, (1, 1024)), w_ukv.d(kc * 128 * 1024, (1024, 128), (1, 1024)))

        def rest(fo, bo, kc=kc):
            for part in range(2):
                act_copy(RK(0, 128, bo + part * 512, (64, 8), (1, 64)), KTf(0, 128, fo + part * 64, (128, 8), (1, 64)))
            dma(s_wukv.d(kc * 1024, (2048, 128), (1, 1024)), RK(0, 128, bo, (1, 1024)))
        pieces.append((ld, rest, False))
    for kc in range(8):

        def ld(fo, kc=kc):
            dma(KTf(0, 128, fo, (1, 1024)), w_out.d(kc * 128 * 1024, (1024, 128), (1, 1024)))

        def rest(fo, bo, kc=kc):
            stg = KTf(0, 128, fo, (1, 1024))
            for v in range(2):
                ob = RK(0, 128, bo + v * 1024, (1, 1024))
                gb = GBC(v, 0, 0, 1024)
                add("dve", lambda e, stg=stg, ob=ob, gb=gb: e.tensor_tensor(out=ob.ap, in0=stg.ap, in1=gb.ap, op=ALU.mult),
                    r=[stg, gb], w=[ob])
                if kc < 4:
                    dma(s_wout_a[v].d(kc * 1024, (4096, 128), (1, 1024)), ob)
                else:
                    m = kc - 4
                    dma(s_wout_b[v].d((2 * m) * 1024, (8192, 64), (1, 1024)), RK(0, 64, bo + v * 1024, (1, 1024)))
                    dma(s_wout_b[v].d((2 * m + 1) * 1024, (8192, 64), (1, 1024)), RK(64, 64, bo + v * 1024, (1, 1024)))
        pieces.append((ld, rest, True))
    for j2 in range(NJ // 2):

        def ld(fo, j2=j2):
            dma(KTf(0, 128, fo, (1024, 2), (1, 1024)), w_fo.d(j2 * 2 * 128 * 1024, (1024, 128), (128 * 1024, 2), (1, 1024)))

        def rest(fo, bo, j2=j2):
            stg = KTf(0, 128, fo, (1024, 2), (1, 1024))
            for v in range(2):
                ob = RK(0, 128, bo + v * 2048, (1024, 2), (1, 1024))
                gb = R(bass.AP(XB.h, (v * 2 + 1) * 1024, [[XB.F, 128], [0, 2], [1, 1024]]), XB.root, (v * 2 + 1) * 4096, (v * 2 + 2) * 4096)
                add("dve", lambda e, stg=stg, ob=ob, gb=gb: e.tensor_tensor(out=ob.ap, in0=stg.ap, in1=gb.ap, op=ALU.mult),
                    r=[stg, gb], w=[ob])
                dma(s_ffn_out[v].d(j2 * 2 * 128 * 1024, (1024, 128), (128 * 1024, 2), (1, 1024)), ob)
        pieces.append((ld, rest, True))
    for kc in range(8):

        def ld(fo, kc=kc):
            dma(KTf(0, 128, fo, (1, 5632)), w_fi.d(kc * 128 * 5632, (5632, 128), (1, 5632)))

        def rest(fo, bo, kc=kc):
            for gu in range(2):
                act_copy(RK(0, 128, bo + gu * 128, (256, NJ), (1, 128)), KTf(0, 128, fo + gu * DFF, (128, NJ), (1, 128)))
            for jh in range(2):
                dma(s_ffn_in.d(jh * 11 * 128 * 2048 + kc * 256, (2048, 128), (128 * 2048, 11), (1, 256)),
                    RK(0, 128, bo + jh * 11 * 256, (256, 11), (1, 256)))
        pieces.append((ld, rest, False))


    order = [i for i, p_ in enumerate(pieces) if not p_[2]] + [i for i, p_ in enumerate(pieces) if p_[2]]
    N_NOGATE = len([1 for p_ in pieces if not p_[2]])
    prep_pos = [0]

    def pump(n, limit=None):
        limit = len(order) if limit is None else limit
        while n > 0 and prep_pos[0] < limit:
            k = prep_pos[0]
            if k == 0:
                pieces[order[0]][0]((0) * 5632)
            if k + 1 < len(order) and k + 1 < limit + 1:
                if k + 1 < len(order):
                    pieces[order[k + 1]][0](((k + 1) % 2) * 5632)
            pieces[order[k]][1]((k % 2) * 5632, (k % 2) * 5632)
            prep_pos[0] += 1
            n -= 1

    for b_ in range(2):
        dma(CF(0, 128, C_SC + b_, (2, 8)), c2.d(b_ * D, (1, 128), (128, 8)), slow=True)
    add("act", lambda e: e.activation(out=CF(0, 128, C_SC, (1, 16)).ap, in_=CF(0, 128, C_SC, (1, 16)).ap, func=AF.Silu),
        r=[CF(0, 128, C_SC, (1, 16))], w=[CF(0, 128, C_SC, (1, 16))])
    MSB = lambda off, n: VSf(0, 2, off, (1, n))
    for half in range(2):
        dma(RI(0, 2, 0, (1, 3072)), b_ada.d(half * 3072, (0, 2), (1, 3072)))
        for kc in range(8):
            st = (half * 8 + kc) % 2
            stg = RINGf(0, 128, st * 3072, (1, 3072))
            dma(stg, w_ada.d(kc * 128 * 6144 + half * 3072, (6144, 128), (1, 3072)))

            def mm(e, kc=kc, st=st):
                ins = None
                for cb in range(6):
                    ins = e.matmul(PS[cb](0, 2, 0, (1, 512)).ap, lhsT=CF(0, 128, C_SC + kc * 2, (1, 2)).ap,
                                   rhs=RINGf(0, 128, st * 3072 + cb * 512, (1, 512)).ap, start=(kc == 0), stop=(kc == 7))
                return ins
            add("pe", mm, r=[stg, CF(0, 128, C_SC, (1, 16))], w=[PS[cb](0, 2, 0, (1, 512)) for cb in range(6)])
            pump(1, N_NOGATE)
        for cb in range(6):
            col = half * 3072 + cb * 512
            add("dve", lambda e, cb=cb, col=col: e.tensor_tensor(out=MSB(col, 512).ap, in0=PS[cb](0, 2, 0, (1, 512)).ap,
                                                                 in1=RI(0, 2, cb * 512, (1, 512)).ap, op=ALU.add),
                r=[PS[cb](0, 2, 0, (1, 512)), RI(0, 2, cb * 512, (1, 512))], w=[MSB(col, 512)])
    WH = [0, 1, 3, 4]

    def tr_mod(e):
        ins = None
        for wi, wh in enumerate(WH):
            for kc in range(8):
                ins = e.transpose(out=PS[6](0, 128, (wi * 8 + kc) * 2, (1, 2)).ap,
                                  in_=MSB(wh * 1024 + kc * 128, 128).ap, identity=IDF(0, 2, 0, (1, 2)).ap)
        return ins
    add("pe", tr_mod, r=[MSB(0, 6144), IDF(0, 2, 0, (1, 2))], w=[PS[6](0, 128, 0, (1, 64))])
    add("dve", lambda e: e.tensor_copy(out=CF(0, 128, C_MOD, (1, 64)).ap, in_=PS[6](0, 128, 0, (1, 64)).ap),
        r=[PS[6](0, 128, 0, (1, 64))], w=[CF(0, 128, C_MOD, (1, 64))])
    for stage, (wi, goff) in enumerate([(1, C_N1G), (3, C_N2G)]):
        for b in range(2):
            src = CF(0, 128, C_MOD + wi * 16 + b, (2, 8))
            dst = CF(0, 128, C_A[(stage, b)], (1, 8))
            add("dve", lambda e, src=src, dst=dst, goff=goff: e.scalar_tensor_tensor(
                out=dst.ap, in0=src.ap, scalar=1.0, in1=CF(0, 128, goff, (1, 8)).ap, op0=ALU.add, op1=ALU.mult),
                r=[src, CF(0, 128, goff, (1, 8))], w=[dst])

    def a_col(stage, b, fc):
        return CF(0, 128, C_A[(stage, b)] + fc, (1, 1))

    def b_col(stage, b, fc):
        wi = 0 if stage == 0 else 2
        return CF(0, 128, C_MOD + wi * 16 + fc * 2 + b, (1, 1))

    def GBC(b, which, off, n):
        return XB(0, 128, (b * 2 + which) * 1024 + off, (1, n))
    for b in range(2):
        for which, wh in enumerate([2, 5]):
            for half in range(2):
                k = 4 + ((b * 4 + which * 2 + half) % 2)
                add("pe", lambda e, b=b, wh=wh, half=half, k=k: e.matmul(
                    PS[k](0, 128, 0, (1, 512)).ap, lhsT=CF(0, 2, C_SEL + b * 128, (1, 128)).ap,
                    rhs=MSB(wh * 1024 + half * 512, 512).ap, start=True, stop=True),
                    r=[CF(0, 2, C_SEL, (1, 256)), MSB(wh * 1024 + half * 512, 512)], w=[PS[k](0, 128, 0, (1, 512))])
                add("act", lambda e, b=b, which=which, half=half, k=k: e.activation(
                    out=GBC(b, which, half * 512, 512).ap, in_=PS[k](0, 128, 0, (1, 512)).ap, func=AF.Copy),
                    r=[PS[k](0, 128, 0, (1, 512))], w=[GBC(b, which, half * 512, 512)])

    pump(1000)
    S.barrier()
    add("pool", lambda e: e.memset(VS(0, 128, 64, (65, 256), (1, 1)).ap, 1.0), w=[VS(0, 128, 64, (65, 256), (1, 1))])

    ring_n = [0]

    def ring_load(pieces):
        k = ring_n[0] % NRING
        ring_n[0] += 1
        base = k * SLOT
        for fn in pieces:
            dst, src = fn(base)
            dma(dst, src)
        return base

    def ld_full(src_t, n, np_=128, src_off=0, row=None):
        row = row if row is not None else src_t.F
        return lambda base: (RING(0, np_, base, (1, n)), src_t.d(src_off, (row, np_), (1, n)))

    def norm_stats(tt, rows):
        if True:
            xin = XB(0, rows, tt * D, (1, D))
            junk = RK(0, rows, tt * D, (1, D))
            ss = RI(0, rows, 4300 + tt, (1, 1))
            rs = RI(0, rows, 4310 + tt, (1, 1))
            add("act", lambda e, xin=xin, junk=junk, ss=ss: e.activation(out=junk.ap, in_=xin.ap, func=AF.Square, accum_out=ss.ap),
                r=[xin], w=[junk, ss])
            add("act", lambda e, ss=ss, rs=rs: e.activation(out=rs.ap, in_=ss.ap, func=AF.Sqrt, scale=1.0 / D, bias=EPS_AP(rows).ap),
                r=[ss, EPS_AP(rows)], w=[rs])
            add("dve", lambda e, rs=rs: e.reciprocal(out=rs.ap, in_=rs.ap), r=[rs], w=[rs])
            add("dve", lambda e, xin=xin, junk=junk, rs=rs: e.tensor_scalar(out=junk.ap, in0=xin.ap, scalar1=rs.ap, scalar2=None, op0=ALU.mult),
                r=[xin, rs], w=[junk])

    def norm_T(stage, b, rows, NT, NQ, skip_stats=False, early=False):
        if not skip_stats:
            for tt in range(NT):
                norm_stats(tt, rows)
        XHT, xh0 = (RIb, 4096) if early else (RK, 0)
        for fcp in range(4):
            k = rot()

            def tr(e, fcp=fcp, k=k):
                ins = None
                for fcl in range(2):
                    fc = fcp * 2 + fcl
                    for tt in range(NT):
                        ins = e.transpose(out=PSb[k](0, 128, fcl * 512 + tt * rows, (1, rows)).ap,
                                          in_=XHT(0, rows, xh0 + tt * D + fc * 128, (1, 128)).ap, identity=ident(rows).ap)
                return ins
            add("pe", tr, r=[XHT(0, rows, xh0, (1, NT * D)), ident(rows)], w=[PSb[k](0, 128, 0, (1, 1024))])
            for fcl in range(2):
                fc = fcp * 2 + fcl
                o = HT(0, 128, fc * TB, (1, NQ))
                i = PSb[k](0, 128, fcl * 512, (1, NQ))
                add("act", lambda e, o=o, i=i, fc=fc: e.activation(out=o.ap, in_=i.ap, func=AF.Identity,
                                                                   scale=a_col(stage, b, fc).ap, bias=b_col(stage, b, fc).ap),
                    r=[i, a_col(stage, b, fc), b_col(stage, b, fc)], w=[o])

    XS = lambda: RI(0, 128, 1024, (1, D))

    def early_load(tok0):
        dma(XS(), x_p.d(tok0 * D, (D, 128), (1, D)))

    def early_stats(tt):
        xin = XS()
        junk = RIb(0, 128, 4096 + tt * D, (1, D))
        ss = RI(0, 128, 4320 + tt, (1, 1))
        rs = RI(0, 128, 4330 + tt, (1, 1))
        add("act", lambda e: e.activation(out=junk.ap, in_=xin.ap, func=AF.Square, accum_out=ss.ap), r=[xin], w=[junk, ss])
        add("act", lambda e: e.activation(out=rs.ap, in_=ss.ap, func=AF.Sqrt, scale=1.0 / D, bias=EPS_AP(128).ap),
            r=[ss, EPS_AP(128)], w=[rs])
        add("dve", lambda e: e.reciprocal(out=rs.ap, in_=rs.ap), r=[rs], w=[rs])
        add("dve", lambda e: e.tensor_scalar(out=junk.ap, in0=xin.ap, scalar1=rs.ap, scalar2=None, op0=ALU.mult),
            r=[xin, rs], w=[junk])

    def EPS_AP(rows):
        return CF(0, rows, 960, (1, 1))
    add("pool", lambda e: e.memset(CF(0, 128, 960, (1, 1)).ap, EPS), w=[CF(0, 128, 960, (1, 1))])

    O_UT, O_V, O_YA, O_YB = 0, 2048, 4096, 6144
    T_SQ, T_SQR, T_QN, T_QR, T_T1, T_T2, T_CKV, T_KR, T_KRO, T_ST = 0, 512, 768, 1280, 1536, 1792, 2048, 2560, 2624, 2688
    TB_QF, TB_KF, TB_CQN, TB_CQT, TB_CKVB, TB_CKVT = 5504, 6272, 7040, 7424, 7808, 8064
    T_ROPE = 4160
    A_PT = [0, 512, 1024, 1536]
    A_OSB, A_RC = 1024, 1536

    def rmsnorm_stats(src, rows, n_in, ss, rs, junk):
        add("act", lambda e: e.activation(out=junk.ap, in_=src.ap, func=AF.Square, accum_out=ss.ap), r=[src], w=[junk, ss])
        add("act", lambda e: e.activation(out=rs.ap, in_=ss.ap, func=AF.Sqrt, scale=1.0 / n_in, bias=EPS_AP(rows).ap),
            r=[ss, EPS_AP(rows)], w=[rs])
        add("dve", lambda e: e.reciprocal(out=rs.ap, in_=rs.ap), r=[rs], w=[rs])

    def rope_ops(x, out, rows, nh, tab):
        xo, oo = x, out
        t1 = RI(0, rows, T_T1, (32, nh), (1, 32))
        add("pool", lambda e: e.tensor_tensor(out=t1.ap, in0=RI(0, rows, xo, (32, nh), (1, 32)).ap,
                                              in1=RI(0, rows, tab, (0, nh), (1, 32)).ap, op=ALU.mult),
            r=[RI(0, rows, xo, (32, nh), (1, 32)), RI(0, rows, tab, (1, 32))], w=[t1])
        for hf in range(2):
            t2 = RI(0, rows, T_T2 + hf * 16, (32, nh), (1, 16))
            xi = RI(0, rows, xo + (1 - hf) * 16, (32, nh), (1, 16))
            sn = RI(0, rows, tab + 32 + hf * 16, (0, nh), (1, 16))
            add("pool", lambda e, t2=t2, xi=xi, sn=sn: e.tensor_tensor(out=t2.ap, in0=xi.ap, in1=sn.ap, op=ALU.mult),
                r=[xi, RI(0, rows, tab + 32, (1, 32))], w=[t2])
        t2a = RI(0, rows, T_T2, (32, nh), (1, 32))
        add("pool", lambda e: e.tensor_tensor(out=oo.ap, in0=t1.ap, in1=t2a.ap, op=ALU.add), r=[t1, t2a], w=[oo])

    RKf = view(RK, F32)

    def ts_default():
        return dict(f=RI, b=RIb, sq=T_SQ, st=T_ST + 16, ckvb=TB_CKVB, ckvT=TB_CKVT, kfull=TB_KF)

    def ts_rk(i):
        B = i * 3072
        return dict(f=RKf, b=RK, sq=B // 2, cf=B // 2 + 512, kf=B // 2 + 768, st=B // 2 + 800,
                    ckvb=B + 1664, ckvT=B + 1920, kfull=B + 2176)

    def kv_steps(ckvf, kropef, kt, rows, w_kv_base, ts, pre=None, alloc=None):
        alloc = alloc or rot
        TF, TBb = ts["f"], ts["b"]
        o_sq, o_st, o_cb, o_ct, o_kf = ts["sq"], ts["st"], ts["ckvb"], ts["ckvT"], ts["kfull"]
        st_ = {}
        steps = []
        if pre is not None:
            steps.append(pre)

        def s1():
            ckvb = TBb(0, rows, o_cb, (1, 256))
            add("pool", lambda e: e.tensor_copy(out=ckvb.ap, in_=ckvf.ap), r=[ckvf], w=[ckvb])
        steps.append(s1)

        def s2():
            k = alloc()
            st_["k"] = k

            def tr(e):
                ins = None
                for kc in range(2):
                    ins = e.transpose(out=PSb[k](0, 128, kc * 128, (1, rows)).ap,
                                      in_=TBb(0, rows, o_cb + kc * 128, (1, 128)).ap, identity=ident(rows).ap)
                return ins
            add("pe", tr, r=[TBb(0, rows, o_cb, (1, 256)), ident(rows)], w=[PSb[k](0, 128, 0, (1, 256))])
        steps.append(s2)

        def s3():
            k = st_["k"]
            ckvT = TBb(0, 128, o_ct, (128, 2), (1, rows))
            add("dve", lambda e: e.tensor_copy(out=ckvT.ap, in_=PSb[k](0, 128, 0, (128, 2), (1, rows)).ap),
                r=[PSb[k](0, 128, 0, (1, 256))], w=[ckvT])
        steps.append(s3)

        def s4():
            ka, kb = alloc(), alloc()
            st_["ka"], st_["kb"] = ka, kb

            def mm(e):
                ins = None
                for kc in range(2):
                    for part, kk in ((0, ka), (1, kb)):
                        ins = e.matmul(PS[kk](0, rows, 0, (1, 512)).ap, lhsT=TBb(0, 128, o_ct + kc * 128, (1, rows)).ap,
                                       rhs=RING(0, 128, w_kv_base + kc * 1024 + part * 512, (1, 512)).ap,
                                       start=(kc == 0), stop=(kc == 1))
                return ins
            add("pe", mm, r=[TBb(0, 128, o_ct, (1, 256)), RING(0, 128, w_kv_base, (1, 2048))],
                w=[PS[ka](0, rows, 0, (1, 512)), PS[kb](0, rows, 0, (1, 512))])
        steps.append(s4)

        def s5():
            ka, kb = st_["ka"], st_["kb"]
            vdst = VS(0, rows, kt * 520, (65, 8), (1, 64))
            sq = TF(0, rows, o_sq, (1, 512))
            add("act", lambda e: e.activation(out=sq.ap, in_=PS[ka](0, rows, 0, (1, 512)).ap, func=AF.Square),
                r=[PS[ka](0, rows, 0, (1, 512))], w=[sq])
            add("act", lambda e: e.activation(out=vdst.ap, in_=PS[kb](0, rows, 0, (64, 8), (1, 64)).ap, func=AF.Copy),
                r=[PS[kb](0, rows, 0, (1, 512))], w=[vdst])
        steps.append(s5)

        def s6():
            st = TF(0, rows, o_st, (1, 8))
            add("dve", lambda e: e.tensor_reduce(out=st.ap, in_=TF(0, rows, o_sq, (64, 8), (1, 64)).ap, axis=AX.X, op=ALU.add),
                r=[TF(0, rows, o_sq, (1, 512))], w=[st])
        steps.append(s6)

        def s7():
            st = TF(0, rows, o_st, (1, 8))
            add("act", lambda e: e.activation(out=st.ap, in_=st.ap, func=AF.Sqrt, scale=1.0 / 64, bias=EPS_AP(rows).ap),
                r=[st, EPS_AP(rows)], w=[st])
        steps.append(s7)

        def s8():
            st = TF(0, rows, o_st, (1, 8))
            add("dve", lambda e: e.reciprocal(out=st.ap, in_=st.ap), r=[st], w=[st])
        steps.append(s8)

        def s9():
            ka = st_["ka"]
            st = TF(0, rows, o_st, (1, 8))
            kf_n = TBb(0, rows, o_kf, (96, 8), (1, 64))
            add("dve", lambda e: e.tensor_tensor(out=kf_n.ap, in0=PS[ka](0, rows, 0, (64, 8), (1, 64)).ap,
                                                 in1=TF(0, rows, o_st, (1, 8), (0, 64)).ap, op=ALU.mult),
                r=[PS[ka](0, rows, 0, (1, 512)), st], w=[kf_n])
            kf_r = TBb(0, rows, o_kf + 64, (96, 8), (1, 32))
            krb = R(bass.AP(kropef.ap.tensor, kropef.ap.offset, [list(kropef.ap.ap[0]), [0, 8], [1, 32]]), kropef.root, kropef.lo, kropef.hi)
            add("pool", lambda e: e.tensor_copy(out=kf_r.ap, in_=krb.ap), r=[kropef], w=[kf_r])
        steps.append(s9)

        def s10():
            kfull = TBb(0, rows, o_kf, (1, 768))
            for hh in range(2):
                k2 = alloc()
                st_["k2", hh] = k2

                def trk(e, hh=hh, k2=k2):
                    ins = None
                    for hl in range(4):
                        h = hh * 4 + hl
                        ins = e.transpose(out=PSb[k2](0, 96, hl * 128, (1, rows)).ap,
                                          in_=TBb(0, rows, o_kf + h * 96, (1, 96)).ap, identity=ident(rows).ap)
                    return ins
                add("pe", trk, r=[kfull, ident(rows)], w=[PSb[k2](0, 96, 0, (1, 512))])
        steps.append(s10)

        def s11():
            for hh in range(2):
                k2 = st_["k2", hh]
                kdst = KT(0, 96, (hh * 4) * SEQ + kt * 128, (SEQ, 4), (1, rows))
                add("act" if hh == 0 else "dve",
                    (lambda e, kdst=kdst, k2=k2: e.activation(out=kdst.ap, in_=PSb[k2](0, 96, 0, (128, 4), (1, rows)).ap, func=AF.Copy)) if hh == 0 else
                    (lambda e, kdst=kdst, k2=k2: e.tensor_copy(out=kdst.ap, in_=PSb[k2](0, 96, 0, (128, 4), (1, rows)).ap)),
                    r=[PSb[k2](0, 96, 0, (1, 512))], w=[kdst])
        steps.append(s11)
        return steps

    def mk_alloc(banks):
        st = [0]

        def alloc():
            k = banks[st[0] % len(banks)]
            st[0] += 1
            return k
        return alloc

    def run_chains(chains, lo=0, hi=None):
        n = max(len(c) for c in chains)
        hi = n if hi is None else min(hi, n)
        for si in range(lo, hi):
            for c in chains:
                if si < len(c):
                    c[si]()

    def kv_common(ckvf, kropef, kt, rows, w_kv_base):
        run_chains([kv_steps(ckvf, kropef, kt, rows, w_kv_base, ts_default())])

    early_state = {"done": False}
    PRE = {}

    def process_block(is_s, blk):
        b = 1 if is_s else 0
        rows = 64 if is_s else 128
        NT = 1 if is_s else 4
        NQ = rows * NT
        xsrc = x_s if is_s else x_p
        ydst = y_s if is_s else y_p
        rsrc = rope_s if is_s else rope_p
        ockv = o_ckv_s if is_s else o_ckv_p
        okr = o_kr_s if is_s else o_kr_p
        t0 = 0 if is_s else blk * TB
        kt0 = 16 if is_s else blk * 4

        dma(XB(0, rows, 0, (D, NT), (1, D)), xsrc.d(t0 * D, (D, rows), (rows * D, NT), (1, D)))
        if is_s:
            wkv0 = ring_load([ld_full(s_win_kv, 2304), lambda base: (RING(0, 128, base + 2304, (1, 2048)), s_wukv.d(0, (2048, 128), (1, 2048)))])
            rot_n[0] = 4
            for g0 in range(0, 16, 4):
                chains = []
                for gi_ in range(4):
                    kt = g0 + gi_
                    if gi_ == 0:
                        ts = ts_default()
                        cf = RI(0, 128, T_CKV, (1, 256))
                        kf = RI(0, 128, T_KRO, (1, 32))
                    else:
                        ts = ts_rk(gi_ - 1)
                        cf = RKf(0, 128, ts["cf"], (1, 256))
                        kf = RKf(0, 128, ts["kf"], (1, 32))

                    def pre(cf=cf, kf=kf, kt=kt):
                        dma(cf, cckv.d(kt * 128 * 256, (256, 128), (1, 256)))
                        dma(kf, ckr.d(kt * 128 * 32, (32, 128), (1, 32)))
                    chains.append(kv_steps(cf, kf, kt, 128, wkv0 + 2304, ts, pre=pre, alloc=mk_alloc([2 * gi_, 2 * gi_ + 1])))
                run_chains(chains)
            rot_n[0] = 4
        if early_state["done"]:
            early_state["done"] = False
        else:
            norm_T(0, b, rows, NT, NQ)
        def pre_or_load(name, pieces):
            v_ = PRE.pop(name, None)
            return v_ if v_ is not None else ring_load(pieces)
        wu = pre_or_load("wu", [ld_full(s_win_u, 4096)])
        wv = pre_or_load("wv", [ld_full(s_win_v, 4096)])
        wq = pre_or_load("wq", [ld_full(s_win_q, 3072)])
        wkv = pre_or_load("wkv", [ld_full(s_win_kv, 2304), lambda base: (RING(0, 128, base + 2304, (1, 2048)), s_wukv.d(0, (2048, 128), (1, 2048)))])

        def emit_IN_u(falloc):
            for c in range(4):
                k = falloc()

                def mm(e, c=c, k=k):
                    ins = None
                    for kc in range(8):
                        ins = e.matmul(PS[k](0, 128, 0, (1, NQ)).ap, lhsT=RING(0, 128, wu + c * 1024 + kc * 128, (1, 128)).ap,
                                       rhs=HT(0, 128, kc * TB, (1, NQ)).ap, start=(kc == 0), stop=(kc == 7))
                    return ins
                add("pe", mm, r=[RING(0, 128, wu, (1, 4096)), HT(0, 128, 0, (1, 8 * TB))], w=[PS[k](0, 128, 0, (1, NQ))])
                o = RK(0, 128, O_UT + c * TB, (1, NQ))
                add("act", lambda e, o=o, k=k: e.activation(out=o.ap, in_=PS[k](0, 128, 0, (1, NQ)).ap, func=AF.Gelu_apprx_tanh),
                    r=[PS[k](0, 128, 0, (1, NQ))], w=[o])

        def emit_IN_v(falloc):
            for tt in range(NT):
                k = falloc()

                def mm(e, tt=tt, k=k):
                    ins = None
                    for kc in range(8):
                        ins = e.matmul(PS[k](0, rows, 0, (1, 512)).ap, lhsT=HT(0, 128, kc * TB + tt * rows, (1, rows)).ap,
                                       rhs=RING(0, 128, wv + kc * 512, (1, 512)).ap, start=(kc == 0), stop=(kc == 7))
                    return ins
                add("pe", mm, r=[RING(0, 128, wv, (1, 4096)), HT(0, 128, 0, (1, 8 * TB))], w=[PS[k](0, rows, 0, (1, 512))])
                o = RK(0, rows, O_V + tt * 512, (1, 512))
                if is_s:
                    vf = RI(0, rows, T_SQ, (1, 512))
                    add("act", lambda e, vf=vf, k=k: e.activation(out=vf.ap, in_=PS[k](0, rows, 0, (1, 512)).ap, func=AF.Gelu_apprx_tanh),
                        r=[PS[k](0, rows, 0, (1, 512))], w=[vf])
                    dma(o_v_s.d(0, (512, rows), (1, 512)), vf)
                    add("pool", lambda e, o=o, vf=vf: e.tensor_copy(out=o.ap, in_=vf.ap), r=[vf], w=[o])
                else:
                    add("act", lambda e, o=o, k=k: e.activation(out=o.ap, in_=PS[k](0, rows, 0, (1, 512)).ap, func=AF.Gelu_apprx_tanh),
                        r=[PS[k](0, rows, 0, (1, 512))], w=[o])

        def emit_G(falloc):
            for g in range(4):
                k = falloc()

                def mm(e, g=g, k=k):
                    ins = None
                    for tt in range(NT):
                        e.matmul(PS[k](0, 128, tt * rows, (1, rows)).ap, lhsT=RK(0, rows, O_V + tt * 512 + g * 128, (1, 128)).ap,
                                 rhs=CB(0, rows, B_WS + g * 128, (1, rows)).ap, start=True, stop=False)
                        ins = e.matmul(PS[k](0, 128, tt * rows, (1, rows)).ap, lhsT=CB(0, 1, B_ONE, (1, 128)).ap,
                                       rhs=CB(0, 1, B_BS + g * 128, (1, rows)).ap, start=False, stop=True)
                    return ins
                add("pe", mm, r=[RK(0, rows, O_V, (1, 2048)), CB(0, 128, B_WS, (1, 1152))], w=[PS[k](0, 128, 0, (1, NQ))])
                o = RK(0, 128, O_YA + g * TB, (1, NQ))
                u_ = RK(0, 128, O_UT + g * TB, (1, NQ))
                add("dve", lambda e, o=o, u_=u_, k=k: e.tensor_tensor(out=o.ap, in0=PS[k](0, 128, 0, (1, NQ)).ap, in1=u_.ap, op=ALU.mult),
                    r=[PS[k](0, 128, 0, (1, NQ)), u_], w=[o])
        rot_n[0] = 4
        XBb = view(XB, BF16)
        SETS = [(RI, RIb), (XB, XBb)]
        Q_SQ, Q_SQR, Q_QR, Q_T1, Q_T2, K_SQ, K_CKVF, K_KRF, K_KROF, K_T1, K_T2, X_ST, X_TAB, X_TABQ, X_TABK = \
            0, 512, 768, 1024, 1280, 1536, 2048, 2304, 2336, 2368, 2400, 2432, 2496, 2560, 2624
        B_CQN, B_CQT, B_QF, B_CKVB, B_CKVT, B_KF = 5376, 5760, 6144, 6912, 7168, 7424

        def tile_chains(tt, F_, Bv, qb, kb_):
            qalloc, kalloc = mk_alloc(qb), mk_alloc(kb_)
            tok0 = t0 + tt * rows
            pre_steps = []
            qs = []
            ks = []
            stq = {}

            def p0():
                dma(F_(0, rows, X_TAB, (1, 64)), rsrc.d(tok0 * 64, (64, rows), (1, 64)))
                for (dst, gC, gS) in ((X_TABQ, C_QRG, C_QRS), (X_TABK, C_KRG, C_KRS)):
                    for part, g_ in ((0, gC), (1, gS)):
                        o = F_(0, rows, dst + part * 32, (1, 32))
                        i = F_(0, rows, X_TAB + part * 32, (1, 32))
                        gg = CF(0, rows, g_, (1, 32))
                        add("pool", lambda e, o=o, i=i, gg=gg: e.tensor_tensor(out=o.ap, in0=i.ap, in1=gg.ap, op=ALU.mult),
                            r=[i, gg], w=[o])
            pre_steps.append(p0)

            def rope_chain(steps, x_off, nh, tab, t1o, t2o, out_fn):
                def r0():
                    t1 = F_(0, rows, t1o, (32, nh), (1, 32))
                    add("pool", lambda e: e.tensor_tensor(out=t1.ap, in0=F_(0, rows, x_off, (32, nh), (1, 32)).ap,
                                                          in1=F_(0, rows, tab, (0, nh), (1, 32)).ap, op=ALU.mult),
                        r=[F_(0, rows, x_off, (1, 32 * nh)), F_(0, rows, tab, (1, 32))], w=[t1])
                    for hf in range(2):
                        t2 = F_(0, rows, t2o + hf * 16, (32, nh), (1, 16))
                        xi = F_(0, rows, x_off + (1 - hf) * 16, (32, nh), (1, 16))
                        sn = F_(0, rows, tab + 32 + hf * 16, (0, nh), (1, 16))
                        add("pool", lambda e, t2=t2, xi=xi, sn=sn: e.tensor_tensor(out=t2.ap, in0=xi.ap, in1=sn.ap, op=ALU.mult),
                            r=[xi, F_(0, rows, tab + 32, (1, 32))], w=[t2])
                steps.append(r0)

                def r1():
                    oo = out_fn()
                    t1 = F_(0, rows, t1o, (32, nh), (1, 32))
                    t2a = F_(0, rows, t2o, (32, nh), (1, 32))
                    add("pool", lambda e: e.tensor_tensor(out=oo.ap, in0=t1.ap, in1=t2a.ap, op=ALU.add), r=[t1, t2a], w=[oo])
                steps.append(r1)

            def q0():
                k = qalloc()
                stq["cq"] = k

                def mmq(e):
                    ins = None
                    for kc in range(8):
                        ins = e.matmul(PS[k](0, rows, 0, (1, 384)).ap, lhsT=HT(0, 128, kc * TB + tt * rows, (1, rows)).ap,
                                       rhs=RING(0, 128, wq + kc * 384, (1, 384)).ap, start=(kc == 0), stop=(kc == 7))
                    return ins
                add("pe", mmq, r=[RING(0, 128, wq, (1, 3072)), HT(0, 128, 0, (1, 8 * TB))], w=[PS[k](0, rows, 0, (1, 384))])
            qs.append(q0)

            def q1():
                k = stq["cq"]
                cq = PS[k](0, rows, 0, (1, 384))
                junk = F_(0, rows, Q_SQ, (1, 384))
                ssq = F_(0, rows, X_ST, (1, 1))
                add("act", lambda e: e.activation(out=junk.ap, in_=cq.ap, func=AF.Square, scale=384 ** -0.5, accum_out=ssq.ap),
                    r=[cq], w=[junk, ssq])
            qs.append(q1)

            def q2():
                ssq = F_(0, rows, X_ST, (1, 1))
                add("act", lambda e: e.activation(out=ssq.ap, in_=ssq.ap, func=AF.Sqrt, bias=EPS_AP(rows).ap),
                    r=[ssq, EPS_AP(rows)], w=[ssq])
            qs.append(q2)

            def q3():
                ssq = F_(0, rows, X_ST, (1, 1))
                add("dve", lambda e: e.reciprocal(out=ssq.ap, in_=ssq.ap), r=[ssq], w=[ssq])
            qs.append(q3)

            def q4():
                k = stq["cq"]
                cq = PS[k](0, rows, 0, (1, 384))
                rq = F_(0, rows, X_ST, (1, 1))
                cqn = Bv(0, rows, B_CQN, (1, 384))
                add("dve", lambda e: e.tensor_scalar(out=cqn.ap, in0=cq.ap, scalar1=rq.ap, scalar2=None, op0=ALU.mult),
                    r=[cq, rq], w=[cqn])
            qs.append(q4)

            def q5():
                k2 = qalloc()
                stq["k2"] = k2

                def trq(e):
                    ins = None
                    for kc in range(3):
                        ins = e.transpose(out=PSb[k2](0, 128, kc * 128, (1, rows)).ap,
                                          in_=Bv(0, rows, B_CQN + kc * 128, (1, 128)).ap, identity=ident(rows).ap)
                    return ins
                add("pe", trq, r=[Bv(0, rows, B_CQN, (1, 384)), ident(rows)], w=[PSb[k2](0, 128, 0, (1, 384))])
            qs.append(q5)

            def q6():
                k2 = stq["k2"]
                cqT = Bv(0, 128, B_CQT, (128, 3), (1, rows))
                add("act", lambda e: e.activation(out=cqT.ap, in_=PSb[k2](0, 128, 0, (128, 3), (1, rows)).ap, func=AF.Copy),
                    r=[PSb[k2](0, 128, 0, (1, 384))], w=[cqT])
            qs.append(q6)

            def q7():
                ka, kb = qalloc(), qalloc()
                stq["ka"], stq["kb"] = ka, kb

                def mmuq(e):
                    ins = None
                    for kc in range(3):
                        e.matmul(PS[ka](0, rows, 0, (1, 512)).ap, lhsT=Bv(0, 128, B_CQT + kc * 128, (1, rows)).ap,
                                 rhs=RING(0, 128, wuq + kc * 768, (1, 512)).ap, start=(kc == 0), stop=(kc == 2))
                        ins = e.matmul(PS[kb](0, rows, 0, (1, 256)).ap, lhsT=Bv(0, 128, B_CQT + kc * 128, (1, rows)).ap,
                                       rhs=RING(0, 128, wuq + kc * 768 + 512, (1, 256)).ap, start=(kc == 0), stop=(kc == 2))
                    return ins
                add("pe", mmuq, r=[Bv(0, 128, B_CQT, (1, 384)), RING(0, 128, wuq, (1, 2304))],
                    w=[PS[ka](0, rows, 0, (1, 512)), PS[kb](0, rows, 0, (1, 256))])
            qs.append(q7)

            def q8():
                ka, kb = stq["ka"], stq["kb"]
                sq, sqr = F_(0, rows, Q_SQ, (1, 512)), F_(0, rows, Q_SQR, (1, 256))
                add("act", lambda e: e.activation(out=sq.ap, in_=PS[ka](0, rows, 0, (1, 512)).ap, func=AF.Square),
                    r=[PS[ka](0, rows, 0, (1, 512))], w=[sq])
                add("act", lambda e: e.activation(out=sqr.ap, in_=PS[kb](0, rows, 0, (1, 256)).ap, func=AF.Square),
                    r=[PS[kb](0, rows, 0, (1, 256))], w=[sqr])
            qs.append(q8)

            def q9():
                stn, str_ = F_(0, rows, X_ST + 2, (1, 8)), F_(0, rows, X_ST + 10, (1, 8))
                add("dve", lambda e: e.tensor_reduce(out=stn.ap, in_=F_(0, rows, Q_SQ, (64, 8), (1, 64)).ap, axis=AX.X, op=ALU.add),
                    r=[F_(0, rows, Q_SQ, (1, 512))], w=[stn])
                add("dve", lambda e: e.tensor_reduce(out=str_.ap, in_=F_(0, rows, Q_SQR, (32, 8), (1, 32)).ap, axis=AX.X, op=ALU.add),
                    r=[F_(0, rows, Q_SQR, (1, 256))], w=[str_])
            qs.append(q9)

            def q10():
                stn, str_ = F_(0, rows, X_ST + 2, (1, 8)), F_(0, rows, X_ST + 10, (1, 8))
                add("act", lambda e: e.activation(out=stn.ap, in_=stn.ap, func=AF.Sqrt, scale=1.0 / 64, bias=EPS_AP(rows).ap),
                    r=[stn, EPS_AP(rows)], w=[stn])
                add("act", lambda e: e.activation(out=str_.ap, in_=str_.ap, func=AF.Sqrt, scale=1.0 / 32, bias=EPS_AP(rows).ap),
                    r=[str_, EPS_AP(rows)], w=[str_])
            qs.append(q10)

            def q11():
                st16 = F_(0, rows, X_ST + 2, (1, 16))
                add("dve", lambda e: e.reciprocal(out=st16.ap, in_=st16.ap), r=[st16], w=[st16])
            qs.append(q11)

            def q12():
                ka, kb = stq["ka"], stq["kb"]
                qf_n = Bv(0, rows, B_QF, (96, 8), (1, 64))
                add("dve", lambda e: e.tensor_tensor(out=qf_n.ap, in0=PS[ka](0, rows, 0, (64, 8), (1, 64)).ap,
                                                     in1=F_(0, rows, X_ST + 2, (1, 8), (0, 64)).ap, op=ALU.mult),
                    r=[PS[ka](0, rows, 0, (1, 512)), F_(0, rows, X_ST + 2, (1, 8))], w=[qf_n])
                qr = F_(0, rows, Q_QR, (32, 8), (1, 32))
                add("dve", lambda e: e.tensor_tensor(out=qr.ap, in0=PS[kb](0, rows, 0, (32, 8), (1, 32)).ap,
                                                     in1=F_(0, rows, X_ST + 10, (1, 8), (0, 32)).ap, op=ALU.mult),
                    r=[PS[kb](0, rows, 0, (1, 256)), F_(0, rows, X_ST + 10, (1, 8))], w=[qr])
            qs.append(q12)
            rope_chain(qs, Q_QR, 8, X_TABQ, Q_T1, Q_T2, lambda: Bv(0, rows, B_QF + 64, (96, 8), (1, 32)))

            def q15():
                qfull = Bv(0, rows, B_QF, (1, 768))
                for hh in range(2):
                    k3 = qalloc()
                    stq["k3", hh] = k3

                    def trqf(e, hh=hh, k3=k3):
                        ins = None
                        for hl in range(4):
                            h = hh * 4 + hl
                            ins = e.transpose(out=PSb[k3](0, 96, hl * 128, (1, rows)).ap,
                                              in_=Bv(0, rows, B_QF + h * 96, (1, 96)).ap, identity=ident(rows).ap)
                        return ins
                    add("pe", trqf, r=[qfull, ident(rows)], w=[PSb[k3](0, 96, 0, (1, 512))])
            qs.append(q15)

            def q16():
                gcol = CF(0, 96, C_GQ96, (1, 1))
                for hh in range(2):
                    k3 = stq["k3", hh]
                    qdst = QTv(0, 96, (hh * 4) * TB + tt * rows, (TB, 4), (1, rows))
                    src = PSb[k3](0, 96, 0, (128, 4), (1, rows))
                    if hh == 0:
                        add("act", lambda e, qdst=qdst, src=src: e.activation(out=qdst.ap, in_=src.ap, func=AF.Identity, scale=gcol.ap, bias=CF(0, 96, C_GQ96 + 1, (1, 1)).ap),
                            r=[PSb[k3](0, 96, 0, (1, 512)), gcol], w=[qdst])
                    else:
                        add("dve", lambda e, qdst=qdst, src=src: e.tensor_scalar(out=qdst.ap, in0=src.ap, scalar1=gcol.ap, scalar2=None, op0=ALU.mult),
                            r=[PSb[k3](0, 96, 0, (1, 512)), gcol], w=[qdst])
            qs.append(q16)

            def k0():
                k = kalloc()
                stq["kv"] = k

                def mmkv(e):
                    ins = None
                    for kc in range(8):
                        ins = e.matmul(PS[k](0, rows, 0, (1, 288)).ap, lhsT=HT(0, 128, kc * TB + tt * rows, (1, rows)).ap,
                                       rhs=RING(0, 128, wkv + kc * 288, (1, 288)).ap, start=(kc == 0), stop=(kc == 7))
                    return ins
                add("pe", mmkv, r=[RING(0, 128, wkv, (1, 2304)), HT(0, 128, 0, (1, 8 * TB))], w=[PS[k](0, rows, 0, (1, 288))])
            ks.append(k0)

            def k1():
                k = stq["kv"]
                ckvp, krp = PS[k](0, rows, 0, (1, 256)), PS[k](0, rows, 256, (1, 32))
                j1, j2 = F_(0, rows, K_SQ, (1, 256)), F_(0, rows, K_SQ + 256, (1, 32))
                s1, s2 = F_(0, rows, X_ST + 24, (1, 1)), F_(0, rows, X_ST + 25, (1, 1))
                add("act", lambda e: e.activation(out=j1.ap, in_=ckvp.ap, func=AF.Square, scale=1.0 / 16, accum_out=s1.ap),
                    r=[ckvp], w=[j1, s1])
                add("act", lambda e: e.activation(out=j2.ap, in_=krp.ap, func=AF.Square, scale=32 ** -0.5, accum_out=s2.ap),
                    r=[krp], w=[j2, s2])
            ks.append(k1)

            def k2():
                s12 = F_(0, rows, X_ST + 24, (1, 2))
                add("act", lambda e: e.activation(out=s12.ap, in_=s12.ap, func=AF.Sqrt, bias=EPS_AP(rows).ap),
                    r=[s12, EPS_AP(rows)], w=[s12])
            ks.append(k2)

            def k3():
                s12 = F_(0, rows, X_ST + 24, (1, 2))
                add("dve", lambda e: e.reciprocal(out=s12.ap, in_=s12.ap), r=[s12], w=[s12])
            ks.append(k3)

            def k4():
                k = stq["kv"]
                ckvp, krp = PS[k](0, rows, 0, (1, 256)), PS[k](0, rows, 256, (1, 32))
                r1, r2 = F_(0, rows, X_ST + 24, (1, 1)), F_(0, rows, X_ST + 25, (1, 1))
                ckvf = F_(0, rows, K_CKVF, (1, 256))
                add("dve", lambda e: e.scalar_tensor_tensor(out=ckvf.ap, in0=ckvp.ap, scalar=r1.ap, in1=CF(0, rows, C_KVG, (1, 256)).ap,
                                                            op0=ALU.mult, op1=ALU.mult),
                    r=[ckvp, r1, CF(0, rows, C_KVG, (1, 256))], w=[ckvf])
                dma(ockv.d(tok0 * 256, (256, rows), (1, 256)), ckvf)
                krf = F_(0, rows, K_KRF, (1, 32))
                add("dve", lambda e: e.tensor_scalar(out=krf.ap, in0=krp.ap, scalar1=r2.ap, scalar2=None, op0=ALU.mult),
                    r=[krp, r2], w=[krf])
            ks.append(k4)
            rope_chain(ks, K_KRF, 1, X_TABK, K_T1, K_T2, lambda: F_(0, rows, K_KROF, (32, 1), (1, 32)))

            def k7():
                dma(okr.d(tok0 * 32, (32, rows), (1, 32)), F_(0, rows, K_KROF, (1, 32)))
            ks.append(k7)
            ts = dict(f=F_, b=Bv, sq=K_SQ, st=X_ST + 32, ckvb=B_CKVB, ckvT=B_CKVT, kfull=B_KF)
            kvs = kv_steps(F_(0, rows, K_CKVF, (1, 256)), F_(0, rows, K_KROF, (1, 32)), kt0 + tt, rows, wkv + 2304, ts, alloc=kalloc)
            ks.extend(kvs)
            return pre_steps, qs, ks

        PAIR = 2
        for tp in range(0, NT, PAIR):
            chains = []
            for j_, tt in enumerate(range(tp, min(tp + PAIR, NT))):
                F_, Bv = SETS[j_]
                pre_steps, qs, ks = tile_chains(tt, F_, Bv, [4 * j_, 4 * j_ + 1], [4 * j_ + 2, 4 * j_ + 3])
                for p_ in pre_steps:
                    p_()
                chains.append(qs)
                chains.append(ks)
            if tp == 0:
                cuts = [5 if (ci % 2 == 0) else 9 for ci in range(len(chains))]
                run_chains([c[:cut] for c, cut in zip(chains, cuts)])
                fill_alloc = mk_alloc([1, 3, 5, 7])
                emit_IN_u(fill_alloc)
                wuq = ring_load([ld_full(s_wuq, 2304)])
                emit_IN_v(fill_alloc)
                emit_G(fill_alloc)
                run_chains([c[cut:] for c, cut in zip(chains, cuts)])
            else:
                run_chains(chains)
        rot_n[0] = 4
        if NT > 1 and PAIR > 1:
            dma(XB(0, rows, 0, (D, NT), (1, D)), xsrc.d(t0 * D, (D, rows), (rows * D, NT), (1, D)))

        G = 512 // NQ
        if is_s:
            ktl = [(kt, 128 if kt < 16 else 64, 0) for kt in range(17)]
        else:
            ktl = [(kt, 128, 0 if kt < kt0 else (kt - kt0) * 128) for kt in range(kt0 + 4)]
        groups = [ktl[i:i + G] for i in range(0, len(ktl), G)]
        if is_s:
            groups = [ktl[0:8], ktl[8:16], ktl[16:17]]
        ybz = RK(64, 64, O_YB, (1, 8 * TB))
        add("pool", lambda e: e.memset(ybz.ap, 0.0), w=[ybz])
        units = [(h, gi) for h in range(8) for gi in range(len(groups))]
        NU = len(units)
        LOOK = 2
        ubank = {}
        deferred = []

        def emit_S(u):
            h, gi = units[u]
            grp = groups[gi]
            k = rot()
            ubank[u] = k
            rk = grp[0][1]
            c0 = grp[0][2]

            def mms(e, grp=grp, k=k, h=h):
                ins = None
                for s_, (kt, rk_, c0_) in enumerate(grp):
                    ins = e.matmul(PS[k](0, rk_, s_ * NQ + c0_, (1, NQ - c0_)).ap,
                                   lhsT=KT(0, 96, h * SEQ + kt * 128, (1, rk_)).ap,
                                   rhs=QTv(0, 96, h * TB + c0_, (1, NQ - c0_)).ap, start=True, stop=True)
                return ins
            kt_lo, kt_hi = grp[0][0], grp[-1][0]
            add("pe", mms, r=[KT(0, 96, h * SEQ + kt_lo * 128, (1, (kt_hi - kt_lo + 1) * 128)), QTv(0, 96, h * TB, (1, NQ))],
                w=[PS[k](0, 128, 0, (1, 512))])
            pb = A_PT[u % 4]
            ncol = len(grp) * NQ - c0
            pt = RIb(0, rk, pb + c0, (1, ncol))
            add("act", lambda e, pt=pt, k=k, rk=rk, c0=c0, ncol=ncol: e.activation(
                out=pt.ap, in_=PS[k](0, rk, c0, (1, ncol)).ap, func=AF.Exp, scale=SCALE),
                r=[PS[k](0, 128, 0, (1, 512))], w=[pt])
            if (not is_s) and grp[0][0] >= kt0:
                mz = RIb(64, 64, pb + c0, (1, 64))
                add("pool", lambda e, mz=mz: e.memset(mz.ap, 0.0), w=[mz])

        def emit_PV(u):
            h, gi = units[u]
            grp = groups[gi]
            acc = 4 + (h % 2)
            pb = A_PT[u % 4]
            first = (gi == 0)
            last = (gi == len(groups) - 1)
            kt_lo, kt_hi = grp[0][0], grp[-1][0]

            def mmv(e, grp=grp, h=h, pb=pb, acc=acc, first=first, last=last):
                ins = None
                for s_, (kt, rk_, c0_) in enumerate(grp):
                    ins = e.matmul(PS[acc](0, 65, c0_, (1, NQ - c0_)).ap,
                                   lhsT=VS(0, rk_, kt * 520 + h * 65, (1, 65)).ap,
                                   rhs=RIb(0, rk_, pb + s_ * NQ + c0_, (1, NQ - c0_)).ap,
                                   start=(first and s_ == 0), stop=(last and s_ == len(grp) - 1))
                return ins
            add("pe", mmv, r=[VS(0, 128, kt_lo * 520, (1, (kt_hi - kt_lo + 1) * 520)), RIb(0, 128, pb, (1, 512))],
                w=[PS[acc](0, 65, 0, (1, NQ))])
            if last:
                ob = A_OSB + (h % 2) * 1024
                osb = RI(0, 65, ob, (1, NQ))
                add("act", lambda e, osb=osb, acc=acc: e.activation(out=osb.ap, in_=PS[acc](0, 65, 0, (1, NQ)).ap, func=AF.Copy),
                    r=[PS[acc](0, 65, 0, (1, NQ))], w=[osb])
                rc = RI(64, 1, ob + 512, (1, NQ))
                add("dve", lambda e, rc=rc, ob=ob: e.reciprocal(out=rc.ap, in_=RI(64, 1, ob, (1, NQ)).ap), r=[osb], w=[rc])

                def fin(h=h, ob=ob, osb=osb, rc=rc):
                    add("pe", lambda e: e.matmul(PS[6](0, 64, 0, (1, NQ)).ap, lhsT=CF(64, 1, C_ONE, (1, 64)).ap, rhs=rc.ap, start=True, stop=True),
                        r=[rc, CF(64, 1, C_ONE, (1, 64))], w=[PS[6](0, 64, 0, (1, NQ))])
                    yb = RK(0, 64, O_YB + h * TB, (1, NQ))
                    add("dve", lambda e: e.tensor_tensor(out=yb.ap, in0=RI(0, 64, ob, (1, NQ)).ap, in1=PS[6](0, 64, 0, (1, NQ)).ap, op=ALU.mult),
                        r=[osb, PS[6](0, 64, 0, (1, NQ))], w=[yb])
                deferred.append((u + min(6, 2 * len(groups) - 2), fin))

        for u in range(min(LOOK, NU)):
            emit_S(u)
        for u in range(NU):
            if u + LOOK < NU:
                emit_S(u + LOOK)
            emit_PV(u)
            while deferred and deferred[0][0] <= u:
                deferred.pop(0)[1]()
        while deferred:
            deferred.pop(0)[1]()

        woa = ring_load([ld_full(s_wout_a[b], 4096)])
        wob0 = ring_load([ld_full(s_wout_b[b], 4096, np_=64, src_off=0),
                          lambda base: (RING(64, 64, base, (1, 4096)), s_wout_b[b].d(0, (8192, 64), (1, 4096)))])
        wob1 = ring_load([ld_full(s_wout_b[b], 4096, np_=64, src_off=4096),
                          lambda base: (RING(64, 64, base, (1, 4096)), s_wout_b[b].d(4096, (8192, 64), (1, 4096)))])
        for tt in range(NT):
            for half in range(2):
                k = rot()

                def mmo(e, tt=tt, half=half, k=k):
                    ins = None
                    for g in range(4):
                        ins = e.matmul(PS[k](0, rows, 0, (1, 512)).ap, lhsT=RK(0, 128, O_YA + g * TB + tt * rows, (1, rows)).ap,
                                       rhs=RING(0, 128, woa + g * 1024 + half * 512, (1, 512)).ap, start=(g == 0), stop=False)
                    for h in range(8):
                        wb = wob0 if h < 4 else wob1
                        ins = e.matmul(PS[k](0, rows, 0, (1, 512)).ap, lhsT=RK(0, 128, O_YB + h * TB + tt * rows, (1, rows)).ap,
                                       rhs=RING(0, 128, wb + (h % 4) * 1024 + half * 512, (1, 512)).ap, start=False, stop=(h == 7))
                    return ins
                add("pe", mmo, r=[RK(0, 128, O_YA, (1, 6144)), RING(0, 128, woa, (1, 4096)), RING(0, 128, wob0, (1, 4096)), RING(0, 128, wob1, (1, 4096))],
                    w=[PS[k](0, rows, 0, (1, 512))])
                xr = XB(0, rows, tt * D + half * 512, (1, 512))
                add("dve", lambda e, xr=xr, k=k: e.tensor_tensor(out=xr.ap, in0=PS[k](0, rows, 0, (1, 512)).ap, in1=xr.ap, op=ALU.add),
                    r=[PS[k](0, rows, 0, (1, 512)), xr], w=[xr])
            norm_stats(tt, rows)
        norm_T(1, b, rows, NT, NQ, skip_stats=True)
        passes = [(0, 1)] if NT == 1 else [(0, 2), (2, 4)]
        def do_prefetch():
            PRE["wu"] = ring_load([ld_full(s_win_u, 4096)])
            PRE["wv"] = ring_load([ld_full(s_win_v, 4096)])
            PRE["wq"] = ring_load([ld_full(s_win_q, 3072)])
            PRE["wkv"] = ring_load([ld_full(s_win_kv, 2304), lambda base: (RING(0, 128, base + 2304, (1, 2048)), s_wukv.d(0, (2048, 128), (1, 2048)))])
        pending_out = [None]
        nxt_tok0 = 0 if is_s else (blk + 1) * TB
        do_early = is_s or blk + 1 < SEQ // TB
        for pi, (ta, tb_) in enumerate(passes):
            for j in range(NJ):
                if pi == 0 and do_early:
                    if j % 5 == 0 and j // 5 < 4:
                        early_load(nxt_tok0 + (j // 5) * 128)
                    if j % 5 == 2 and j // 5 < 4:
                        early_stats(j // 5)
                accb = 4
                if pi == 0:
                    pieces = [lambda base, j=j: (RING(0, 128, base, (1, 2048)), s_ffn_in.d(j * 128 * 2048, (2048, 128), (1, 2048))),
                              lambda base, j=j: (RING(0, 128, base + 2048, (1, 1024)), s_ffn_out[b].d(j * 128 * 1024, (1024, 128), (1, 1024)))]
                    wf = ring_load(pieces)
                    WT, wo = RING, wf + 2048
                else:
                    if j == 6 and do_early:
                        do_prefetch()
                    wf = None
                    mslot = (2048, 0, 4096)[(j // 2) % 3]
                    if j % 2 == 0:
                        dma(RIb(0, 128, mslot, (1024, 2), (1, 1024)),
                            s_ffn_out[b].d(j * 128 * 1024, (1024, 128), (128 * 1024, 2), (1, 1024)))
                    WT, wo = RIb, mslot + (j % 2) * 1024
                at = RK(0, 128, j * TB, (1, NQ))
                if pi == 0:
                    kg, ku = rot(), rot()

                    def mmf(e, wf=wf, kg=kg, ku=ku):
                        ins = None
                        for kc in range(8):
                            e.matmul(PS[kg](0, 128, 0, (1, NQ)).ap, lhsT=RING(0, 128, wf + kc * 256, (1, 128)).ap,
                                     rhs=HT(0, 128, kc * TB, (1, NQ)).ap, start=(kc == 0), stop=(kc == 7))
                        for kc in range(8):
                            ins = e.matmul(PS[ku](0, 128, 0, (1, NQ)).ap, lhsT=RING(0, 128, wf + kc * 256 + 128, (1, 128)).ap,
                                           rhs=HT(0, 128, kc * TB, (1, NQ)).ap, start=(kc == 0), stop=(kc == 7))
                        return ins
                    add("pe", mmf, r=[RING(0, 128, wf, (1, 2048)), HT(0, 128, 0, (1, 8 * TB))],
                        w=[PS[kg](0, 128, 0, (1, NQ)), PS[ku](0, 128, 0, (1, NQ))])
                    sg = RI(0, 128, (j % 2) * 512, (1, NQ))
                    add("act", lambda e, sg=sg, kg=kg: e.activation(out=sg.ap, in_=PS[kg](0, 128, 0, (1, NQ)).ap, func=AF.Silu),
                        r=[PS[kg](0, 128, 0, (1, NQ))], w=[sg])
                    add("dve", lambda e, at=at, sg=sg, ku=ku: e.tensor_tensor(out=at.ap, in0=PS[ku](0, 128, 0, (1, NQ)).ap, in1=sg.ap, op=ALU.mult),
                        r=[PS[ku](0, 128, 0, (1, NQ)), sg], w=[at])

                def mmfo(e, wf=wf, j=j, ta=ta, tb_=tb_, accb=accb, WT=WT, wo=wo):
                    ins = None
                    for tt in range(ta, tb_):
                        for half in range(2):
                            kk = accb + (tt - ta) * 2 + half
                            ins = e.matmul(PS[kk](0, rows, 0, (1, 512)).ap, lhsT=RK(0, 128, j * TB + tt * rows, (1, rows)).ap,
                                           rhs=WT(0, 128, wo + half * 512, (1, 512)).ap, start=(j == 0), stop=(j == NJ - 1))
                    return ins

                def emit_out(mmfo=mmfo, at=at, wf=wf, ta=ta, tb_=tb_, accb=accb, WT=WT, wo=wo):
                    add("pe", mmfo, r=[at, WT(0, 128, wo, (1, 1024))],
                        w=[PS[accb + (tt - ta) * 2 + half](0, rows, 0, (1, 512)) for tt in range(ta, tb_) for half in range(2)])
                if pi == 0:
                    if pending_out[0] is not None:
                        pending_out[0]()
                    pending_out[0] = emit_out
                else:
                    emit_out()
            if pending_out[0] is not None:
                pending_out[0]()
                pending_out[0] = None
            if pi == 0 and do_early:
                norm_T(0, 0, 128, 4, TB, skip_stats=True, early=True)
                early_state["done"] = True
            if pi == len(passes) - 1 and do_early and len(passes) == 1:
                do_prefetch()
            for tt in range(ta, tb_):
                for half in range(2):
                    kk = accb + (tt - ta) * 2 + half
                    xr = XB(0, rows, tt * D + half * 512, (1, 512))
                    add("dve", lambda e, xr=xr, kk=kk: e.tensor_tensor(out=xr.ap, in0=PS[kk](0, rows, 0, (1, 512)).ap, in1=xr.ap, op=ALU.add),
                        r=[PS[kk](0, rows, 0, (1, 512)), xr], w=[xr])
                dma(ydst.d((t0 + tt * rows) * D, (D, rows), (1, D)), XB(0, rows, tt * D, (1, D)))

    QTv = RK

    process_block(True, 0)
    for blk in range(SEQ // TB):
        process_block(False, blk)

    print("sbuf bytes remaining", nc.sbuf_bytes_remaining, "ops", len(S.ops), {k: len(v) for k, v in S.eng_ops.items()})
    return nc, S, None


def _finish(nc, S):
    from contextlib import ExitStack
    with ExitStack() as st:
        esem = {n: st.enter_context(nc.semaphore(f"sem_{n}")) for n in ("pe", "act", "dve", "pool")}
        dsems = [st.enter_context(nc.semaphore(f"dsem{k}")) for k in range(NDSEM)]
        block = st.enter_context(nc.Block())
        S.emit(nc, block, esem, dsems)
    return nc


_CACHE = {}


def _rope_table(pos):
    half = 16
    inv = (np.float32(10000.0) ** (-np.arange(half, dtype=np.float32) / np.float32(half))).astype(np.float32)
    ang = pos.astype(np.float32)[:, None] * inv[None, :]
    cos, sin = np.cos(ang).astype(np.float32), np.sin(ang).astype(np.float32)
    return np.ascontiguousarray(np.concatenate([cos, cos, -sin, sin], axis=1).astype(np.float32))


def kernel(x_prompt, x_sample, cache_ckv, cache_krope, c_prompt, c_sample, w_ada, b_ada, norm1_g, w_in, w_s, b_s,
           q_norm_g, w_uq, kv_norm_g, w_ukv, qn_g, qr_g, kn_g, kr_g, w_out, norm2_g, w_ffn_in, w_ffn_out):
    f = lambda a: np.ascontiguousarray(np.asarray(a, dtype=np.float32))
    if "nc" not in _CACHE:
        nc, S, _ = build_program()
        _finish(nc, S)
        _CACHE["nc"] = nc
    nc = _CACHE["nc"]
    shared = {
        "w_ada": f(w_ada)[0], "b_ada": f(b_ada), "norm1_g": f(norm1_g), "norm2_g": f(norm2_g), "w_in": f(w_in)[0],
        "w_s": f(w_s)[0].reshape(512, 128), "b_s": f(b_s).reshape(1, 512), "q_norm_g": f(q_norm_g), "w_uq": f(w_uq)[0],
        "kv_norm_g": f(kv_norm_g), "w_ukv": f(w_ukv)[0], "qn_g": f(qn_g), "qr_g": f(qr_g), "kn_g": f(kn_g), "kr_g": f(kr_g),
        "w_out": f(w_out)[0], "w_ffn_in": f(w_ffn_in)[0], "w_ffn_out": f(w_ffn_out)[0],
        "rope_p": _rope_table(np.arange(SEQ)), "rope_s": _rope_table(PAST + np.arange(DEC)),
    }
    xp, xs, cc, ck = f(x_prompt), f(x_sample), f(cache_ckv), f(cache_krope)
    cp, cs = f(c_prompt), f(c_sample)
    in_maps = []
    for b in range(8):
        m = dict(shared)
        m["x_p"] = xp[b]
        m["x_s"] = xs[b]
        m["cckv"] = cc[0, b]
        m["ckr"] = ck[0, b]
        m["c2"] = np.ascontiguousarray(np.stack([cp[b], cs[b]], axis=0))
        in_maps.append(m)
    res = run_bass_kernel_spmd(nc, in_maps, core_ids=list(range(8)))
    rr = res.results
    yp = np.stack([rr[b]["y_p"] for b in range(8)], axis=0)
    ys = np.stack([rr[b]["y_s"] for b in range(8)], axis=0)
    ckv_p = np.stack([rr[b]["o_ckv_p"] for b in range(8)], axis=0)[None]
    kr_p = np.stack([rr[b]["o_kr_p"] for b in range(8)], axis=0)[None]
    ckv_s = np.stack([rr[b]["o_ckv_s"] for b in range(8)], axis=0)[None]
    kr_s = np.stack([rr[b]["o_kr_s"] for b in range(8)], axis=0)[None]
    v_s = np.stack([rr[b]["o_v_s"] for b in range(8)], axis=0)[None]
    return (yp.astype(np.float32), ys.astype(np.float32), ckv_p.astype(np.float32), kr_p.astype(np.float32),
            ckv_s.astype(np.float32), kr_s.astype(np.float32), v_s.astype(np.float32))
```

```python
import numpy as np
import concourse.bass as bass
import concourse.mybir as mybir
from concourse.bass_utils import run_bass_kernel_spmd

F32 = mybir.dt.float32
BF16 = mybir.dt.bfloat16
AF = mybir.ActivationFunctionType
ALU = mybir.AluOpType
AX = mybir.AxisListType

D = 1024
SEQ = 4096
DEC = 64
PAST = 2048
TB = 512
DFF = 2816
NJ = 22
EPS = 1e-6
SCALE = 96 ** -0.5
SLOT = 4352
NRING = 4
NDSEM = 8


def _dsize(dt):
    return 4 if dt == F32 else 2


class R:
    __slots__ = ("ap", "root", "lo", "hi")

    def __init__(self, ap, root, lo, hi):
        self.ap, self.root, self.lo, self.hi = ap, root, lo, hi


class T:
    def __init__(self, h, root, base_bytes, dtype, F, dram=False, whole=False):
        self.h, self.root, self.base, self.dt, self.F, self.dram, self.whole = h, root, base_bytes, dtype, F, dram, whole

    def __call__(self, p0, n_p, off, *dims):
        if not dims:
            raise ValueError("need dims")
        ap = bass.AP(self.h, p0 * self.F + off, [[self.F, n_p]] + [[s, c] for (s, c) in dims])
        ext = sum(s * (c - 1) for (s, c) in dims) + 1
        sz = _dsize(self.dt)
        if self.whole:
            return R(ap, self.root, 0, 1 << 30)
        return R(ap, self.root, self.base + off * sz, self.base + (off + ext) * sz)

    def d(self, off, *dims):
        ap = bass.AP(self.h, off, [[s, c] for (s, c) in dims])
        ext = sum(s * (c - 1) for (s, c) in dims) + 1
        sz = _dsize(self.dt)
        return R(ap, self.root, off * sz, (off + ext) * sz)


class Op:
    __slots__ = ("eng", "fn", "dma", "deps", "id", "seq", "dsem", "dval", "dprev")


class Sched:
    ENG = ("pe", "act", "dve", "pool", "sp")

    def __init__(self):
        self.ops = []
        self.eng_ops = {e: [] for e in self.ENG}
        self.segs = {}
        self.pending = {e: {} for e in self.ENG}
        self.dma_all = []
        self.ndma = 0

    def _touch(self, op, acc, is_write):
        segs = self.segs.setdefault(acc.root, [])
        lo, hi = acc.lo, acc.hi
        out = []
        covered = []
        for s in segs:
            if s[1] <= lo or s[0] >= hi:
                out.append(s)
                continue
            if s[0] < lo:
                out.append([s[0], lo, s[2], list(s[3])])
            if s[1] > hi:
                out.append([hi, s[1], s[2], list(s[3])])
            mid = [max(s[0], lo), min(s[1], hi), s[2], list(s[3])]
            covered.append(mid)
        for m in covered:
            if m[2] is not None and m[2] != op.id:
                if is_write:
                    op.deps.setdefault(m[2], False)
                else:
                    op.deps[m[2]] = True
            if is_write:
                for rd in m[3]:
                    if rd != op.id:
                        op.deps.setdefault(rd, False)
        if is_write:
            out.append([lo, hi, op.id, []])
        else:
            covered.sort()
            cur = lo
            for m in covered:
                if m[0] > cur:
                    out.append([cur, m[0], None, [op.id]])
                m[3].append(op.id)
                out.append(m)
                cur = m[1]
            if cur < hi:
                out.append([cur, hi, None, [op.id]])
        out.sort(key=lambda s: s[0])
        self.segs[acc.root] = out

    def add(self, eng, fn, r=(), w=(), dma=False):
        op = Op()
        op.eng, op.fn, op.dma, op.deps, op.id = eng, fn, dma, {}, len(self.ops)
        op.seq = op.dsem = op.dval = op.dprev = None
        for k, v in self.pending[eng].items():
            op.deps[k] = v
        self.pending[eng] = {}
        for a in r:
            self._touch(op, a, False)
        for a in w:
            self._touch(op, a, True)
        self.ops.append(op)
        self.eng_ops[eng].append(op)
        if dma:
            op.dsem = self.ndma % NDSEM
            op.dval = 16 * (self.ndma // NDSEM + 1)
            self.ndma += 1
            self.dma_all.append(op.id)
        else:
            op.seq = len([1 for o in self.eng_ops[eng]])
        return op

    def barrier(self):
        deps = {}
        for e in self.ENG:
            if e != "sp" and self.eng_ops[e]:
                deps[self.eng_ops[e][-1].id] = True
        for i in self.dma_all:
            deps[i] = True
        self.dma_all = []
        for e in self.ENG:
            self.pending[e].update(deps)

    def emit(self, nc, block, esem, dsems):
        ops = self.ops

        def run(name, e):
            known = {}
            for op in self.eng_ops[name]:
                for did in sorted(op.deps):
                    dop = ops[did]
                    raw = op.deps[did]
                    if dop.dma:
                        sem, val = dsems[dop.dsem], dop.dval
                    else:
                        if dop.eng == name and not op.dma:
                            if name == "pe":
                                continue
                        sem, val = esem[dop.eng], dop.seq
                    if known.get(sem.num, 0) >= val:
                        continue
                    e.wait_ge(sem, val)
                    known[sem.num] = val
                if op.dma:
                    sem = dsems[op.dsem]
                    if op.dval > 16 and known.get(sem.num, 0) < op.dval - 16:
                        e.wait_ge(sem, op.dval - 16)
                        known[sem.num] = op.dval - 16
                    ins = op.fn(e)
                    ins.then_inc(sem, 16)
                else:
                    ins = op.fn(e)
                    ins.then_inc(esem[name], 1)

        @block.tensor
        def _(e):
            run("pe", e)

        @block.scalar
        def _(e):
            run("act", e)

        @block.vector
        def _(e):
            run("dve", e)

        @block.gpsimd
        def _(e):
            run("pool", e)

        @block.sync
        def _(e):
            run("sp", e)
            for k in range(min(NDSEM, self.ndma)):
                n_uses = (self.ndma - 1 - k) // NDSEM + 1
                e.wait_ge(dsems[k], 16 * n_uses)


def build_program():
    nc = bass.Bass("TRN2", target_bir_lowering=False)
    S = Sched()

    def dram(name, shape, kind, dt=F32):
        h = nc.dram_tensor(name, list(shape), dt, kind=kind)
        return T(h, name, 0, dt, shape[-1], dram=True)

    x_p = dram("x_p", [SEQ, D], "ExternalInput")
    x_s = dram("x_s", [DEC, D], "ExternalInput")
    cckv = dram("cckv", [PAST, 256], "ExternalInput")
    ckr = dram("ckr", [PAST, 32], "ExternalInput")
    c2 = dram("c2", [2, D], "ExternalInput")
    w_ada = dram("w_ada", [D, 6 * D], "ExternalInput")
    b_ada = dram("b_ada", [1, 6 * D], "ExternalInput")
    n1g = dram("norm1_g", [1, D], "ExternalInput")
    n2g = dram("norm2_g", [1, D], "ExternalInput")
    w_in = dram("w_in", [D, 1696], "ExternalInput")
    w_s = dram("w_s", [512, 128], "ExternalInput")
    b_s = dram("b_s", [1, 512], "ExternalInput")
    qng = dram("q_norm_g", [1, 384], "ExternalInput")
    w_uq = dram("w_uq", [384, 768], "ExternalInput")
    kvg = dram("kv_norm_g", [1, 256], "ExternalInput")
    w_ukv = dram("w_ukv", [256, 1024], "ExternalInput")
    qn_g = dram("qn_g", [1, 64], "ExternalInput")
    qr_g = dram("qr_g", [1, 32], "ExternalInput")
    kn_g = dram("kn_g", [1, 64], "ExternalInput")
    kr_g = dram("kr_g", [1, 32], "ExternalInput")
    w_out = dram("w_out", [D, D], "ExternalInput")
    w_fi = dram("w_ffn_in", [D, 2 * DFF], "ExternalInput")
    w_fo = dram("w_ffn_out", [DFF, D], "ExternalInput")
    rope_p = dram("rope_p", [SEQ, 64], "ExternalInput")
    rope_s = dram("rope_s", [DEC, 64], "ExternalInput")

    y_p = dram("y_p", [SEQ, D], "ExternalOutput")
    y_s = dram("y_s", [DEC, D], "ExternalOutput")
    o_ckv_p = dram("o_ckv_p", [SEQ, 256], "ExternalOutput")
    o_kr_p = dram("o_kr_p", [SEQ, 32], "ExternalOutput")
    o_ckv_s = dram("o_ckv_s", [DEC, 256], "ExternalOutput")
    o_kr_s = dram("o_kr_s", [DEC, 32], "ExternalOutput")
    o_v_s = dram("o_v_s", [DEC, 512], "ExternalOutput")

    s_win_u = dram("s_win_u", [128, 4096], "Internal", BF16)
    s_win_v = dram("s_win_v", [128, 4096], "Internal", BF16)
    s_win_q = dram("s_win_q", [128, 3072], "Internal", BF16)
    s_win_kv = dram("s_win_kv", [128, 2304], "Internal", BF16)
    s_wuq = dram("s_wuq", [128, 2304], "Internal", BF16)
    s_wukv = dram("s_wukv", [128, 2048], "Internal", BF16)
    s_wout_a = [dram(f"s_wout_a{v}", [128, 4096], "Internal", BF16) for v in range(2)]
    s_wout_b = [dram(f"s_wout_b{v}", [64, 8192], "Internal", BF16) for v in range(2)]
    s_ffn_in = dram("s_ffn_in", [NJ * 128, 2048], "Internal", BF16)
    s_ffn_out = [dram(f"s_ffn_out{v}", [NJ * 128, 1024], "Internal", BF16) for v in range(2)]

    def sb(name, F, dt):
        h = nc.alloc_sbuf_tensor(name, [128, F], dt)
        return T(h, name, 0, dt, F)

    def view(t, dt):
        h = t.h.bitcast(dt)
        F = t.F * _dsize(t.dt) // _dsize(dt)
        return T(h, t.root, t.base, dt, F)

    KT = sb("KT", 8 * SEQ, BF16)
    VS = sb("VS", 32 * 8 * 65, BF16)
    XB = sb("XB", 4 * D, F32)
    HT = sb("HT", 8 * TB, BF16)
    RK = sb("RK", NJ * TB, BF16)
    RI = sb("RI", 4352, F32)
    RIb = view(RI, BF16)
    RING = sb("RING", NRING * SLOT, BF16)
    CF = sb("CF", 1152, F32)
    CB = sb("CB", 1536, BF16)
    IDF = sb("IDF", 128, F32)
    KTf = view(KT, F32)
    RINGf = view(RING, F32)
    VSf = view(VS, F32)

    PS = []
    PSb = []
    for k in range(8):
        h = nc.alloc_psum_tensor(f"ps{k}", [128, 512], F32)
        PS.append(T(h, f"ps{k}", 0, F32, 512, whole=True))
        PSb.append(T(h.bitcast(BF16), f"ps{k}", 0, BF16, 1024, whole=True))

    rot_state = [0]
    rot_n = [4]

    def rot():
        k = rot_state[0] % rot_n[0]
        rot_state[0] = (k + 1) % rot_n[0]
        return k

    C_A = {(0, 0): 0, (0, 1): 8, (1, 0): 16, (1, 1): 24}
    C_MOD, C_N1G, C_N2G, C_QNG = 32, 96, 104, 112
    C_GQ, C_QRG, C_KRG, C_KVG, C_KNG, C_SC, C_SEL, C_ONE = 128, 192, 224, 256, 512, 576, 640, 896
    C_QRS, C_KRS, C_GQ96 = 1024, 1056, 1088
    B_ID, B_WS, B_BS, B_ONE = 0, 128, 640, 1152

    add = S.add

    def dma(out, in_, slow=False):
        if slow:
            add("sp", lambda e: e.dma_start(out=out.ap, in_=in_.ap, allow_slow_non_contiguous=True), r=[in_], w=[out], dma=True)
        else:
            add("sp", lambda e: e.dma_start(out=out.ap, in_=in_.ap), r=[in_], w=[out], dma=True)

    add("pool", lambda e: e.memset(IDF(0, 128, 0, (1, 128)).ap, 0.0), w=[IDF(0, 128, 0, (1, 128))])
    add("pool", lambda e: e.affine_select(out=IDF(0, 128, 0, (1, 128)).ap, in_=IDF(0, 128, 0, (1, 128)).ap,
                                          pattern=[[-1, 128]], compare_op=ALU.not_equal, fill=1.0, base=0,
                                          channel_multiplier=1),
        r=[IDF(0, 128, 0, (1, 128))], w=[IDF(0, 128, 0, (1, 128))])
    add("pool", lambda e: e.tensor_copy(out=CB(0, 128, B_ID, (1, 128)).ap, in_=IDF(0, 128, 0, (1, 128)).ap),
        r=[IDF(0, 128, 0, (1, 128))], w=[CB(0, 128, B_ID, (1, 128))])
    add("pool", lambda e: e.memset(CB(0, 1, B_ONE, (1, 128)).ap, 1.0), w=[CB(0, 1, B_ONE, (1, 128))])
    add("pool", lambda e: e.memset(CF(0, 128, C_ONE, (1, 64)).ap, 1.0), w=[CF(0, 128, C_ONE, (1, 64))])
    add("pool", lambda e: e.memset(CF(0, 2, C_SEL, (1, 256)).ap, 0.0), w=[CF(0, 2, C_SEL, (1, 256))])
    add("pool", lambda e: e.affine_select(out=CF(0, 2, C_SEL, (1, 128)).ap, in_=CF(0, 2, C_SEL, (1, 128)).ap,
                                          pattern=[[0, 128]], compare_op=ALU.not_equal, fill=1.0, base=0,
                                          channel_multiplier=1),
        r=[CF(0, 2, C_SEL, (1, 128))], w=[CF(0, 2, C_SEL, (1, 128))])
    add("pool", lambda e: e.affine_select(out=CF(0, 2, C_SEL + 128, (1, 128)).ap, in_=CF(0, 2, C_SEL + 128, (1, 128)).ap,
                                          pattern=[[0, 128]], compare_op=ALU.not_equal, fill=1.0, base=-1,
                                          channel_multiplier=1),
        r=[CF(0, 2, C_SEL + 128, (1, 128))], w=[CF(0, 2, C_SEL + 128, (1, 128))])

    ident = lambda n: CB(0, n, B_ID, (1, n))

    dma(CF(0, 128, C_N1G, (1, 8)), n1g.d(0, (1, 128), (128, 8)), slow=True)
    dma(CF(0, 128, C_N2G, (1, 8)), n2g.d(0, (1, 128), (128, 8)), slow=True)
    dma(CF(0, 128, C_QNG, (1, 3)), qng.d(0, (1, 128), (128, 3)), slow=True)
    dma(CF(0, 128, C_GQ, (1, 64)), qn_g.d(0, (0, 128), (1, 64)))
    dma(CF(0, 128, C_KNG, (1, 64)), kn_g.d(0, (0, 128), (1, 64)))
    dma(CF(0, 128, C_QRG, (1, 32)), qr_g.d(0, (0, 128), (1, 32)))
    dma(CF(0, 128, C_KRG, (1, 32)), kr_g.d(0, (0, 128), (1, 32)))
    dma(CF(0, 128, C_KVG, (1, 256)), kvg.d(0, (0, 128), (1, 256)))
    for hf_ in range(2):
        dma(CF(0, 128, C_QRS + hf_ * 16, (1, 16)), qr_g.d((1 - hf_) * 16, (0, 128), (1, 16)))
        dma(CF(0, 128, C_KRS + hf_ * 16, (1, 16)), kr_g.d((1 - hf_) * 16, (0, 128), (1, 16)))
    add("dve", lambda e: e.tensor_tensor(out=CF(0, 128, C_GQ, (1, 64)).ap, in0=CF(0, 128, C_GQ, (1, 64)).ap,
                                         in1=CF(0, 128, C_KNG, (1, 64)).ap, op=ALU.mult),
        r=[CF(0, 128, C_GQ, (1, 64)), CF(0, 128, C_KNG, (1, 64))], w=[CF(0, 128, C_GQ, (1, 64))])
    add("pool", lambda e: e.memset(CF(0, 128, C_GQ96, (1, 1)).ap, 1.0), w=[CF(0, 128, C_GQ96, (1, 1))])
    add("pool", lambda e: e.memset(CF(0, 128, C_GQ96 + 1, (1, 1)).ap, 0.0), w=[CF(0, 128, C_GQ96 + 1, (1, 1))])
    add("dve", lambda e: e.tensor_tensor(out=CF(0, 64, C_KNG, (1, 64)).ap, in0=CF(0, 64, C_GQ, (1, 64)).ap,
                                         in1=IDF(0, 64, 0, (1, 64)).ap, op=ALU.mult),
        r=[CF(0, 64, C_GQ, (1, 64)), IDF(0, 64, 0, (1, 64))], w=[CF(0, 64, C_KNG, (1, 64))])
    add("dve", lambda e: e.tensor_reduce(out=CF(0, 64, C_GQ96, (1, 1)).ap, in_=CF(0, 64, C_KNG, (1, 64)).ap, axis=AX.X, op=ALU.add),
        r=[CF(0, 64, C_KNG, (1, 64))], w=[CF(0, 64, C_GQ96, (1, 1))])

    dma(RI(0, 128, 0, (128, 4), (1, 128)), w_s.d(0, (128, 128), (128 * 128, 4), (1, 128)))
    add("dve", lambda e: e.tensor_copy(out=RIb(0, 128, 1024, (1, 512)).ap, in_=RI(0, 128, 0, (1, 512)).ap),
        r=[RI(0, 128, 0, (1, 512))], w=[RIb(0, 128, 1024, (1, 512))])
    for g in range(4):
        add("pe", lambda e, g=g: e.transpose(out=PSb[0](0, 128, g * 128, (1, 128)).ap,
                                             in_=RIb(0, 128, 1024 + g * 128, (1, 128)).ap, identity=ident(128).ap),
            r=[RIb(0, 128, 1024 + g * 128, (1, 128)), ident(128)], w=[PSb[0](0, 128, g * 128, (1, 128))])
    add("dve", lambda e: e.tensor_copy(out=CB(0, 128, B_WS, (1, 512)).ap, in_=PSb[0](0, 128, 0, (1, 512)).ap),
        r=[PSb[0](0, 128, 0, (1, 512))], w=[CB(0, 128, B_WS, (1, 512))])
    add("dve", lambda e: e.memset(CB(64, 64, B_WS, (128, 4), (1, 64)).ap, 0.0), w=[CB(64, 64, B_WS, (128, 4), (1, 64))])
    dma(RI(0, 1, 1024, (1, 512)), b_s.d(0, (512, 1), (1, 512)))
    add("dve", lambda e: e.tensor_copy(out=CB(0, 1, B_BS, (1, 512)).ap, in_=RI(0, 1, 1024, (1, 512)).ap),
        r=[RI(0, 1, 1024, (1, 512))], w=[CB(0, 1, B_BS, (1, 512))])

    prep_i = [0]

    def prep_bufs():
        i = prep_i[0]
        prep_i[0] += 1
        return (i % 2) * 5632, (i % 2) * 5632

    def act_copy(o, i):
        add("act", lambda e: e.activation(out=o.ap, in_=i.ap, func=AF.Copy), r=[i], w=[o])

    pieces = []

    for kc in range(8):

        def ld(fo, kc=kc):
            dma(KTf(0, 128, fo, (1, 1696)), w_in.d(kc * 128 * 1696, (1696, 128), (1, 1696)))

        def rest(fo, bo, kc=kc):
            act_copy(RK(0, 128, bo, (1, 1696)), KTf(0, 128, fo, (1, 1696)))
            dma(s_win_u.d(kc * 128, (4096, 128), (1024, 4), (1, 128)), RK(0, 128, bo, (128, 4), (1, 128)))
            dma(s_win_v.d(kc * 512, (4096, 128), (1, 512)), RK(0, 128, bo + 512, (1, 512)))
            dma(s_win_q.d(kc * 384, (3072, 128), (1, 384)), RK(0, 128, bo + 1024, (1, 384)))
            dma(s_win_kv.d(kc * 288, (2304, 128), (1, 288)), RK(0, 128, bo + 1408, (1, 288)))
        pieces.append((ld, rest, False))
    for kc in range(3):

        def ld(fo, kc=kc):
            dma(KTf(0, 128, fo, (1, 768)), w_uq.d(kc * 128 * 768, (768, 128), (1, 768)))

        def rest(fo, bo, kc=kc):
            sc = CF(0, 128, C_QNG + kc, (1, 1))
            for (io, n, oo) in ((0, 64, 0), (64, 32, 512)):
                i_ = KTf(0, 128, fo + io, (96, 8), (1, n))
                o_ = RK(0, 128, bo + oo, (n, 8), (1, n))
                add("dve", lambda e, i_=i_, o_=o_, sc=sc: e.tensor_scalar(out=o_.ap, in0=i_.ap, scalar1=sc.ap, scalar2=None, op0=ALU.mult),
                    r=[i_, sc], w=[o_])
            dma(s_wuq.d(kc * 768, (2304, 128), (1, 768)), RK(0, 128, bo, (1, 768)))
        pieces.append((ld, rest, False))
    for kc in range(2):

        def ld(fo, kc=kc):
            dma(KTf(0, 128, fo, (1, 1024)), w_ukv.d(kc * 128 * 1024, (1024, 128), (1, 1024)))

        def rest(fo, bo, kc=kc):
            for part in range(2):
                act_copy(RK(0, 128, bo + part * 512, (64, 8), (1, 64)), KTf(0, 128, fo + part * 64, (128, 8), (1, 64)))
            dma(s_wukv.d(kc * 1024, (2048, 128), (1, 1024)), RK(0, 128, bo, (1, 1024)))
        pieces.append((ld, rest, False))
    for kc in range(8):

        def ld(fo, kc=kc):
            dma(KTf(0, 128, fo, (1, 1024)), w_out.d(kc * 128 * 1024, (1024, 128), (1, 1024)))

        def rest(fo, bo, kc=kc):
            stg = KTf(0, 128, fo, (1, 1024))
            for v in range(2):
                ob = RK(0, 128, bo + v * 1024, (1, 1024))
                gb = GBC(v, 0, 0, 1024)
                add("dve", lambda e, stg=stg, ob=ob, gb=gb: e.tensor_tensor(out=ob.ap, in0=stg.ap, in1=gb.ap, op=ALU.mult),
                    r=[stg, gb], w=[ob])
                if kc < 4:
                    dma(s_wout_a[v].d(kc * 1024, (4096, 128), (1, 1024)), ob)
                else:
                    m = kc - 4
                    dma(s_wout_b[v].d((2 * m) * 1024, (8192, 64), (1, 1024)), RK(0, 64, bo + v * 1024, (1, 1024)))
                    dma(s_wout_b[v].d((2 * m + 1) * 1024, (8192, 64), (1, 1024)), RK(64, 64, bo + v * 1024, (1, 1024)))
        pieces.append((ld, rest, True))
    for j2 in range(NJ // 2):

        def ld(fo, j2=j2):
            dma(KTf(0, 128, fo, (1024, 2), (1, 1024)), w_fo.d(j2 * 2 * 128 * 1024, (1024, 128), (128 * 1024, 2), (1, 1024)))

        def rest(fo, bo, j2=j2):
            stg = KTf(0, 128, fo, (1024, 2), (1, 1024))
            for v in range(2):
                ob = RK(0, 128, bo + v * 2048, (1024, 2), (1, 1024))
                gb = R(bass.AP(XB.h, (v * 2 + 1) * 1024, [[XB.F, 128], [0, 2], [1, 1024]]), XB.root, (v * 2 + 1) * 4096, (v * 2 + 2) * 4096)
                add("dve", lambda e, stg=stg, ob=ob, gb=gb: e.tensor_tensor(out=ob.ap, in0=stg.ap, in1=gb.ap, op=ALU.mult),
                    r=[stg, gb], w=[ob])
                dma(s_ffn_out[v].d(j2 * 2 * 128 * 1024, (1024, 128), (128 * 1024, 2), (1, 1024)), ob)
        pieces.append((ld, rest, True))
    for kc in range(8):

        def ld(fo, kc=kc):
            dma(KTf(0, 128, fo, (1, 5632)), w_fi.d(kc * 128 * 5632, (5632, 128), (1, 5632)))

        def rest(fo, bo, kc=kc):
            for gu in range(2):
                act_copy(RK(0, 128, bo + gu * 128, (256, NJ), (1, 128)), KTf(0, 128, fo + gu * DFF, (128, NJ), (1, 128)))
            for jh in range(2):
                dma(s_ffn_in.d(jh * 11 * 128 * 2048 + kc * 256, (2048, 128), (128 * 2048, 11), (1, 256)),
                    RK(0, 128, bo + jh * 11 * 256, (256, 11), (1, 256)))
        pieces.append((ld, rest, False))


    order = [i for i, p_ in enumerate(pieces) if not p_[2]] + [i for i, p_ in enumerate(pieces) if p_[2]]
    N_NOGATE = len([1 for p_ in pieces if not p_[2]])
    prep_pos = [0]

    def pump(n, limit=None):
        limit = len(order) if limit is None else limit
        while n > 0 and prep_pos[0] < limit:
            k = prep_pos[0]
            if k == 0:
                pieces[order[0]][0]((0) * 5632)
            if k + 1 < len(order) and k + 1 < limit + 1:
                if k + 1 < len(order):
                    pieces[order[k + 1]][0](((k + 1) % 2) * 5632)
            pieces[order[k]][1]((k % 2) * 5632, (k % 2) * 5632)
            prep_pos[0] += 1
            n -= 1

    for b_ in range(2):
        dma(CF(0, 128, C_SC + b_, (2, 8)), c2.d(b_ * D, (1, 128), (128, 8)), slow=True)
    add("act", lambda e: e.activation(out=CF(0, 128, C_SC, (1, 16)).ap, in_=CF(0, 128, C_SC, (1, 16)).ap, func=AF.Silu),
        r=[CF(0, 128, C_SC, (1, 16))], w=[CF(0, 128, C_SC, (1, 16))])
    MSB = lambda off, n: VSf(0, 2, off, (1, n))
    for half in range(2):
        dma(RI(0, 2, 0, (1, 3072)), b_ada.d(half * 3072, (0, 2), (1, 3072)))
        for kc in range(8):
            st = (half * 8 + kc) % 2
            stg = RINGf(0, 128, st * 3072, (1, 3072))
            dma(stg, w_ada.d(kc * 128 * 6144 + half * 3072, (6144, 128), (1, 3072)))

            def mm(e, kc=kc, st=st):
                ins = None
                for cb in range(6):
                    ins = e.matmul(PS[cb](0, 2, 0, (1, 512)).ap, lhsT=CF(0, 128, C_SC + kc * 2, (1, 2)).ap,
                                   rhs=RINGf(0, 128, st * 3072 + cb * 512, (1, 512)).ap, start=(kc == 0), stop=(kc == 7))
                return ins
            add("pe", mm, r=[stg, CF(0, 128, C_SC, (1, 16))], w=[PS[cb](0, 2, 0, (1, 512)) for cb in range(6)])
            pump(1, N_NOGATE)
        for cb in range(6):
            col = half * 3072 + cb * 512
            add("dve", lambda e, cb=cb, col=col: e.tensor_tensor(out=MSB(col, 512).ap, in0=PS[cb](0, 2, 0, (1, 512)).ap,
                                                                 in1=RI(0, 2, cb * 512, (1, 512)).ap, op=ALU.add),
                r=[PS[cb](0, 2, 0, (1, 512)), RI(0, 2, cb * 512, (1, 512))], w=[MSB(col, 512)])
    WH = [0, 1, 3, 4]

    def tr_mod(e):
        ins = None
        for wi, wh in enumerate(WH):
            for kc in range(8):
                ins = e.transpose(out=PS[6](0, 128, (wi * 8 + kc) * 2, (1, 2)).ap,
                                  in_=MSB(wh * 1024 + kc * 128, 128).ap, identity=IDF(0, 2, 0, (1, 2)).ap)
        return ins
    add("pe", tr_mod, r=[MSB(0, 6144), IDF(0, 2, 0, (1, 2))], w=[PS[6](0, 128, 0, (1, 64))])
    add("dve", lambda e: e.tensor_copy(out=CF(0, 128, C_MOD, (1, 64)).ap, in_=PS[6](0, 128, 0, (1, 64)).ap),
        r=[PS[6](0, 128, 0, (1, 64))], w=[CF(0, 128, C_MOD, (1, 64))])
    for stage, (wi, goff) in enumerate([(1, C_N1G), (3, C_N2G)]):
        for b in range(2):
            src = CF(0, 128, C_MOD + wi * 16 + b, (2, 8))
            dst = CF(0, 128, C_A[(stage, b)], (1, 8))
            add("dve", lambda e, src=src, dst=dst, goff=goff: e.scalar_tensor_tensor(
                out=dst.ap, in0=src.ap, scalar=1.0, in1=CF(0, 128, goff, (1, 8)).ap, op0=ALU.add, op1=ALU.mult),
                r=[src, CF(0, 128, goff, (1, 8))], w=[dst])

    def a_col(stage, b, fc):
        return CF(0, 128, C_A[(stage, b)] + fc, (1, 1))

    def b_col(stage, b, fc):
        wi = 0 if stage == 0 else 2
        return CF(0, 128, C_MOD + wi * 16 + fc * 2 + b, (1, 1))

    def GBC(b, which, off, n):
        return XB(0, 128, (b * 2 + which) * 1024 + off, (1, n))
    for b in range(2):
        for which, wh in enumerate([2, 5]):
            for half in range(2):
                k = 4 + ((b * 4 + which * 2 + half) % 2)
                add("pe", lambda e, b=b, wh=wh, half=half, k=k: e.matmul(
                    PS[k](0, 128, 0, (1, 512)).ap, lhsT=CF(0, 2, C_SEL + b * 128, (1, 128)).ap,
                    rhs=MSB(wh * 1024 + half * 512, 512).ap, start=True, stop=True),
                    r=[CF(0, 2, C_SEL, (1, 256)), MSB(wh * 1024 + half * 512, 512)], w=[PS[k](0, 128, 0, (1, 512))])
                add("act", lambda e, b=b, which=which, half=half, k=k: e.activation(
                    out=GBC(b, which, half * 512, 512).ap, in_=PS[k](0, 128, 0, (1, 512)).ap, func=AF.Copy),
                    r=[PS[k](0, 128, 0, (1, 512))], w=[GBC(b, which, half * 512, 512)])

    pump(1000)
    S.barrier()
    add("pool", lambda e: e.memset(VS(0, 128, 64, (65, 256), (1, 1)).ap, 1.0), w=[VS(0, 128, 64, (65, 256), (1, 1))])

    ring_n = [0]

    def ring_load(pieces):
        k = ring_n[0] % NRING
        ring_n[0] += 1
        base = k * SLOT
        for fn in pieces:
            dst, src = fn(base)
            dma(dst, src)
        return base

    def ld_full(src_t, n, np_=128, src_off=0, row=None):
        row = row if row is not None else src_t.F
        return lambda base: (RING(0, np_, base, (1, n)), src_t.d(src_off, (row, np_), (1, n)))

    def norm_stats(tt, rows):
        if True:
            xin = XB(0, rows, tt * D, (1, D))
            junk = RK(0, rows, tt * D, (1, D))
            ss = RI(0, rows, 4300 + tt, (1, 1))
            rs = RI(0, rows, 4310 + tt, (1, 1))
            add("act", lambda e, xin=xin, junk=junk, ss=ss: e.activation(out=junk.ap, in_=xin.ap, func=AF.Square, accum_out=ss.ap),
                r=[xin], w=[junk, ss])
            add("act", lambda e, ss=ss, rs=rs: e.activation(out=rs.ap, in_=ss.ap, func=AF.Sqrt, scale=1.0 / D, bias=EPS_AP(rows).ap),
                r=[ss, EPS_AP(rows)], w=[rs])
            add("dve", lambda e, rs=rs: e.reciprocal(out=rs.ap, in_=rs.ap), r=[rs], w=[rs])
            add("dve", lambda e, xin=xin, junk=junk, rs=rs: e.tensor_scalar(out=junk.ap, in0=xin.ap, scalar1=rs.ap, scalar2=None, op0=ALU.mult),
                r=[xin, rs], w=[junk])

    def norm_T(stage, b, rows, NT, NQ, skip_stats=False, early=False):
        if not skip_stats:
            for tt in range(NT):
                norm_stats(tt, rows)
        XHT, xh0 = (RIb, 4096) if early else (RK, 0)
        for fcp in range(4):
            k = rot()

            def tr(e, fcp=fcp, k=k):
                ins = None
                for fcl in range(2):
                    fc = fcp * 2 + fcl
                    for tt in range(NT):
                        ins = e.transpose(out=PSb[k](0, 128, fcl * 512 + tt * rows, (1, rows)).ap,
                                          in_=XHT(0, rows, xh0 + tt * D + fc * 128, (1, 128)).ap, identity=ident(rows).ap)
                return ins
            add("pe", tr, r=[XHT(0, rows, xh0, (1, NT * D)), ident(rows)], w=[PSb[k](0, 128, 0, (1, 1024))])
            for fcl in range(2):
                fc = fcp * 2 + fcl
                o = HT(0, 128, fc * TB, (1, NQ))
                i = PSb[k](0, 128, fcl * 512, (1, NQ))
                add("act", lambda e, o=o, i=i, fc=fc: e.activation(out=o.ap, in_=i.ap, func=AF.Identity,
                                                                   scale=a_col(stage, b, fc).ap, bias=b_col(stage, b, fc).ap),
                    r=[i, a_col(stage, b, fc), b_col(stage, b, fc)], w=[o])

    XS = lambda: RI(0, 128, 1024, (1, D))

    def early_load(tok0):
        dma(XS(), x_p.d(tok0 * D, (D, 128), (1, D)))

    def early_stats(tt):
        xin = XS()
        junk = RIb(0, 128, 4096 + tt * D, (1, D))
        ss = RI(0, 128, 4320 + tt, (1, 1))
        rs = RI(0, 128, 4330 + tt, (1, 1))
        add("act", lambda e: e.activation(out=junk.ap, in_=xin.ap, func=AF.Square, accum_out=ss.ap), r=[xin], w=[junk, ss])
        add("act", lambda e: e.activation(out=rs.ap, in_=ss.ap, func=AF.Sqrt, scale=1.0 / D, bias=EPS_AP(128).ap),
            r=[ss, EPS_AP(128)], w=[rs])
        add("dve", lambda e: e.reciprocal(out=rs.ap, in_=rs.ap), r=[rs], w=[rs])
        add("dve", lambda e: e.tensor_scalar(out=junk.ap, in0=xin.ap, scalar1=rs.ap, scalar2=None, op0=ALU.mult),
            r=[xin, rs], w=[junk])

    def EPS_AP(rows):
        return CF(0, rows, 960, (1, 1))
    add("pool", lambda e: e.memset(CF(0, 128, 960, (1, 1)).ap, EPS), w=[CF(0, 128, 960, (1, 1))])

    O_UT, O_V, O_YA, O_YB = 0, 2048, 4096, 6144
    T_SQ, T_SQR, T_QN, T_QR, T_T1, T_T2, T_CKV, T_KR, T_KRO, T_ST = 0, 512, 768, 1280, 1536, 1792, 2048, 2560, 2624, 2688
    TB_QF, TB_KF, TB_CQN, TB_CQT, TB_CKVB, TB_CKVT = 5504, 6272, 7040, 7424, 7808, 8064
    T_ROPE = 4160
    A_PT = [0, 512, 1024, 1536]
    A_OSB, A_RC = 1024, 1536

    def rmsnorm_stats(src, rows, n_in, ss, rs, junk):
        add("act", lambda e: e.activation(out=junk.ap, in_=src.ap, func=AF.Square, accum_out=ss.ap), r=[src], w=[junk, ss])
        add("act", lambda e: e.activation(out=rs.ap, in_=ss.ap, func=AF.Sqrt, scale=1.0 / n_in, bias=EPS_AP(rows).ap),
            r=[ss, EPS_AP(rows)], w=[rs])
        add("dve", lambda e: e.reciprocal(out=rs.ap, in_=rs.ap), r=[rs], w=[rs])

    def rope_ops(x, out, rows, nh, tab):
        xo, oo = x, out
        t1 = RI(0, rows, T_T1, (32, nh), (1, 32))
        add("pool", lambda e: e.tensor_tensor(out=t1.ap, in0=RI(0, rows, xo, (32, nh), (1, 32)).ap,
                                              in1=RI(0, rows, tab, (0, nh), (1, 32)).ap, op=ALU.mult),
            r=[RI(0, rows, xo, (32, nh), (1, 32)), RI(0, rows, tab, (1, 32))], w=[t1])
        for hf in range(2):
            t2 = RI(0, rows, T_T2 + hf * 16, (32, nh), (1, 16))
            xi = RI(0, rows, xo + (1 - hf) * 16, (32, nh), (1, 16))
            sn = RI(0, rows, tab + 32 + hf * 16, (0, nh), (1, 16))
            add("pool", lambda e, t2=t2, xi=xi, sn=sn: e.tensor_tensor(out=t2.ap, in0=xi.ap, in1=sn.ap, op=ALU.mult),
                r=[xi, RI(0, rows, tab + 32, (1, 32))], w=[t2])
        t2a = RI(0, rows, T_T2, (32, nh), (1, 32))
        add("pool", lambda e: e.tensor_tensor(out=oo.ap, in0=t1.ap, in1=t2a.ap, op=ALU.add), r=[t1, t2a], w=[oo])

    RKf = view(RK, F32)

    def ts_default():
        return dict(f=RI, b=RIb, sq=T_SQ, st=T_ST + 16, ckvb=TB_CKVB, ckvT=TB_CKVT, kfull=TB_KF)

    def ts_rk(i):
        B = i * 3072
        return dict(f=RKf, b=RK, sq=B // 2, cf=B // 2 + 512, kf=B // 2 + 768, st=B // 2 + 800,
                    ckvb=B + 1664, ckvT=B + 1920, kfull=B + 2176)

    def kv_steps(ckvf, kropef, kt, rows, w_kv_base, ts, pre=None, alloc=None):
        alloc = alloc or rot
        TF, TBb = ts["f"], ts["b"]
        o_sq, o_st, o_cb, o_ct, o_kf = ts["sq"], ts["st"], ts["ckvb"], ts["ckvT"], ts["kfull"]
        st_ = {}
        steps = []
        if pre is not None:
            steps.append(pre)

        def s1():
            ckvb = TBb(0, rows, o_cb, (1, 256))
            add("pool", lambda e: e.tensor_copy(out=ckvb.ap, in_=ckvf.ap), r=[ckvf], w=[ckvb])
        steps.append(s1)

        def s2():
            k = alloc()
            st_["k"] = k

            def tr(e):
                ins = None
                for kc in range(2):
                    ins = e.transpose(out=PSb[k](0, 128, kc * 128, (1, rows)).ap,
                                      in_=TBb(0, rows, o_cb + kc * 128, (1, 128)).ap, identity=ident(rows).ap)
                return ins
            add("pe", tr, r=[TBb(0, rows, o_cb, (1, 256)), ident(rows)], w=[PSb[k](0, 128, 0, (1, 256))])
        steps.append(s2)

        def s3():
            k = st_["k"]
            ckvT = TBb(0, 128, o_ct, (128, 2), (1, rows))
            add("dve", lambda e: e.tensor_copy(out=ckvT.ap, in_=PSb[k](0, 128, 0, (128, 2), (1, rows)).ap),
                r=[PSb[k](0, 128, 0, (1, 256))], w=[ckvT])
        steps.append(s3)

        def s4():
            ka, kb = alloc(), alloc()
            st_["ka"], st_["kb"] = ka, kb

            def mm(e):
                ins = None
                for kc in range(2):
                    for part, kk in ((0, ka), (1, kb)):
                        ins = e.matmul(PS[kk](0, rows, 0, (1, 512)).ap, lhsT=TBb(0, 128, o_ct + kc * 128, (1, rows)).ap,
                                       rhs=RING(0, 128, w_kv_base + kc * 1024 + part * 512, (1, 512)).ap,
                                       start=(kc == 0), stop=(kc == 1))
                return ins
            add("pe", mm, r=[TBb(0, 128, o_ct, (1, 256)), RING(0, 128, w_kv_base, (1, 2048))],
                w=[PS[ka](0, rows, 0, (1, 512)), PS[kb](0, rows, 0, (1, 512))])
        steps.append(s4)

        def s5():
            ka, kb = st_["ka"], st_["kb"]
            vdst = VS(0, rows, kt * 520, (65, 8), (1, 64))
            sq = TF(0, rows, o_sq, (1, 512))
            add("act", lambda e: e.activation(out=sq.ap, in_=PS[ka](0, rows, 0, (1, 512)).ap, func=AF.Square),
                r=[PS[ka](0, rows, 0, (1, 512))], w=[sq])
            add("act", lambda e: e.activation(out=vdst.ap, in_=PS[kb](0, rows, 0, (64, 8), (1, 64)).ap, func=AF.Copy),
                r=[PS[kb](0, rows, 0, (1, 512))], w=[vdst])
        steps.append(s5)

        def s6():
            st = TF(0, rows, o_st, (1, 8))
            add("dve", lambda e: e.tensor_reduce(out=st.ap, in_=TF(0, rows, o_sq, (64, 8), (1, 64)).ap, axis=AX.X, op=ALU.add),
                r=[TF(0, rows, o_sq, (1, 512))], w=[st])
        steps.append(s6)

        def s7():
            st = TF(0, rows, o_st, (1, 8))
            add("act", lambda e: e.activation(out=st.ap, in_=st.ap, func=AF.Sqrt, scale=1.0 / 64, bias=EPS_AP(rows).ap),
                r=[st, EPS_AP(rows)], w=[st])
        steps.append(s7)

        def s8():
            st = TF(0, rows, o_st, (1, 8))
            add("dve", lambda e: e.reciprocal(out=st.ap, in_=st.ap), r=[st], w=[st])
        steps.append(s8)

        def s9():
            ka = st_["ka"]
            st = TF(0, rows, o_st, (1, 8))
            kf_n = TBb(0, rows, o_kf, (96, 8), (1, 64))
            add("dve", lambda e: e.tensor_tensor(out=kf_n.ap, in0=PS[ka](0, rows, 0, (64, 8), (1, 64)).ap,
                                                 in1=TF(0, rows, o_st, (1, 8), (0, 64)).ap, op=ALU.mult),
                r=[PS[ka](0, rows, 0, (1, 512)), st], w=[kf_n])
            kf_r = TBb(0, rows, o_kf + 64, (96, 8), (1, 32))
            krb = R(bass.AP(kropef.ap.tensor, kropef.ap.offset, [list(kropef.ap.ap[0]), [0, 8], [1, 32]]), kropef.root, kropef.lo, kropef.hi)
            add("pool", lambda e: e.tensor_copy(out=kf_r.ap, in_=krb.ap), r=[kropef], w=[kf_r])
        steps.append(s9)

        def s10():
            kfull = TBb(0, rows, o_kf, (1, 768))
            for hh in range(2):
                k2 = alloc()
                st_["k2", hh] = k2

                def trk(e, hh=hh, k2=k2):
                    ins = None
                    for hl in range(4):
                        h = hh * 4 + hl
                        ins = e.transpose(out=PSb[k2](0, 96, hl * 128, (1, rows)).ap,
                                          in_=TBb(0, rows, o_kf + h * 96, (1, 96)).ap, identity=ident(rows).ap)
                    return ins
                add("pe", trk, r=[kfull, ident(rows)], w=[PSb[k2](0, 96, 0, (1, 512))])
        steps.append(s10)

        def s11():
            for hh in range(2):
                k2 = st_["k2", hh]
                kdst = KT(0, 96, (hh * 4) * SEQ + kt * 128, (SEQ, 4), (1, rows))
                add("act" if hh == 0 else "dve",
                    (lambda e, kdst=kdst, k2=k2: e.activation(out=kdst.ap, in_=PSb[k2](0, 96, 0, (128, 4), (1, rows)).ap, func=AF.Copy)) if hh == 0 else
                    (lambda e, kdst=kdst, k2=k2: e.tensor_copy(out=kdst.ap, in_=PSb[k2](0, 96, 0, (128, 4), (1, rows)).ap)),
                    r=[PSb[k2](0, 96, 0, (1, 512))], w=[kdst])
        steps.append(s11)
        return steps

    def mk_alloc(banks):
        st = [0]

        def alloc():
            k = banks[st[0] % len(banks)]
            st[0] += 1
            return k
        return alloc

    def run_chains(chains, lo=0, hi=None):
        n = max(len(c) for c in chains)
        hi = n if hi is None else min(hi, n)
        for si in range(lo, hi):
            for c in chains:
                if si < len(c):
                    c[si]()

    def kv_common(ckvf, kropef, kt, rows, w_kv_base):
        run_chains([kv_steps(ckvf, kropef, kt, rows, w_kv_base, ts_default())])

    early_state = {"done": False}
    PRE = {}

    def process_block(is_s, blk):
        b = 1 if is_s else 0
        rows = 64 if is_s else 128
        NT = 1 if is_s else 4
        NQ = rows * NT
        xsrc = x_s if is_s else x_p
        ydst = y_s if is_s else y_p
        rsrc = rope_s if is_s else rope_p
        ockv = o_ckv_s if is_s else o_ckv_p
        okr = o_kr_s if is_s else o_kr_p
        t0 = 0 if is_s else blk * TB
        kt0 = 16 if is_s else blk * 4

        dma(XB(0, rows, 0, (D, NT), (1, D)), xsrc.d(t0 * D, (D, rows), (rows * D, NT), (1, D)))
        if is_s:
            wkv0 = ring_load([ld_full(s_win_kv, 2304), lambda base: (RING(0, 128, base + 2304, (1, 2048)), s_wukv.d(0, (2048, 128), (1, 2048)))])
            rot_n[0] = 4
            for g0 in range(0, 16, 4):
                chains = []
                for gi_ in range(4):
                    kt = g0 + gi_
                    if gi_ == 0:
                        ts = ts_default()
                        cf = RI(0, 128, T_CKV, (1, 256))
                        kf = RI(0, 128, T_KRO, (1, 32))
                    else:
                        ts = ts_rk(gi_ - 1)
                        cf = RKf(0, 128, ts["cf"], (1, 256))
                        kf = RKf(0, 128, ts["kf"], (1, 32))

                    def pre(cf=cf, kf=kf, kt=kt):
                        dma(cf, cckv.d(kt * 128 * 256, (256, 128), (1, 256)))
                        dma(kf, ckr.d(kt * 128 * 32, (32, 128), (1, 32)))
                    chains.append(kv_steps(cf, kf, kt, 128, wkv0 + 2304, ts, pre=pre, alloc=mk_alloc([2 * gi_, 2 * gi_ + 1])))
                run_chains(chains)
            rot_n[0] = 4
        if early_state["done"]:
            early_state["done"] = False
        else:
            norm_T(0, b, rows, NT, NQ)
        def pre_or_load(name, pieces):
            v_ = PRE.pop(name, None)
            return v_ if v_ is not None else ring_load(pieces)
        wu = pre_or_load("wu", [ld_full(s_win_u, 4096)])
        wv = pre_or_load("wv", [ld_full(s_win_v, 4096)])
        wq = pre_or_load("wq", [ld_full(s_win_q, 3072)])
        wkv = pre_or_load("wkv", [ld_full(s_win_kv, 2304), lambda base: (RING(0, 128, base + 2304, (1, 2048)), s_wukv.d(0, (2048, 128), (1, 2048)))])

        def emit_IN_u(falloc):
            for c in range(4):
                k = falloc()

                def mm(e, c=c, k=k):
                    ins = None
                    for kc in range(8):
                        ins = e.matmul(PS[k](0, 128, 0, (1, NQ)).ap, lhsT=RING(0, 128, wu + c * 1024 + kc * 128, (1, 128)).ap,
                                       rhs=HT(0, 128, kc * TB, (1, NQ)).ap, start=(kc == 0), stop=(kc == 7))
                    return ins
                add("pe", mm, r=[RING(0, 128, wu, (1, 4096)), HT(0, 128, 0, (1, 8 * TB))], w=[PS[k](0, 128, 0, (1, NQ))])
                o = RK(0, 128, O_UT + c * TB, (1, NQ))
                add("act", lambda e, o=o, k=k: e.activation(out=o.ap, in_=PS[k](0, 128, 0, (1, NQ)).ap, func=AF.Gelu_apprx_tanh),
                    r=[PS[k](0, 128, 0, (1, NQ))], w=[o])

        def emit_IN_v(falloc):
            for tt in range(NT):
                k = falloc()

                def mm(e, tt=tt, k=k):
                    ins = None
                    for kc in range(8):
                        ins = e.matmul(PS[k](0, rows, 0, (1, 512)).ap, lhsT=HT(0, 128, kc * TB + tt * rows, (1, rows)).ap,
                                       rhs=RING(0, 128, wv + kc * 512, (1, 512)).ap, start=(kc == 0), stop=(kc == 7))
                    return ins
                add("pe", mm, r=[RING(0, 128, wv, (1, 4096)), HT(0, 128, 0, (1, 8 * TB))], w=[PS[k](0, rows, 0, (1, 512))])
                o = RK(0, rows, O_V + tt * 512, (1, 512))
                if is_s:
                    vf = RI(0, rows, T_SQ, (1, 512))
                    add("act", lambda e, vf=vf, k=k: e.activation(out=vf.ap, in_=PS[k](0, rows, 0, (1, 512)).ap, func=AF.Gelu_apprx_tanh),
                        r=[PS[k](0, rows, 0, (1, 512))], w=[vf])
                    dma(o_v_s.d(0, (512, rows), (1, 512)), vf)
                    add("pool", lambda e, o=o, vf=vf: e.tensor_copy(out=o.ap, in_=vf.ap), r=[vf], w=[o])
                else:
                    add("act", lambda e, o=o, k=k: e.activation(out=o.ap, in_=PS[k](0, rows, 0, (1, 512)).ap, func=AF.Gelu_apprx_tanh),
                        r=[PS[k](0, rows, 0, (1, 512))], w=[o])

        def emit_G(falloc):
            for g in range(4):
                k = falloc()

                def mm(e, g=g, k=k):
                    ins = None
                    for tt in range(NT):
                        e.matmul(PS[k](0, 128, tt * rows, (1, rows)).ap, lhsT=RK(0, rows, O_V + tt * 512 + g * 128, (1, 128)).ap,
                                 rhs=CB(0, rows, B_WS + g * 128, (1, rows)).ap, start=True, stop=False)
                        ins = e.matmul(PS[k](0, 128, tt * rows, (1, rows)).ap, lhsT=CB(0, 1, B_ONE, (1, 128)).ap,
                                       rhs=CB(0, 1, B_BS + g * 128, (1, rows)).ap, start=False, stop=True)
                    return ins
                add("pe", mm, r=[RK(0, rows, O_V, (1, 2048)), CB(0, 128, B_WS, (1, 1152))], w=[PS[k](0, 128, 0, (1, NQ))])
                o = RK(0, 128, O_YA + g * TB, (1, NQ))
                u_ = RK(0, 128, O_UT + g * TB, (1, NQ))
                add("dve", lambda e, o=o, u_=u_, k=k: e.tensor_tensor(out=o.ap, in0=PS[k](0, 128, 0, (1, NQ)).ap, in1=u_.ap, op=ALU.mult),
                    r=[PS[k](0, 128, 0, (1, NQ)), u_], w=[o])
        rot_n[0] = 4
        XBb = view(XB, BF16)
        SETS = [(RI, RIb), (XB, XBb)]
        Q_SQ, Q_SQR, Q_QR, Q_T1, Q_T2, K_SQ, K_CKVF, K_KRF, K_KROF, K_T1, K_T2, X_ST, X_TAB, X_TABQ, X_TABK = \
            0, 512, 768, 1024, 1280, 1536, 2048, 2304, 2336, 2368, 2400, 2432, 2496, 2560, 2624
        B_CQN, B_CQT, B_QF, B_CKVB, B_CKVT, B_KF = 5376, 5760, 6144, 6912, 7168, 7424

        def tile_chains(tt, F_, Bv, qb, kb_):
            qalloc, kalloc = mk_alloc(qb), mk_alloc(kb_)
            tok0 = t0 + tt * rows
            pre_steps = []
            qs = []
            ks = []
            stq = {}

            def p0():
                dma(F_(0, rows, X_TAB, (1, 64)), rsrc.d(tok0 * 64, (64, rows), (1, 64)))
                for (dst, gC, gS) in ((X_TABQ, C_QRG, C_QRS), (X_TABK, C_KRG, C_KRS)):
                    for part, g_ in ((0, gC), (1, gS)):
                        o = F_(0, rows, dst + part * 32, (1, 32))
                        i = F_(0, rows, X_TAB + part * 32, (1, 32))
                        gg = CF(0, rows, g_, (1, 32))
                        add("pool", lambda e, o=o, i=i, gg=gg: e.tensor_tensor(out=o.ap, in0=i.ap, in1=gg.ap, op=ALU.mult),
                            r=[i, gg], w=[o])
            pre_steps.append(p0)

            def rope_chain(steps, x_off, nh, tab, t1o, t2o, out_fn):
                def r0():
                    t1 = F_(0, rows, t1o, (32, nh), (1, 32))
                    add("pool", lambda e: e.tensor_tensor(out=t1.ap, in0=F_(0, rows, x_off, (32, nh), (1, 32)).ap,
                                                          in1=F_(0, rows, tab, (0, nh), (1, 32)).ap, op=ALU.mult),
                        r=[F_(0, rows, x_off, (1, 32 * nh)), F_(0, rows, tab, (1, 32))], w=[t1])
                    for hf in range(2):
                        t2 = F_(0, rows, t2o + hf * 16, (32, nh), (1, 16))
                        xi = F_(0, rows, x_off + (1 - hf) * 16, (32, nh), (1, 16))
                        sn = F_(0, rows, tab + 32 + hf * 16, (0, nh), (1, 16))
                        add("pool", lambda e, t2=t2, xi=xi, sn=sn: e.tensor_tensor(out=t2.ap, in0=xi.ap, in1=sn.ap, op=ALU.mult),
                            r=[xi, F_(0, rows, tab + 32, (1, 32))], w=[t2])
                steps.append(r0)

                def r1():
                    oo = out_fn()
                    t1 = F_(0, rows, t1o, (32, nh), (1, 32))
                    t2a = F_(0, rows, t2o, (32, nh), (1, 32))
                    add("pool", lambda e: e.tensor_tensor(out=oo.ap, in0=t1.ap, in1=t2a.ap, op=ALU.add), r=[t1, t2a], w=[oo])
                steps.append(r1)

            def q0():
                k = qalloc()
                stq["cq"] = k

                def mmq(e):
                    ins = None
                    for kc in range(8):
                        ins = e.matmul(PS[k](0, rows, 0, (1, 384)).ap, lhsT=HT(0, 128, kc * TB + tt * rows, (1, rows)).ap,
                                       rhs=RING(0, 128, wq + kc * 384, (1, 384)).ap, start=(kc == 0), stop=(kc == 7))
                    return ins
                add("pe", mmq, r=[RING(0, 128, wq, (1, 3072)), HT(0, 128, 0, (1, 8 * TB))], w=[PS[k](0, rows, 0, (1, 384))])
            qs.append(q0)

            def q1():
                k = stq["cq"]
                cq = PS[k](0, rows, 0, (1, 384))
                junk = F_(0, rows, Q_SQ, (1, 384))
                ssq = F_(0, rows, X_ST, (1, 1))
                add("act", lambda e: e.activation(out=junk.ap, in_=cq.ap, func=AF.Square, scale=384 ** -0.5, accum_out=ssq.ap),
                    r=[cq], w=[junk, ssq])
            qs.append(q1)

            def q2():
                ssq = F_(0, rows, X_ST, (1, 1))
                add("act", lambda e: e.activation(out=ssq.ap, in_=ssq.ap, func=AF.Sqrt, bias=EPS_AP(rows).ap),
                    r=[ssq, EPS_AP(rows)], w=[ssq])
            qs.append(q2)

            def q3():
                ssq = F_(0, rows, X_ST, (1, 1))
                add("dve", lambda e: e.reciprocal(out=ssq.ap, in_=ssq.ap), r=[ssq], w=[ssq])
            qs.append(q3)

            def q4():
                k = stq["cq"]
                cq = PS[k](0, rows, 0, (1, 384))
                rq = F_(0, rows, X_ST, (1, 1))
                cqn = Bv(0, rows, B_CQN, (1, 384))
                add("dve", lambda e: e.tensor_scalar(out=cqn.ap, in0=cq.ap, scalar1=rq.ap, scalar2=None, op0=ALU.mult),
                    r=[cq, rq], w=[cqn])
            qs.append(q4)

            def q5():
                k2 = qalloc()
                stq["k2"] = k2

                def trq(e):
                    ins = None
                    for kc in range(3):
                        ins = e.transpose(out=PSb[k2](0, 128, kc * 128, (1, rows)).ap,
                                          in_=Bv(0, rows, B_CQN + kc * 128, (1, 128)).ap, identity=ident(rows).ap)
                    return ins
                add("pe", trq, r=[Bv(0, rows, B_CQN, (1, 384)), ident(rows)], w=[PSb[k2](0, 128, 0, (1, 384))])
            qs.append(q5)

            def q6():
                k2 = stq["k2"]
                cqT = Bv(0, 128, B_CQT, (128, 3), (1, rows))
                add("act", lambda e: e.activation(out=cqT.ap, in_=PSb[k2](0, 128, 0, (128, 3), (1, rows)).ap, func=AF.Copy),
                    r=[PSb[k2](0, 128, 0, (1, 384))], w=[cqT])
            qs.append(q6)

            def q7():
                ka, kb = qalloc(), qalloc()
                stq["ka"], stq["kb"] = ka, kb

                def mmuq(e):
                    ins = None
                    for kc in range(3):
                        e.matmul(PS[ka](0, rows, 0, (1, 512)).ap, lhsT=Bv(0, 128, B_CQT + kc * 128, (1, rows)).ap,
                                 rhs=RING(0, 128, wuq + kc * 768, (1, 512)).ap, start=(kc == 0), stop=(kc == 2))
                        ins = e.matmul(PS[kb](0, rows, 0, (1, 256)).ap, lhsT=Bv(0, 128, B_CQT + kc * 128, (1, rows)).ap,
                                       rhs=RING(0, 128, wuq + kc * 768 + 512, (1, 256)).ap, start=(kc == 0), stop=(kc == 2))
                    return ins
                add("pe", mmuq, r=[Bv(0, 128, B_CQT, (1, 384)), RING(0, 128, wuq, (1, 2304))],
                    w=[PS[ka](0, rows, 0, (1, 512)), PS[kb](0, rows, 0, (1, 256))])
            qs.append(q7)

            def q8():
                ka, kb = stq["ka"], stq["kb"]
                sq, sqr = F_(0, rows, Q_SQ, (1, 512)), F_(0, rows, Q_SQR, (1, 256))
                add("act", lambda e: e.activation(out=sq.ap, in_=PS[ka](0, rows, 0, (1, 512)).ap, func=AF.Square),
                    r=[PS[ka](0, rows, 0, (1, 512))], w=[sq])
                add("act", lambda e: e.activation(out=sqr.ap, in_=PS[kb](0, rows, 0, (1, 256)).ap, func=AF.Square),
                    r=[PS[kb](0, rows, 0, (1, 256))], w=[sqr])
            qs.append(q8)

            def q9():
                stn, str_ = F_(0, rows, X_ST + 2, (1, 8)), F_(0, rows, X_ST + 10, (1, 8))
                add("dve", lambda e: e.tensor_reduce(out=stn.ap, in_=F_(0, rows, Q_SQ, (64, 8), (1, 64)).ap, axis=AX.X, op=ALU.add),
                    r=[F_(0, rows, Q_SQ, (1, 512))], w=[stn])
                add("dve", lambda e: e.tensor_reduce(out=str_.ap, in_=F_(0, rows, Q_SQR, (32, 8), (1, 32)).ap, axis=AX.X, op=ALU.add),
                    r=[F_(0, rows, Q_SQR, (1, 256))], w=[str_])
            qs.append(q9)

            def q10():
                stn, str_ = F_(0, rows, X_ST + 2, (1, 8)), F_(0, rows, X_ST + 10, (1, 8))
                add("act", lambda e: e.activation(out=stn.ap, in_=stn.ap, func=AF.Sqrt, scale=1.0 / 64, bias=EPS_AP(rows).ap),
                    r=[stn, EPS_AP(rows)], w=[stn])
                add("act", lambda e: e.activation(out=str_.ap, in_=str_.ap, func=AF.Sqrt, scale=1.0 / 32, bias=EPS_AP(rows).ap),
                    r=[str_, EPS_AP(rows)], w=[str_])
            qs.append(q10)

            def q11():
                st16 = F_(0, rows, X_ST + 2, (1, 16))
                add("dve", lambda e: e.reciprocal(out=st16.ap, in_=st16.ap), r=[st16], w=[st16])
            qs.append(q11)

            def q12():
                ka, kb = stq["ka"], stq["kb"]
                qf_n = Bv(0, rows, B_QF, (96, 8), (1, 64))
                add("dve", lambda e: e.tensor_tensor(out=qf_n.ap, in0=PS[ka](0, rows, 0, (64, 8), (1, 64)).ap,
                                                     in1=F_(0, rows, X_ST + 2, (1, 8), (0, 64)).ap, op=ALU.mult),
                    r=[PS[ka](0, rows, 0, (1, 512)), F_(0, rows, X_ST + 2, (1, 8))], w=[qf_n])
                qr = F_(0, rows, Q_QR, (32, 8), (1, 32))
                add("dve", lambda e: e.tensor_tensor(out=qr.ap, in0=PS[kb](0, rows, 0, (32, 8), (1, 32)).ap,
                                                     in1=F_(0, rows, X_ST + 10, (1, 8), (0, 32)).ap, op=ALU.mult),
                    r=[PS[kb](0, rows, 0, (1, 256)), F_(0, rows, X_ST + 10, (1, 8))], w=[qr])
            qs.append(q12)
            rope_chain(qs, Q_QR, 8, X_TABQ, Q_T1, Q_T2, lambda: Bv(0, rows, B_QF + 64, (96, 8), (1, 32)))

            def q15():
                qfull = Bv(0, rows, B_QF, (1, 768))
                for hh in range(2):
                    k3 = qalloc()
                    stq["k3", hh] = k3

                    def trqf(e, hh=hh, k3=k3):
                        ins = None
                        for hl in range(4):
                            h = hh * 4 + hl
                            ins = e.transpose(out=PSb[k3](0, 96, hl * 128, (1, rows)).ap,
                                              in_=Bv(0, rows, B_QF + h * 96, (1, 96)).ap, identity=ident(rows).ap)
                        return ins
                    add("pe", trqf, r=[qfull, ident(rows)], w=[PSb[k3](0, 96, 0, (1, 512))])
            qs.append(q15)

            def q16():
                gcol = CF(0, 96, C_GQ96, (1, 1))
                for hh in range(2):
                    k3 = stq["k3", hh]
                    qdst = QTv(0, 96, (hh * 4) * TB + tt * rows, (TB, 4), (1, rows))
                    src = PSb[k3](0, 96, 0, (128, 4), (1, rows))
                    if hh == 0:
                        add("act", lambda e, qdst=qdst, src=src: e.activation(out=qdst.ap, in_=src.ap, func=AF.Identity, scale=gcol.ap, bias=CF(0, 96, C_GQ96 + 1, (1, 1)).ap),
                            r=[PSb[k3](0, 96, 0, (1, 512)), gcol], w=[qdst])
                    else:
                        add("dve", lambda e, qdst=qdst, src=src: e.tensor_scalar(out=qdst.ap, in0=src.ap, scalar1=gcol.ap, scalar2=None, op0=ALU.mult),
                            r=[PSb[k3](0, 96, 0, (1, 512)), gcol], w=[qdst])
            qs.append(q16)

            def k0():
                k = kalloc()
                stq["kv"] = k

                def mmkv(e):
                    ins = None
                    for kc in range(8):
                        ins = e.matmul(PS[k](0, rows, 0, (1, 288)).ap, lhsT=HT(0, 128, kc * TB + tt * rows, (1, rows)).ap,
                                       rhs=RING(0, 128, wkv + kc * 288, (1, 288)).ap, start=(kc == 0), stop=(kc == 7))
                    return ins
                add("pe", mmkv, r=[RING(0, 128, wkv, (1, 2304)), HT(0, 128, 0, (1, 8 * TB))], w=[PS[k](0, rows, 0, (1, 288))])
            ks.append(k0)

            def k1():
                k = stq["kv"]
                ckvp, krp = PS[k](0, rows, 0, (1, 256)), PS[k](0, rows, 256, (1, 32))
                j1, j2 = F_(0, rows, K_SQ, (1, 256)), F_(0, rows, K_SQ + 256, (1, 32))
                s1, s2 = F_(0, rows, X_ST + 24, (1, 1)), F_(0, rows, X_ST + 25, (1, 1))
                add("act", lambda e: e.activation(out=j1.ap, in_=ckvp.ap, func=AF.Square, scale=1.0 / 16, accum_out=s1.ap),
                    r=[ckvp], w=[j1, s1])
                add("act", lambda e: e.activation(out=j2.ap, in_=krp.ap, func=AF.Square, scale=32 ** -0.5, accum_out=s2.ap),
                    r=[krp], w=[j2, s2])
            ks.append(k1)

            def k2():
                s12 = F_(0, rows, X_ST + 24, (1, 2))
                add("act", lambda e: e.activation(out=s12.ap, in_=s12.ap, func=AF.Sqrt, bias=EPS_AP(rows).ap),
                    r=[s12, EPS_AP(rows)], w=[s12])
            ks.append(k2)

            def k3():
                s12 = F_(0, rows, X_ST + 24, (1, 2))
                add("dve", lambda e: e.reciprocal(out=s12.ap, in_=s12.ap), r=[s12], w=[s12])
            ks.append(k3)

            def k4():
                k = stq["kv"]
                ckvp, krp = PS[k](0, rows, 0, (1, 256)), PS[k](0, rows, 256, (1, 32))
                r1, r2 = F_(0, rows, X_ST + 24, (1, 1)), F_(0, rows, X_ST + 25, (1, 1))
                ckvf = F_(0, rows, K_CKVF, (1, 256))
                add("dve", lambda e: e.scalar_tensor_tensor(out=ckvf.ap, in0=ckvp.ap, scalar=r1.ap, in1=CF(0, rows, C_KVG, (1, 256)).ap,
                                                            op0=ALU.mult, op1=ALU.mult),
                    r=[ckvp, r1, CF(0, rows, C_KVG, (1, 256))], w=[ckvf])
                dma(ockv.d(tok0 * 256, (256, rows), (1, 256)), ckvf)
                krf = F_(0, rows, K_KRF, (1, 32))
                add("dve", lambda e: e.tensor_scalar(out=krf.ap, in0=krp.ap, scalar1=r2.ap, scalar2=None, op0=ALU.mult),
                    r=[krp, r2], w=[krf])
            ks.append(k4)
            rope_chain(ks, K_KRF, 1, X_TABK, K_T1, K_T2, lambda: F_(0, rows, K_KROF, (32, 1), (1, 32)))

            def k7():
                dma(okr.d(tok0 * 32, (32, rows), (1, 32)), F_(0, rows, K_KROF, (1, 32)))
            ks.append(k7)
            ts = dict(f=F_, b=Bv, sq=K_SQ, st=X_ST + 32, ckvb=B_CKVB, ckvT=B_CKVT, kfull=B_KF)
            kvs = kv_steps(F_(0, rows, K_CKVF, (1, 256)), F_(0, rows, K_KROF, (1, 32)), kt0 + tt, rows, wkv + 2304, ts, alloc=kalloc)
            ks.extend(kvs)
            return pre_steps, qs, ks

        PAIR = 2
        for tp in range(0, NT, PAIR):
            chains = []
            for j_, tt in enumerate(range(tp, min(tp + PAIR, NT))):
                F_, Bv = SETS[j_]
                pre_steps, qs, ks = tile_chains(tt, F_, Bv, [4 * j_, 4 * j_ + 1], [4 * j_ + 2, 4 * j_ + 3])
                for p_ in pre_steps:
                    p_()
                chains.append(qs)
                chains.append(ks)
            if tp == 0:
                cuts = [5 if (ci % 2 == 0) else 9 for ci in range(len(chains))]
                run_chains([c[:cut] for c, cut in zip(chains, cuts)])
                fill_alloc = mk_alloc([1, 3, 5, 7])
                emit_IN_u(fill_alloc)
                wuq = ring_load([ld_full(s_wuq, 2304)])
                emit_IN_v(fill_alloc)
                emit_G(fill_alloc)
                run_chains([c[cut:] for c, cut in zip(chains, cuts)])
            else:
                run_chains(chains)
        rot_n[0] = 4
        if NT > 1 and PAIR > 1:
            dma(XB(0, rows, 0, (D, NT), (1, D)), xsrc.d(t0 * D, (D, rows), (rows * D, NT), (1, D)))

        G = 512 // NQ
        if is_s:
            ktl = [(kt, 128 if kt < 16 else 64, 0) for kt in range(17)]
        else:
            ktl = [(kt, 128, 0 if kt < kt0 else (kt - kt0) * 128) for kt in range(kt0 + 4)]
        groups = [ktl[i:i + G] for i in range(0, len(ktl), G)]
        if is_s:
            groups = [ktl[0:8], ktl[8:16], ktl[16:17]]
        ybz = RK(64, 64, O_YB, (1, 8 * TB))
        add("pool", lambda e: e.memset(ybz.ap, 0.0), w=[ybz])
        units = [(h, gi) for h in range(8) for gi in range(len(groups))]
        NU = len(units)
        LOOK = 2
        ubank = {}
        deferred = []

        def emit_S(u):
            h, gi = units[u]
            grp = groups[gi]
            k = rot()
            ubank[u] = k
            rk = grp[0][1]
            c0 = grp[0][2]

            def mms(e, grp=grp, k=k, h=h):
                ins = None
                for s_, (kt, rk_, c0_) in enumerate(grp):
                    ins = e.matmul(PS[k](0, rk_, s_ * NQ + c0_, (1, NQ - c0_)).ap,
                                   lhsT=KT(0, 96, h * SEQ + kt * 128, (1, rk_)).ap,
                                   rhs=QTv(0, 96, h * TB + c0_, (1, NQ - c0_)).ap, start=True, stop=True)
                return ins
            kt_lo, kt_hi = grp[0][0], grp[-1][0]
            add("pe", mms, r=[KT(0, 96, h * SEQ + kt_lo * 128, (1, (kt_hi - kt_lo + 1) * 128)), QTv(0, 96, h * TB, (1, NQ))],
                w=[PS[k](0, 128, 0, (1, 512))])
            pb = A_PT[u % 4]
            ncol = len(grp) * NQ - c0
            pt = RIb(0, rk, pb + c0, (1, ncol))
            add("act", lambda e, pt=pt, k=k, rk=rk, c0=c0, ncol=ncol: e.activation(
                out=pt.ap, in_=PS[k](0, rk, c0, (1, ncol)).ap, func=AF.Exp, scale=SCALE),
                r=[PS[k](0, 128, 0, (1, 512))], w=[pt])
            if (not is_s) and grp[0][0] >= kt0:
                mz = RIb(64, 64, pb + c0, (1, 64))
                add("pool", lambda e, mz=mz: e.memset(mz.ap, 0.0), w=[mz])

        def emit_PV(u):
            h, gi = units[u]
            grp = groups[gi]
            acc = 4 + (h % 2)
            pb = A_PT[u % 4]
            first = (gi == 0)
            last = (gi == len(groups) - 1)
            kt_lo, kt_hi = grp[0][0], grp[-1][0]

            def mmv(e, grp=grp, h=h, pb=pb, acc=acc, first=first, last=last):
                ins = None
                for s_, (kt, rk_, c0_) in enumerate(grp):
                    ins = e.matmul(PS[acc](0, 65, c0_, (1, NQ - c0_)).ap,
                                   lhsT=VS(0, rk_, kt * 520 + h * 65, (1, 65)).ap,
                                   rhs=RIb(0, rk_, pb + s_ * NQ + c0_, (1, NQ - c0_)).ap,
                                   start=(first and s_ == 0), stop=(last and s_ == len(grp) - 1))
                return ins
            add("pe", mmv, r=[VS(0, 128, kt_lo * 520, (1, (kt_hi - kt_lo + 1) * 520)), RIb(0, 128, pb, (1, 512))],
                w=[PS[acc](0, 65, 0, (1, NQ))])
            if last:
                ob = A_OSB + (h % 2) * 1024
                osb = RI(0, 65, ob, (1, NQ))
                add("act", lambda e, osb=osb, acc=acc: e.activation(out=osb.ap, in_=PS[acc](0, 65, 0, (1, NQ)).ap, func=AF.Copy),
                    r=[PS[acc](0, 65, 0, (1, NQ))], w=[osb])
                rc = RI(64, 1, ob + 512, (1, NQ))
                add("dve", lambda e, rc=rc, ob=ob: e.reciprocal(out=rc.ap, in_=RI(64, 1, ob, (1, NQ)).ap), r=[osb], w=[rc])

                def fin(h=h, ob=ob, osb=osb, rc=rc):
                    add("pe", lambda e: e.matmul(PS[6](0, 64, 0, (1, NQ)).ap, lhsT=CF(64, 1, C_ONE, (1, 64)).ap, rhs=rc.ap, start=True, stop=True),
                        r=[rc, CF(64, 1, C_ONE, (1, 64))], w=[PS[6](0, 64, 0, (1, NQ))])
                    yb = RK(0, 64, O_YB + h * TB, (1, NQ))
                    add("dve", lambda e: e.tensor_tensor(out=yb.ap, in0=RI(0, 64, ob, (1, NQ)).ap, in1=PS[6](0, 64, 0, (1, NQ)).ap, op=ALU.mult),
                        r=[osb, PS[6](0, 64, 0, (1, NQ))], w=[yb])
                deferred.append((u + min(6, 2 * len(groups) - 2), fin))

        for u in range(min(LOOK, NU)):
            emit_S(u)
        for u in range(NU):
            if u + LOOK < NU:
                emit_S(u + LOOK)
            emit_PV(u)
            while deferred and deferred[0][0] <= u:
                deferred.pop(0)[1]()
        while deferred:
            deferred.pop(0)[1]()

        woa = ring_load([ld_full(s_wout_a[b], 4096)])
        wob0 = ring_load([ld_full(s_wout_b[b], 4096, np_=64, src_off=0),
                          lambda base: (RING(64, 64, base, (1, 4096)), s_wout_b[b].d(0, (8192, 64), (1, 4096)))])
        wob1 = ring_load([ld_full(s_wout_b[b], 4096, np_=64, src_off=4096),
                          lambda base: (RING(64, 64, base, (1, 4096)), s_wout_b[b].d(4096, (8192, 64), (1, 4096)))])
        for tt in range(NT):
            for half in range(2):
                k = rot()

                def mmo(e, tt=tt, half=half, k=k):
                    ins = None
                    for g in range(4):
                        ins = e.matmul(PS[k](0, rows, 0, (1, 512)).ap, lhsT=RK(0, 128, O_YA + g * TB + tt * rows, (1, rows)).ap,
                                       rhs=RING(0, 128, woa + g * 1024 + half * 512, (1, 512)).ap, start=(g == 0), stop=False)
                    for h in range(8):
                        wb = wob0 if h < 4 else wob1
                        ins = e.matmul(PS[k](0, rows, 0, (1, 512)).ap, lhsT=RK(0, 128, O_YB + h * TB + tt * rows, (1, rows)).ap,
                                       rhs=RING(0, 128, wb + (h % 4) * 1024 + half * 512, (1, 512)).ap, start=False, stop=(h == 7))
                    return ins
                add("pe", mmo, r=[RK(0, 128, O_YA, (1, 6144)), RING(0, 128, woa, (1, 4096)), RING(0, 128, wob0, (1, 4096)), RING(0, 128, wob1, (1, 4096))],
                    w=[PS[k](0, rows, 0, (1, 512))])
                xr = XB(0, rows, tt * D + half * 512, (1, 512))
                add("dve", lambda e, xr=xr, k=k: e.tensor_tensor(out=xr.ap, in0=PS[k](0, rows, 0, (1, 512)).ap, in1=xr.ap, op=ALU.add),
                    r=[PS[k](0, rows, 0, (1, 512)), xr], w=[xr])
            norm_stats(tt, rows)
        norm_T(1, b, rows, NT, NQ, skip_stats=True)
        passes = [(0, 1)] if NT == 1 else [(0, 2), (2, 4)]
        def do_prefetch():
            PRE["wu"] = ring_load([ld_full(s_win_u, 4096)])
            PRE["wv"] = ring_load([ld_full(s_win_v, 4096)])
            PRE["wq"] = ring_load([ld_full(s_win_q, 3072)])
            PRE["wkv"] = ring_load([ld_full(s_win_kv, 2304), lambda base: (RING(0, 128, base + 2304, (1, 2048)), s_wukv.d(0, (2048, 128), (1, 2048)))])
        pre_issued = set()

        def mini_dma(j):
            mslot_ = (2048, 0, 4096)[(j // 2) % 3]
            dma(RIb(0, 128, mslot_, (1024, 2), (1, 1024)),
                s_ffn_out[b].d(j * 128 * 1024, (1024, 128), (128 * 1024, 2), (1, 1024)))
        pending_out = [None]
        nxt_tok0 = 0 if is_s else (blk + 1) * TB
        do_early = is_s or blk + 1 < SEQ // TB
        for pi, (ta, tb_) in enumerate(passes):
            for j in range(NJ):
                if pi == 0 and do_early:
                    if j % 5 == 0 and j // 5 < 4:
                        early_load(nxt_tok0 + (j // 5) * 128)
                    if j % 5 == 2 and j // 5 < 4:
                        early_stats(j // 5)
                accb = 4
                if pi == 0:
                    pieces = [lambda base, j=j: (RING(0, 128, base, (1, 2048)), s_ffn_in.d(j * 128 * 2048, (2048, 128), (1, 2048))),
                              lambda base, j=j: (RING(0, 128, base + 2048, (1, 1024)), s_ffn_out[b].d(j * 128 * 1024, (1024, 128), (1, 1024)))]
                    wf = ring_load(pieces)
                    WT, wo = RING, wf + 2048
                else:
                    if j == 6 and do_early:
                        do_prefetch()
                    wf = None
                    mslot = (2048, 0, 4096)[(j // 2) % 3]
                    if j % 2 == 0 and j not in pre_issued:
                        mini_dma(j)
                    WT, wo = RIb, mslot + (j % 2) * 1024
                at = RK(0, 128, j * TB, (1, NQ))
                if pi == 0:
                    kg, ku = rot(), rot()

                    def mmf(e, wf=wf, kg=kg, ku=ku):
                        ins = None
                        for kc in range(8):
                            e.matmul(PS[kg](0, 128, 0, (1, NQ)).ap, lhsT=RING(0, 128, wf + kc * 256, (1, 128)).ap,
                                     rhs=HT(0, 128, kc * TB, (1, NQ)).ap, start=(kc == 0), stop=(kc == 7))
                        for kc in range(8):
                            ins = e.matmul(PS[ku](0, 128, 0, (1, NQ)).ap, lhsT=RING(0, 128, wf + kc * 256 + 128, (1, 128)).ap,
                                           rhs=HT(0, 128, kc * TB, (1, NQ)).ap, start=(kc == 0), stop=(kc == 7))
                        return ins
                    add("pe", mmf, r=[RING(0, 128, wf, (1, 2048)), HT(0, 128, 0, (1, 8 * TB))],
                        w=[PS[kg](0, 128, 0, (1, NQ)), PS[ku](0, 128, 0, (1, NQ))])
                    sg = RI(0, 128, (j % 2) * 512, (1, NQ))
                    add("act", lambda e, sg=sg, kg=kg: e.activation(out=sg.ap, in_=PS[kg](0, 128, 0, (1, NQ)).ap, func=AF.Silu),
                        r=[PS[kg](0, 128, 0, (1, NQ))], w=[sg])
                    add("dve", lambda e, at=at, sg=sg, ku=ku: e.tensor_tensor(out=at.ap, in0=PS[ku](0, 128, 0, (1, NQ)).ap, in1=sg.ap, op=ALU.mult),
                        r=[PS[ku](0, 128, 0, (1, NQ)), sg], w=[at])

                def mmfo(e, wf=wf, j=j, ta=ta, tb_=tb_, accb=accb, WT=WT, wo=wo):
                    ins = None
                    for tt in range(ta, tb_):
                        for half in range(2):
                            kk = accb + (tt - ta) * 2 + half
                            ins = e.matmul(PS[kk](0, rows, 0, (1, 512)).ap, lhsT=RK(0, 128, j * TB + tt * rows, (1, rows)).ap,
                                           rhs=WT(0, 128, wo + half * 512, (1, 512)).ap, start=(j == 0), stop=(j == NJ - 1))
                    return ins

                def emit_out(mmfo=mmfo, at=at, wf=wf, ta=ta, tb_=tb_, accb=accb, WT=WT, wo=wo):
                    add("pe", mmfo, r=[at, WT(0, 128, wo, (1, 1024))],
                        w=[PS[accb + (tt - ta) * 2 + half](0, rows, 0, (1, 512)) for tt in range(ta, tb_) for half in range(2)])
                if pi == 0:
                    if pending_out[0] is not None:
                        pending_out[0]()
                    pending_out[0] = emit_out
                else:
                    emit_out()
            if pending_out[0] is not None:
                pending_out[0]()
                pending_out[0] = None
            if pi == 0 and do_early:
                norm_T(0, 0, 128, 4, TB, skip_stats=True, early=True)
                early_state["done"] = True
            if pi == 0 and len(passes) > 1:
                for j_ in (0, 2):
                    mini_dma(j_)
                    pre_issued.add(j_)
            if pi == len(passes) - 1 and do_early and len(passes) == 1:
                do_prefetch()
            for tt in range(ta, tb_):
                for half in range(2):
                    kk = accb + (tt - ta) * 2 + half
                    xr = XB(0, rows, tt * D + half * 512, (1, 512))
                    add("dve", lambda e, xr=xr, kk=kk: e.tensor_tensor(out=xr.ap, in0=PS[kk](0, rows, 0, (1, 512)).ap, in1=xr.ap, op=ALU.add),
                        r=[PS[kk](0, rows, 0, (1, 512)), xr], w=[xr])
                dma(ydst.d((t0 + tt * rows) * D, (D, rows), (1, D)), XB(0, rows, tt * D, (1, D)))

    QTv = RK

    process_block(True, 0)
    for blk in range(SEQ // TB):
        process_block(False, blk)

    print("sbuf bytes remaining", nc.sbuf_bytes_remaining, "ops", len(S.ops), {k: len(v) for k, v in S.eng_ops.items()})
    return nc, S, None


def _finish(nc, S):
    from contextlib import ExitStack
    with ExitStack() as st:
        esem = {n: st.enter_context(nc.semaphore(f"sem_{n}")) for n in ("pe", "act", "dve", "pool")}
        dsems = [st.enter_context(nc.semaphore(f"dsem{k}")) for k in range(NDSEM)]
        block = st.enter_context(nc.Block())
        S.emit(nc, block, esem, dsems)
    return nc


_CACHE = {}


def _rope_table(pos):
    half = 16
    inv = (np.float32(10000.0) ** (-np.arange(half, dtype=np.float32) / np.float32(half))).astype(np.float32)
    ang = pos.astype(np.float32)[:, None] * inv[None, :]
    cos, sin = np.cos(ang).astype(np.float32), np.sin(ang).astype(np.float32)
    return np.ascontiguousarray(np.concatenate([cos, cos, -sin, sin], axis=1).astype(np.float32))


def kernel(x_prompt, x_sample, cache_ckv, cache_krope, c_prompt, c_sample, w_ada, b_ada, norm1_g, w_in, w_s, b_s,
           q_norm_g, w_uq, kv_norm_g, w_ukv, qn_g, qr_g, kn_g, kr_g, w_out, norm2_g, w_ffn_in, w_ffn_out):
    f = lambda a: np.ascontiguousarray(np.asarray(a, dtype=np.float32))
    if "nc" not in _CACHE:
        nc, S, _ = build_program()
        _finish(nc, S)
        _CACHE["nc"] = nc
    nc = _CACHE["nc"]
    shared = {
        "w_ada": f(w_ada)[0], "b_ada": f(b_ada), "norm1_g": f(norm1_g), "norm2_g": f(norm2_g), "w_in": f(w_in)[0],
        "w_s": f(w_s)[0].reshape(512, 128), "b_s": f(b_s).reshape(1, 512), "q_norm_g": f(q_norm_g), "w_uq": f(w_uq)[0],
        "kv_norm_g": f(kv_norm_g), "w_ukv": f(w_ukv)[0], "qn_g": f(qn_g), "qr_g": f(qr_g), "kn_g": f(kn_g), "kr_g": f(kr_g),
        "w_out": f(w_out)[0], "w_ffn_in": f(w_ffn_in)[0], "w_ffn_out": f(w_ffn_out)[0],
        "rope_p": _rope_table(np.arange(SEQ)), "rope_s": _rope_table(PAST + np.arange(DEC)),
    }
    xp, xs, cc, ck = f(x_prompt), f(x_sample), f(cache_ckv), f(cache_krope)
    cp, cs = f(c_prompt), f(c_sample)
    in_maps = []
    for b in range(8):
        m = dict(shared)
        m["x_p"] = xp[b]
        m["x_s"] = xs[b]
        m["cckv"] = cc[0, b]
        m["ckr"] = ck[0, b]
        m["c2"] = np.ascontiguousarray(np.stack([cp[b], cs[b]], axis=0))
        in_maps.append(m)
    res = run_bass_kernel_spmd(nc, in_maps, core_ids=list(range(8)))
    rr = res.results
    yp = np.stack([rr[b]["y_p"] for b in range(8)], axis=0)
    ys = np.stack([rr[b]["y_s"] for b in range(8)], axis=0)
    ckv_p = np.stack([rr[b]["o_ckv_p"] for b in range(8)], axis=0)[None]
    kr_p = np.stack([rr[b]["o_kr_p"] for b in range(8)], axis=0)[None]
    ckv_s = np.stack([rr[b]["o_ckv_s"] for b in range(8)], axis=0)[None]
    kr_s = np.stack([rr[b]["o_kr_s"] for b in range(8)], axis=0)[None]
    v_s = np.stack([rr[b]["o_v_s"] for b in range(8)], axis=0)[None]
    return (yp.astype(np.float32), ys.astype(np.float32), ckv_p.astype(np.float32), kr_p.astype(np.float32),
            ckv_s.astype(np.float32), kr_s.astype(np.float32), v_s.astype(np.float32))
```
